# Optimizing a Trainium2 kernel written in Bass

```python
import math
import jax
import jax.numpy as jnp
from jax import lax
import numpy as np

D_MODEL = 1024
BATCH = 2
SEQ = 8192
DEPTH = 2

RWKV_HEADS = 8
RWKV_HEAD_DIM = 64
RWKV_DIM = RWKV_HEADS * RWKV_HEAD_DIM
DECAY_LORA = 64
AAA_LORA = 64
GATE_LORA = 128
RWKV_GN_EPS = 64e-5
MLA_HEADS = 8
QK_NOPE_DIM = 64
QK_ROPE_DIM = 32
V_HEAD_DIM = 64
Q_LORA_RANK = 256
KV_LORA_RANK = 128
MLA_DIM = MLA_HEADS * V_HEAD_DIM
ROPE_THETA = 10000.0
Q_BLOCK = 128
ATTN_SCALE = 1.0 / math.sqrt(QK_NOPE_DIM + QK_ROPE_DIM)
S5_DIM = 512
S5_GROUP = 16
S5_GROUPS = S5_DIM // S5_GROUP
S5_STATE = 64
S5_DT_MIN = 1e-3
S5_DT_MAX = 1e-1
N_BRANCHES = 3
D_FF = ((8 * D_MODEL + 3 * 256 - 1) // (3 * 256)) * 256
LN_EPS = 1e-5
RMS_EPS = 1e-6
DEEPNORM_ALPHA = (2.0 * DEPTH) ** 0.25
DEEPNORM_BETA = (8.0 * DEPTH) ** -0.25
RWKV_COLS = 3 * RWKV_DIM + DECAY_LORA + AAA_LORA + GATE_LORA
MLA_COLS = Q_LORA_RANK + KV_LORA_RANK + QK_ROPE_DIM
S5_COLS = S5_DIM
GATE_COLS = N_BRANCHES * D_MODEL
IN_COLS = RWKV_COLS + MLA_COLS + S5_COLS + GATE_COLS

kernel_name = 'hybrid_rwkv7_mla_s5_gated_block'


def layer_norm(x, g, b):
    xf = x.astype(jnp.float32)
    mu = jnp.mean(xf, -1, keepdims=True)
    var = jnp.mean(jnp.square(xf - mu), -1, keepdims=True)
    return ((xf - mu) * lax.rsqrt(var + LN_EPS) * g + b).astype(x.dtype)


def rms_norm(x, g):
    xf = x.astype(jnp.float32)
    return (xf * lax.rsqrt(jnp.mean(xf * xf, -1, keepdims=True) + RMS_EPS) * g).astype(x.dtype)


def token_shift(p):
    return jnp.pad(p, ((0, 0), (1, 0), (0, 0)))[:, :-1]


def apply_rope(x, cos, sin):
    half = x.shape[-1] // 2
    xf = x.astype(jnp.float32)
    x1, x2 = xf[..., :half], xf[..., half:]
    return jnp.concatenate([x1 * cos - x2 * sin, x2 * cos + x1 * sin], -1).astype(x.dtype)


def rwkv7_scan(r, w, k, v, a, b):
    def step(state, inp):
        r_t, w_t, k_t, v_t, a_t, b_t = inp
        sa = jnp.einsum('bhvk,bhk->bhv', state, a_t)
        state = state * w_t[:, :, None, :] + sa[..., None] * b_t[:, :, None, :] + v_t[..., None] * k_t[:, :, None, :]
        return state, jnp.einsum('bhvk,bhk->bhv', state, r_t)
    bsz, _, h, n = r.shape
    xs = tuple(jnp.swapaxes(z, 0, 1) for z in (r, w, k, v, a, b))
    s0 = jnp.zeros((bsz, h, n, n), jnp.float32)
    _, ys = lax.scan(step, s0, xs)
    return jnp.swapaxes(ys, 0, 1)


def rwkv7_time_mix(p, mu, w0, w2, a0, a2, g2, k_k, k_a, r_k, gn_g, gn_b, w_o):
    bsz, t, _ = p.shape
    f32 = jnp.float32
    p = p + (token_shift(p) - p) * mu
    c0 = RWKV_DIM
    c1 = 2 * RWKV_DIM
    c2 = 3 * RWKV_DIM
    c3 = c2 + DECAY_LORA
    c4 = c3 + AAA_LORA
    r, k, v, dw, da, dg = p[..., :c0], p[..., c0:c1], p[..., c1:c2], p[..., c2:c3], p[..., c3:c4], p[..., c4:]
    w_log = -jax.nn.softplus(-(w0 + jnp.tanh(dw) @ w2).astype(f32)) - 0.5
    decay = jnp.exp(-jnp.exp(w_log))
    a = jax.nn.sigmoid((a0 + da @ a2).astype(f32))
    g = jax.nn.sigmoid(dg) @ g2

    def heads(z):
        return z.astype(f32).reshape(bsz, t, RWKV_HEADS, RWKV_HEAD_DIM)

    r, k, v, a, decay = heads(r), heads(k), heads(v), heads(a), heads(decay)
    kk = k * k_k.astype(f32).reshape(RWKV_HEADS, RWKV_HEAD_DIM)
    kk = kk * lax.rsqrt(jnp.maximum(jnp.sum(kk * kk, -1, keepdims=True), 1e-12))
    k = k * (1.0 + (a - 1.0) * k_a.astype(f32).reshape(RWKV_HEADS, RWKV_HEAD_DIM))
    y = rwkv7_scan(r, decay, k, v, -kk, kk * a)
    m = jnp.mean(y, -1, keepdims=True)
    var = jnp.mean(jnp.square(y - m), -1, keepdims=True)
    y = ((y - m) * lax.rsqrt(var + RWKV_GN_EPS)).reshape(bsz, t, RWKV_DIM) * gn_g + gn_b
    bonus = jnp.sum(r * k * r_k.astype(f32), -1, keepdims=True) * v
    y = y + bonus.reshape(bsz, t, RWKV_DIM)
    return (y * g).astype(p.dtype) @ w_o


def causal_block_attention(q_nope, q_rope, k_nope, k_rope, v):
    bsz, t, h, _ = q_nope.shape
    nb = t // Q_BLOCK
    kpos = jnp.arange(t)

    def to_blocks(z):
        return jnp.moveaxis(z.reshape(bsz, nb, Q_BLOCK, *z.shape[2:]), 1, 0)

    def one_block(args):
        qn, qr, start = args
        s = jnp.einsum('bqhd,bkhd->bhqk', qn, k_nope) + jnp.einsum('bqhr,bkr->bhqk', qr, k_rope)
        s = s.astype(jnp.float32) * ATTN_SCALE
        qpos = start + jnp.arange(Q_BLOCK)
        s = jnp.where(kpos[None, :] <= qpos[:, None], s, jnp.finfo(jnp.float32).min)
        pr = jax.nn.softmax(s, axis=-1).astype(v.dtype)
        return jnp.einsum('bhqk,bkhd->bqhd', pr, v)

    starts = jnp.arange(nb, dtype=jnp.int32) * Q_BLOCK
    out = lax.map(one_block, (to_blocks(q_nope), to_blocks(q_rope), starts))
    return jnp.moveaxis(out, 0, 1).reshape(bsz, t, h, v.shape[-1])


def mla_branch(p, cos, sin, q_norm, q_up, kv_norm, kv_up, w_o):
    bsz, t, _ = p.shape
    c_q = p[..., :Q_LORA_RANK]
    c_kv = p[..., Q_LORA_RANK:Q_LORA_RANK + KV_LORA_RANK]
    k_r = p[..., Q_LORA_RANK + KV_LORA_RANK:]
    q = (rms_norm(c_q, q_norm) @ q_up).reshape(bsz, t, MLA_HEADS, QK_NOPE_DIM + QK_ROPE_DIM)
    kv = (rms_norm(c_kv, kv_norm) @ kv_up).reshape(bsz, t, MLA_HEADS, QK_NOPE_DIM + V_HEAD_DIM)
    q_nope = q[..., :QK_NOPE_DIM]
    q_rope = apply_rope(q[..., QK_NOPE_DIM:], cos[:, None, :], sin[:, None, :])
    k_nope, v = kv[..., :QK_NOPE_DIM], kv[..., QK_NOPE_DIM:]
    k_rope = apply_rope(k_r, cos, sin)
    o = causal_block_attention(q_nope, q_rope, k_nope, k_rope, v)
    return o.reshape(bsz, t, MLA_DIM) @ w_o


def _complex_linear_combine(e1, e2):
    a1r, a1i, b1r, b1i = e1
    a2r, a2i, b2r, b2i = e2
    return (a2r * a1r - a2i * a1i,
            a2r * a1i + a2i * a1r,
            a2r * b1r - a2i * b1i + b2r,
            a2r * b1i + a2i * b1r + b2i)


def s5_branch(u, lambda_re, lambda_im, log_step, b_re, b_im, c_re, c_im, d_skip, w_glu):
    bsz, t, _ = u.shape
    f32 = jnp.float32
    lam_re = jnp.minimum(lambda_re.astype(f32), -1e-4)
    lam_im = lambda_im.astype(f32)
    step = jnp.exp(log_step.astype(f32))[:, None]
    mag = jnp.exp(lam_re * step)
    ang = lam_im * step
    lb_re, lb_im = mag * jnp.cos(ang), mag * jnp.sin(ang)
    den = lam_re * lam_re + lam_im * lam_im
    n_re = lb_re - 1.0
    f_re = (n_re * lam_re + lb_im * lam_im) / den
    f_im = (lb_im * lam_re - n_re * lam_im) / den
    br, bi = b_re.astype(f32), b_im.astype(f32)
    bb_re = f_re[..., None] * br - f_im[..., None] * bi
    bb_im = f_re[..., None] * bi + f_im[..., None] * br
    uf = u.astype(f32)
    ug = uf.reshape(bsz, t, S5_GROUPS, S5_GROUP)
    bu_re = jnp.einsum('btgc,gpc->btgp', ug, bb_re)
    bu_im = jnp.einsum('btgc,gpc->btgp', ug, bb_im)
    a_re = jnp.broadcast_to(lb_re, bu_re.shape)
    a_im = jnp.broadcast_to(lb_im, bu_im.shape)
    _, _, s_re, s_im = lax.associative_scan(_complex_linear_combine, (a_re, a_im, bu_re, bu_im), axis=1)
    y = jnp.einsum('btgp,gcp->btgc', s_re, c_re.astype(f32)) - jnp.einsum('btgp,gcp->btgc', s_im, c_im.astype(f32))
    y = y.reshape(bsz, t, S5_DIM) + d_skip.astype(f32) * uf
    y = jax.nn.gelu(y).astype(u.dtype)
    h = y @ w_glu
    return h[..., :D_MODEL] * jax.nn.sigmoid(h[..., D_MODEL:])


def setup_inputs(seed: int = 0) -> dict:
    key = jax.random.key(seed)
    ks = iter(jax.random.split(key, 48))
    L = DEPTH
    f32 = jnp.float32

    def nrm(shape, scale):
        return jax.random.normal(next(ks), shape, f32) * scale

    n = jnp.arange(RWKV_DIM, dtype=f32) / (RWKV_DIM - 1)
    ratio = jnp.arange(L, dtype=f32) / max(L - 1, 1)
    decay_speed = -7.0 + 5.0 * n[None, :] ** (0.85 + jnp.sqrt(ratio)[:, None])
    inp = {}
    inp['x'] = nrm((BATCH, SEQ, D_MODEL), 1.0)
    inp['w_in'] = nrm((L, D_MODEL, IN_COLS), D_MODEL ** -0.5)
    inp['rwkv_mu'] = jax.random.uniform(next(ks), (L, RWKV_COLS), f32)
    inp['rwkv_w0'] = decay_speed + 0.5 + nrm((L, RWKV_DIM), 0.01)
    inp['rwkv_w2'] = nrm((L, DECAY_LORA, RWKV_DIM), 0.1 * DECAY_LORA ** -0.5)
    inp['rwkv_a0'] = nrm((L, RWKV_DIM), 0.01)
    inp['rwkv_a2'] = nrm((L, AAA_LORA, RWKV_DIM), 0.1 * AAA_LORA ** -0.5)
    inp['rwkv_g2'] = nrm((L, GATE_LORA, RWKV_DIM), GATE_LORA ** -0.5)
    inp['rwkv_k_k'] = 0.85 + nrm((L, RWKV_DIM), 0.02)
    inp['rwkv_k_a'] = 1.0 + nrm((L, RWKV_DIM), 0.02)
    inp['rwkv_r_k'] = -0.04 + nrm((L, RWKV_HEADS, RWKV_HEAD_DIM), 0.02)
    inp['rwkv_gn_g'] = 1.0 + nrm((L, RWKV_DIM), 0.02)
    inp['rwkv_gn_b'] = nrm((L, RWKV_DIM), 0.02)
    inp['rwkv_out'] = nrm((L, RWKV_DIM, D_MODEL), DEEPNORM_BETA * RWKV_DIM ** -0.5)
    inp['mla_q_norm'] = 1.0 + nrm((L, Q_LORA_RANK), 0.02)
    inp['mla_q_up'] = nrm((L, Q_LORA_RANK, MLA_HEADS * (QK_NOPE_DIM + QK_ROPE_DIM)), Q_LORA_RANK ** -0.5)
    inp['mla_kv_norm'] = 1.0 + nrm((L, KV_LORA_RANK), 0.02)
    inp['mla_kv_up'] = nrm((L, KV_LORA_RANK, MLA_HEADS * (QK_NOPE_DIM + V_HEAD_DIM)), KV_LORA_RANK ** -0.5)
    inp['mla_out'] = nrm((L, MLA_DIM, D_MODEL), DEEPNORM_BETA * MLA_DIM ** -0.5)
    inp['s5_lambda_re'] = -0.5 + nrm((L, S5_GROUPS, S5_STATE), 0.01)
    inp['s5_lambda_im'] = jnp.broadcast_to(math.pi * jnp.arange(S5_STATE, dtype=f32), (L, S5_GROUPS, S5_STATE))
    inp['s5_log_step'] = jax.random.uniform(next(ks), (L, S5_GROUPS), f32, math.log(S5_DT_MIN), math.log(S5_DT_MAX))
    inp['s5_b_re'] = nrm((L, S5_GROUPS, S5_STATE, S5_GROUP), (2 * S5_GROUP) ** -0.5)
    inp['s5_b_im'] = nrm((L, S5_GROUPS, S5_STATE, S5_GROUP), (2 * S5_GROUP) ** -0.5)
    inp['s5_c_re'] = nrm((L, S5_GROUPS, S5_GROUP, S5_STATE), S5_STATE ** -0.5)
    inp['s5_c_im'] = nrm((L, S5_GROUPS, S5_GROUP, S5_STATE), S5_STATE ** -0.5)
    inp['s5_d'] = nrm((L, S5_DIM), 1.0)
    inp['s5_glu'] = nrm((L, S5_DIM, 2 * D_MODEL), DEEPNORM_BETA * S5_DIM ** -0.5)
    inp['gate_b'] = nrm((L, N_BRANCHES, D_MODEL), 0.01)
    inp['w_out'] = nrm((L, D_MODEL, D_MODEL), DEEPNORM_BETA * D_MODEL ** -0.5)
    inp['ln1_g'] = 1.0 + nrm((L, D_MODEL), 0.02)
    inp['ln1_b'] = nrm((L, D_MODEL), 0.02)
    inp['ffn_w1'] = nrm((L, D_MODEL, D_FF), D_MODEL ** -0.5)
    inp['ffn_w3'] = nrm((L, D_MODEL, D_FF), DEEPNORM_BETA * D_MODEL ** -0.5)
    inp['ffn_w2'] = nrm((L, D_FF, D_MODEL), DEEPNORM_BETA * D_FF ** -0.5)
    inp['ln2_g'] = 1.0 + nrm((L, D_MODEL), 0.02)
    inp['ln2_b'] = nrm((L, D_MODEL), 0.02)
    return inp


def reference(x, w_in, rwkv_mu, rwkv_w0, rwkv_w2, rwkv_a0, rwkv_a2, rwkv_g2, rwkv_k_k, rwkv_k_a, rwkv_r_k,
              rwkv_gn_g, rwkv_gn_b, rwkv_out, mla_q_norm, mla_q_up, mla_kv_norm, mla_kv_up, mla_out,
              s5_lambda_re, s5_lambda_im, s5_log_step, s5_b_re, s5_b_im, s5_c_re, s5_c_im, s5_d, s5_glu,
              gate_b, w_out, ln1_g, ln1_b, ffn_w1, ffn_w3, ffn_w2, ln2_g, ln2_b):
    bsz, t, _ = x.shape
    pos = jnp.arange(t, dtype=jnp.float32)
    inv_freq = ROPE_THETA ** (-jnp.arange(0, QK_ROPE_DIM, 2, dtype=jnp.float32) / QK_ROPE_DIM)
    ang = pos[:, None] * inv_freq[None, :]
    cos, sin = jnp.cos(ang), jnp.sin(ang)
    o_mla = RWKV_COLS
    o_s5 = o_mla + MLA_COLS
    o_gate = o_s5 + S5_COLS
    for l in range(DEPTH):
        p = x @ w_in[l]
        y_a = rwkv7_time_mix(p[..., :o_mla], rwkv_mu[l], rwkv_w0[l], rwkv_w2[l], rwkv_a0[l], rwkv_a2[l],
                             rwkv_g2[l], rwkv_k_k[l], rwkv_k_a[l], rwkv_r_k[l], rwkv_gn_g[l], rwkv_gn_b[l],
                             rwkv_out[l])
        y_b = mla_branch(p[..., o_mla:o_s5], cos, sin, mla_q_norm[l], mla_q_up[l], mla_kv_norm[l],
                         mla_kv_up[l], mla_out[l])
        y_c = s5_branch(p[..., o_s5:o_gate], s5_lambda_re[l], s5_lambda_im[l], s5_log_step[l], s5_b_re[l],
                        s5_b_im[l], s5_c_re[l], s5_c_im[l], s5_d[l], s5_glu[l])
        gates = jax.nn.sigmoid(p[..., o_gate:].reshape(bsz, t, N_BRANCHES, D_MODEL) + gate_b[l])
        merged = gates[..., 0, :] * y_a + gates[..., 1, :] * y_b + gates[..., 2, :] * y_c
        x = layer_norm(DEEPNORM_ALPHA * x + merged @ w_out[l], ln1_g[l], ln1_b[l])
        h = jax.nn.silu(x @ ffn_w1[l]) * (x @ ffn_w3[l])
        x = layer_norm(DEEPNORM_ALPHA * x + h @ ffn_w2[l], ln2_g[l], ln2_b[l])
    return x
```

```python
import contextlib
import numpy as np
import concourse.bass as bass
import concourse.mybir as mybir

F32 = mybir.dt.float32
BF16 = mybir.dt.bfloat16
ALU = mybir.AluOpType
AF = mybir.ActivationFunctionType
AX = mybir.AxisListType


class T:
    def __init__(self, name, ap):
        self.name = name
        self.ap = ap
        self.w = None
        self.r = {}

    def __getitem__(self, k):
        return self.ap[k]


class Prog:
    NDMA = 48

    def __init__(self, nc):
        self.nc = nc
        self.st = contextlib.ExitStack()
        self.engs = ["tensor", "vector", "scalar", "gpsimd", "sync"]
        self.ops = {e: [] for e in self.engs}
        self.cnt = {e: 0 for e in self.engs}
        self.known = {e: {} for e in self.engs}
        self.esem = {e: self.st.enter_context(nc.semaphore("es_" + e)) for e in self.engs}
        self.dsem = [self.st.enter_context(nc.semaphore(f"ds{i}")) for i in range(self.NDMA)]
        self.dcnt = [0] * self.NDMA
        self.csem = self.st.enter_context(nc.semaphore("cc_sem"))
        self.ccnt = 0
        self.dma_i = 0
        self.nuniq = 0

    ARENA_WORDS = 52800

    def use_arena(self):
        self.arena = self.st.enter_context(self.nc.sbuf_tensor("arena", [128, self.ARENA_WORDS], F32))
        self.aoff = 0

    def arena_reset(self):
        self.aoff = 0

    def sbuf(self, name, shape, dt=F32):
        if getattr(self, "arena", None) is None:
            t = self.st.enter_context(self.nc.sbuf_tensor(name, list(shape), dt))
            return T(name, t)
        shape = list(shape)
        nelem = 1
        for d in shape[1:]:
            nelem *= d
        four = dt in (F32, mybir.dt.int32)
        words = nelem if four else (nelem + 1) // 2
        assert self.aoff + words <= self.ARENA_WORDS, ("arena overflow", name, self.aoff, words)
        ap = self.arena[0:shape[0], self.aoff:self.aoff + words]
        self.aoff += words
        if dt != F32:
            ap = ap.bitcast(dt)
            if not four:
                ap = ap[:, 0:nelem]
        if len(shape) == 3:
            ap = ap.rearrange("p (a b) -> p a b", a=shape[1])
        elif len(shape) == 4:
            ap = ap.rearrange("p (a b c) -> p a b c", a=shape[1], b=shape[2])
        return T(name, ap)

    def barrier(self):
        for eng in self.engs:
            waits = []
            kn = self.known[eng]
            for e2 in self.engs:
                if e2 == eng or self.cnt[e2] == 0:
                    continue
                k = ("e", e2)
                if kn.get(k, 0) < self.cnt[e2]:
                    kn[k] = self.cnt[e2]
                    waits.append((k, self.cnt[e2]))
            for slot in range(self.NDMA):
                k = ("d", slot)
                if self.dcnt[slot] > 0 and kn.get(k, 0) < self.dcnt[slot]:
                    kn[k] = self.dcnt[slot]
                    waits.append((k, self.dcnt[slot]))
            if self.ccnt > 0 and kn.get(("c", 0), 0) < self.ccnt:
                kn[("c", 0)] = self.ccnt
                waits.append((("c", 0), self.ccnt))
            self.ops[eng].append((None, waits, None))

    def psum(self, name, shape, dt=F32):
        t = self.st.enter_context(self.nc.psum_tensor(name, list(shape), dt))
        return T(name, t)

    def dram(self, name, shape, dt, kind="Internal"):
        t = self.nc.dram_tensor(name, list(shape), dt, kind=kind)
        return T(name, t.ap())

    def sub(self, t, name, key):
        return T(name, t.ap[key])

    def _tokkey(self, tok):
        return (tok[0], tok[1])

    def op(self, eng, fn, reads=(), writes=(), inc=True, dma=False, touch=(), amt=16):
        deps = {}

        def add(tok):
            if tok is None:
                return
            k = self._tokkey(tok)
            if deps.get(k, 0) < tok[2]:
                deps[k] = tok[2]

        for t in reads:
            add(t.w)
        for t in list(writes) + list(touch):
            add(t.w)
            for k, v in t.r.items():
                add((k[0], k[1], v))
        if dma and amt == 1:
            self.ccnt += 1
            tok = ("c", 0, self.ccnt, 1)
        elif dma:
            slot = self.dma_i % self.NDMA
            self.dma_i += 1
            if self.dcnt[slot] > 0:
                add(("d", slot, self.dcnt[slot]))
            self.dcnt[slot] += amt
            tok = ("d", slot, self.dcnt[slot], amt)
        else:
            tok = ("e", eng, self.cnt[eng] + 1)
            if inc:
                self.cnt[eng] += 1
        waits = []
        kn = self.known[eng]
        for k, v in deps.items():
            if k[0] == "e" and k[1] == eng:
                if eng == "tensor":
                    continue
                assert v <= self.cnt[eng] or (v == tok[2] and False), (eng, v, self.cnt[eng])
            if kn.get(k, 0) >= v:
                continue
            kn[k] = v
            waits.append((k, v))
        self.ops[eng].append((fn, waits, tok if (inc or dma) else None))
        for t in reads:
            k = self._tokkey(tok)
            if t.r.get(k, 0) < tok[2]:
                t.r[k] = tok[2]
        for t in writes:
            t.w = tok
            t.r = {}
        return tok

    def wait_tiles(self, eng, tiles):
        self.op(eng, None, reads=tiles, inc=False)

    def wait_all_dma(self, eng="sync"):
        waits = []
        for slot in range(self.NDMA):
            if self.dcnt[slot] > 0 and self.known[eng].get(("d", slot), 0) < self.dcnt[slot]:
                waits.append((("d", slot), self.dcnt[slot]))
                self.known[eng][("d", slot)] = self.dcnt[slot]
        if self.ccnt > 0 and self.known[eng].get(("c", 0), 0) < self.ccnt:
            self.known[eng][("c", 0)] = self.ccnt
            waits.append((("c", 0), self.ccnt))
        self.ops[eng].append((None, waits, None))

    def _sem(self, k):
        if k[0] == "c":
            return self.csem
        return self.esem[k[1]] if k[0] == "e" else self.dsem[k[1]]

    def emit(self):
        nc = self.nc
        with nc.Block() as block:
            def mk(name):
                def body(e):
                    for fn, waits, tok in self.ops[name]:
                        for k, v in waits:
                            e.wait_ge(self._sem(k), v)
                        if fn is None:
                            continue
                        ins = fn(e)
                        if tok is not None:
                            if tok[0] == "c":
                                ins.then_inc(self.csem, 1)
                            elif tok[0] == "d":
                                ins.then_inc(self.dsem[tok[1]], tok[3])
                            else:
                                ins.then_inc(self.esem[name], 1)
                return body
            block.tensor(mk("tensor"))
            block.vector(mk("vector"))
            block.scalar(mk("scalar"))
            block.gpsimd(mk("gpsimd"))
            block.sync(mk("sync"))

    def close(self):
        self.st.close()

    def dma(self, eng, out_ap, in_ap, reads=(), writes=()):
        return self.op(eng, lambda e: e.dma_start(out=out_ap, in_=in_ap), reads=reads, writes=writes, dma=True)

    def mm(self, out_ap, lhsT, rhs, start=True, stop=True, reads=(), writes=(), **kw):
        return self.op("tensor", lambda e: e.matmul(out_ap, lhsT, rhs, start=start, stop=stop, **kw),
                       reads=reads, writes=writes if stop else (), touch=() if stop else writes, inc=True)

    def tr(self, out_ap, in_ap, ident_ap, reads=(), writes=(), inc=True):
        return self.op("tensor", lambda e: e.matmul(out_ap, in_ap, ident_ap, start=True, stop=True), reads=reads, writes=writes if inc else (), inc=inc)

    def act(self, out_ap, in_ap, func, reads=(), writes=(), eng="scalar", **kw):
        return self.op(eng, lambda e: e.activation(out_ap, in_ap, func, **kw), reads=reads, writes=writes)

    def tt(self, eng, out_ap, a, b, op, reads=(), writes=()):
        return self.op(eng, lambda e: e.tensor_tensor(out_ap, a, b, op), reads=reads, writes=writes)

    def ts(self, eng, out_ap, a, s1, s2, op0, op1=None, reads=(), writes=()):
        if op1 is None:
            return self.op(eng, lambda e: e.tensor_scalar(out_ap, a, s1, None, op0), reads=reads, writes=writes)
        return self.op(eng, lambda e: e.tensor_scalar(out_ap, a, s1, s2, op0, op1), reads=reads, writes=writes)

    def stt(self, eng, out_ap, in0, scalar, in1, op0, op1, reads=(), writes=()):
        return self.op(eng, lambda e: e.scalar_tensor_tensor(out_ap, in0, scalar, in1, op0, op1), reads=reads, writes=writes)

    def copy(self, eng, out_ap, in_ap, reads=(), writes=()):
        if eng == "scalar":
            return self.op(eng, lambda e: e.copy(out_ap, in_ap), reads=reads, writes=writes)
        return self.op(eng, lambda e: e.tensor_copy(out_ap, in_ap), reads=reads, writes=writes)

    def memset(self, eng, ap, val, writes=()):
        return self.op(eng, lambda e: e.memset(ap, val), writes=writes)


import math, os
RSTOP = int(os.environ.get('RSTOP', '9'))
CSTOP = int(os.environ.get('CSTOP', '99'))
G2V = os.environ.get('G2V', '')
import numpy as np

TT = 256
NTILE = 32
QB = TT // 128
SUB = 256
CH = 64
ATTN_SCALE = 1.0 / math.sqrt(96.0)
NEG_EXPM05 = -math.exp(-0.5)


def build_mixer(ntile=NTILE, do_rwkv=True, do_mla=True, do_s5=True, ctx=None):
    if ctx is None:
        nc = bass.Bass("TRN2", target_bir_lowering=False)
        P = Prog(nc)
        din = lambda n, s: P.dram(n, s, F32, kind="ExternalInput")
    else:
        P = ctx["P"]; din = ctx["din"]
    xT_d = din("xT", [1024, 8192])
    wmix_d = din("wmix", [1024, 1408])
    vecs_d = din("vecs", [128, 16])
    w2a2_d = din("w2a2", [128, 128])
    g2_d = din("g2", [128, 128])
    gng_d = din("gng", [128, 64])
    gnb_d = din("gnb", [128, 64])
    qup_d = din("qup", [256, 4 * 96])
    kvupk_d = din("kvupk", [128, 128])
    kvupv_d = din("kvupv", [128, 128])
    rope_d = din("rope", [2, 32, 8192])
    ident_d = din("ident", [128, 128])
    maskb_d = din("maskb", [128, 640])
    bdm_d = din("bdm", [128, 2])
    blk_d = din("blkones", [128, 128])
    tri_d = din("tri", [128, 128])
    rmask_d = din("rmask", [128, SUB])
    jm_d = din("jm", [128, 128])
    s5sp_d = din("s5sp", [128, 24])
    s5gc_d = din("s5gc", [128, 4 * 64 + 1])
    gmask_d = din("gmask", [128, 8])
    cmain_d = din("cmain", [128, 128])
    cswap_d = din("cswap", [128, 128])
    sign1_d = din("sign1", [128, 1])
    tidx_d = din("tidx", [128, SUB])
    if ctx is None:
        orw_d = P.dram("orw", [8192, 128], F32, kind="ExternalOutput")
        omla_d = P.dram("omla", [8192, 128], F32, kind="ExternalOutput")
        os5_d = P.dram("os5", [128, 8192], F32, kind="ExternalOutput")
    else:
        orw_d, omla_d, os5_d = ctx["outs"]
    if ctx is not None and ctx.get("oap") is not None:
        oap = ctx["oap"]
    else:
        def oap(kind, t0, n):
            if kind == "orw":
                return orw_d.ap[t0:t0 + n, :]
            if kind == "omla":
                return omla_d.ap[t0:t0 + n, :]
            return os5_d.ap[:, t0:t0 + n]

    sb = P.sbuf
    def load(name, d, shape, dt=F32, eng="sync"):
        t = sb(name, shape, dt)
        P.dma("gpsimd" if dt != F32 else eng, t[:], d[:], writes=[t])
        return t
    ident = load("ident_s", ident_d, [128, 128])
    maskb = load("maskb_s", maskb_d, [128, 640])
    bdm = load("bdm_s", bdm_d, [128, 2])
    blk = load("blk_s", blk_d, [128, 128])
    tri = load("tri_s", tri_d, [128, 128], BF16)
    rmask = load("rmask_s", rmask_d, [128, SUB])
    jm = load("jm_s", jm_d, [128, 128])
    vecs = load("vecs_s", vecs_d, [128, 16])
    w2a2 = load("w2a2_s", w2a2_d, [128, 128])
    g2 = load("g2_s", g2_d, [128, 128])
    gng = load("gng_s", gng_d, [128, 64])
    gnb = load("gnb_s", gnb_d, [128, 64])
    kvupk = load("kvupk_s", kvupk_d, [128, 128], BF16)
    kvupv = load("kvupv_s", kvupv_d, [128, 128], BF16)
    qup = sb("qup_s", [128, 2, 384], BF16)
    P.dma("gpsimd", qup[:], qup_d.ap.rearrange("(k p) m -> p k m", p=128), writes=[qup])
    wmix = sb("wmix_s", [128, 8, 1408], BF16)
    for k in range(8):
        P.dma("gpsimd", wmix[:, k, :], wmix_d[k * 128:(k + 1) * 128, :], writes=[wmix])
    ones = sb("ones_s", [128, 128])
    P.memset("vector", ones[:], 1.0, writes=[ones])

    ps = [P.psum(f"ps{i}", [128, 512]) for i in range(8)] if ctx is None else ctx["ps"]
    gctr = [0]

    def gen_ps():
        gctr[0] += 1
        return ps[4 + gctr[0] % 4]

    V = lambda c: vecs[:, c:c + 1]

    xbf = sb("xbf", [128, 8, TT], BF16)
    praw = [sb(f"praw{m}", [128, TT + 1]) for m in range(5)]
    for m in range(5):
        P.memset("vector", praw[m][:, 0:1], 0.0, writes=[praw[m]])
    dtmp = sb("dtmp", [128, TT])
    cct = sb("cct", [128, TT])
    sst = sb("sst", [128, TT])

    if do_s5:
        s5sp = load("s5sp_s", s5sp_d, [128, 24])
        s5gc = load("s5gc_s", s5gc_d, [128, 257])
        gmask = load("gmask_s", gmask_d, [128, 8])
        cmain = load("cmain_s", cmain_d, [128, 128])
        cswap = load("cswap_s", cswap_d, [128, 128])
        sign1 = load("sign1_s", sign1_d, [128, 1])
        tidx = load("tidx_s", tidx_d, [128, SUB])
        TWO_PI = 2.0 * math.pi

        scs = {}

        def sincos(name, ang, shape, want_cos, out=None):
            key = tuple(shape)
            if key not in scs:
                scs[key] = (sb(f"scf{len(scs)}", shape), sb(f"sci{len(scs)}", shape, mybir.dt.int32), sb(f"scg{len(scs)}", shape))
            f, fi, g = scs[key]
            o = sb(name + "_o", shape) if out is None else out
            P.ts("vector", f[:], ang[:], 1.0 / TWO_PI, 0.25 if want_cos else 0.0, ALU.mult, ALU.add, reads=[ang], writes=[f])
            P.copy("vector", fi[:], f[:], reads=[f], writes=[fi])
            P.copy("vector", g[:], fi[:], reads=[fi], writes=[g])
            P.tt("vector", f[:], f[:], g[:], ALU.subtract, reads=[f, g], writes=[f])
            P.ts("vector", g[:], f[:], 0.5, None, ALU.is_ge, reads=[f], writes=[g])
            P.tt("vector", f[:], f[:], g[:], ALU.subtract, reads=[f, g], writes=[f])
            P.ts("vector", g[:], f[:], -0.5, None, ALU.is_lt, reads=[f], writes=[g])
            P.tt("vector", f[:], f[:], g[:], ALU.add, reads=[f, g], writes=[f])
            oap = o[:] if out is None else out_ap[0]
            P.act(oap, f[:], AF.Sin, scale=TWO_PI, reads=[f], writes=[o])
            return o

        step_sp = sb("step_sp", [128, 8])
        P.act(step_sp[:], s5sp[:, 16:24], AF.Exp, reads=[s5sp], writes=[step_sp])
        lre_sp = sb("lre_sp", [128, 8])
        P.ts("vector", lre_sp[:], s5sp[:, 0:8], -1e-4, None, ALU.min, reads=[s5sp], writes=[lre_sp])
        P.tt("vector", lre_sp[:], lre_sp[:], step_sp[:], ALU.mult, reads=[lre_sp, step_sp], writes=[lre_sp])
        mag_sp = sb("mag_sp", [128, 8])
        P.act(mag_sp[:], lre_sp[:], AF.Exp, reads=[lre_sp], writes=[mag_sp])
        th_sp = sb("th_sp", [128, 8])
        P.tt("vector", th_sp[:], s5sp[:, 8:16], step_sp[:], ALU.mult, reads=[s5sp, step_sp], writes=[th_sp])
        thf = sb("thf", [128, 8]); thi = sb("thi", [128, 8], mybir.dt.int32); thg = sb("thg", [128, 8])
        P.ts("vector", thf[:], th_sp[:], 1.0 / TWO_PI, None, ALU.mult, reads=[th_sp], writes=[thf])
        P.copy("vector", thi[:], thf[:], reads=[thf], writes=[thi])
        P.copy("vector", thg[:], thi[:], reads=[thi], writes=[thg])
        P.tt("vector", thf[:], thf[:], thg[:], ALU.subtract, reads=[thf, thg], writes=[thf])
        ctab = sb("ctab", [128, 8, SUB]); stab = sb("stab", [128, 8, SUB])
        angt = sb("angt", [128, SUB])
        for g in range(8):
            P.ts("vector", angt[:], tidx[:], thf[:, g:g + 1], TWO_PI, ALU.mult, ALU.mult, reads=[tidx, thf], writes=[angt])
            P.ts("vector", angt[:], angt[:], TWO_PI * (SUB + 1), None, ALU.add, reads=[angt], writes=[angt])
            out_ap = [stab[:, g, :]]
            sincos(f"sc_s{g}", angt, [128, SUB], False, out=stab)
            out_ap = [ctab[:, g, :]]
            sincos(f"sc_c{g}", angt, [128, SUB], True, out=ctab)
        step_gc = sb("step_gc", [128, 1])
        P.act(step_gc[:], s5gc[:, 256:257], AF.Exp, reads=[s5gc], writes=[step_gc])
        lre = sb("lre_gc", [128, 64])
        P.ts("vector", lre[:], s5gc[:, 0:64], -1e-4, None, ALU.min, reads=[s5gc], writes=[lre])
        lim = s5gc
        magg = sb("mag_gc", [128, 64])
        P.act(magg[:], lre[:], AF.Exp, scale=step_gc[:, 0:1], reads=[lre, step_gc], writes=[magg])
        angg = sb("ang_gc", [128, 64])
        P.ts("vector", angg[:], s5gc[:, 64:128], step_gc[:, 0:1], None, ALU.mult, reads=[s5gc, step_gc], writes=[angg])
        sing = sincos("sg", angg, [128, 64], False)
        cosg = sincos("cg", angg, [128, 64], True)
        lbre = sb("lbre", [128, 64]); lbim = sb("lbim", [128, 64])
        P.tt("vector", lbre[:], magg[:], cosg[:], ALU.mult, reads=[magg, cosg], writes=[lbre])
        P.tt("vector", lbim[:], magg[:], sing[:], ALU.mult, reads=[magg, sing], writes=[lbim])
        den = sb("den", [128, 64]); t1 = sb("s5t1", [128, 64]); t2 = sb("s5t2", [128, 64])
        P.tt("vector", den[:], lre[:], lre[:], ALU.mult, reads=[lre], writes=[den])
        P.tt("vector", t1[:], s5gc[:, 64:128], s5gc[:, 64:128], ALU.mult, reads=[s5gc], writes=[t1])
        P.tt("vector", den[:], den[:], t1[:], ALU.add, reads=[den, t1], writes=[den])
        P.op("vector", lambda e: e.reciprocal(den[:], den[:]), reads=[den], writes=[den])
        nre = sb("nre", [128, 64])
        P.ts("vector", nre[:], lbre[:], -1.0, None, ALU.add, reads=[lbre], writes=[nre])
        fre = sb("fre", [128, 64]); fim = sb("fim", [128, 64])
        P.tt("vector", t1[:], nre[:], lre[:], ALU.mult, reads=[nre, lre], writes=[t1])
        P.tt("vector", t2[:], lbim[:], s5gc[:, 64:128], ALU.mult, reads=[lbim, s5gc], writes=[t2])
        P.tt("vector", t1[:], t1[:], t2[:], ALU.add, reads=[t1, t2], writes=[t1])
        P.tt("vector", fre[:], t1[:], den[:], ALU.mult, reads=[t1, den], writes=[fre])
        P.tt("vector", t1[:], lbim[:], lre[:], ALU.mult, reads=[lbim, lre], writes=[t1])
        P.tt("vector", t2[:], nre[:], s5gc[:, 64:128], ALU.mult, reads=[nre, s5gc], writes=[t2])
        P.tt("vector", t1[:], t1[:], t2[:], ALU.subtract, reads=[t1, t2], writes=[t1])
        P.tt("vector", fim[:], t1[:], den[:], ALU.mult, reads=[t1, den], writes=[fim])
        bbre = sb("bbre", [128, 64]); bbim = sb("bbim", [128, 64])
        bre = s5gc[:, 128:192]; bim = s5gc[:, 192:256]
        P.tt("vector", t1[:], fre[:], bre, ALU.mult, reads=[fre, s5gc], writes=[t1])
        P.tt("vector", t2[:], fim[:], bim, ALU.mult, reads=[fim, s5gc], writes=[t2])
        P.tt("vector", bbre[:], t1[:], t2[:], ALU.subtract, reads=[t1, t2], writes=[bbre])
        P.tt("vector", t1[:], fre[:], bim, ALU.mult, reads=[fre, s5gc], writes=[t1])
        P.tt("vector", t2[:], fim[:], bre, ALU.mult, reads=[fim, s5gc], writes=[t2])
        P.tt("vector", bbim[:], t1[:], t2[:], ALU.add, reads=[t1, t2], writes=[bbim])
        W1 = sb("s5W1", [128, 8, 128]); W2 = sb("s5W2", [128, 8, 128])
        ngm = sb("ngmask", [128, 8])
        P.ts("vector", ngm[:], gmask[:], -1.0, None, ALU.mult, reads=[gmask], writes=[ngm])
        for g in range(8):
            P.ts("vector", W1[:, g, 0:64], bbre[:], gmask[:, g:g + 1], None, ALU.mult, reads=[bbre, gmask], writes=[W1])
            P.ts("vector", W1[:, g, 64:128], bbim[:], gmask[:, g:g + 1], None, ALU.mult, reads=[bbim, gmask], writes=[W1])
            P.ts("vector", W2[:, g, 0:64], bbim[:], gmask[:, g:g + 1], None, ALU.mult, reads=[bbim, gmask], writes=[W2])
            P.ts("vector", W2[:, g, 64:128], bbre[:], ngm[:, g:g + 1], None, ALU.mult, reads=[bbre, ngm], writes=[W2])
        wc1 = sb("wc1", [128, 128]); wc2 = sb("wc2", [128, 128])
        P.ts("vector", wc1[:], cmain[:], sign1[:, 0:1], None, ALU.mult, reads=[cmain, sign1], writes=[wc1])
        P.ts("vector", wc2[:], cswap[:], -1.0, None, ALU.mult, reads=[cswap], writes=[wc2])
        Wc1g = sb("Wc1g", [128, 8, 128]); Wc2g = sb("Wc2g", [128, 8, 128])
        P.memset("vector", Wc1g[:], 0.0, writes=[Wc1g]); P.memset("vector", Wc2g[:], 0.0, writes=[Wc2g])
        for g in range(8):
            P.copy("vector", Wc1g[:, g, g * 16:(g + 1) * 16], wc1[:, g * 16:(g + 1) * 16], reads=[wc1], writes=[Wc1g])
            P.copy("vector", Wc2g[:, g, g * 16:(g + 1) * 16], wc2[:, g * 16:(g + 1) * 16], reads=[wc2], writes=[Wc2g])
        h0 = sb("s5h0", [128, 8])
        P.memset("vector", h0[:], 0.0, writes=[h0])
        ubuf = sb("s5u", [128, TT])
        s5x = [sb(f"s5x{i}", [128, SUB]) for i in range(4)]
        s5y = sb("s5y", [128, SUB]); s5y2 = sb("s5y2", [128, SUB]); s5o = sb("s5o", [128, SUB])

    if do_mla:
        Kt = [[P.sbuf(f"Kt{h}_{i}", [96, TT], BF16) for i in range(ntile)] for h in range(2)]
        Vaug = [P.sbuf(f"Va{i}", [128, 2, 65], BF16) for i in range(ntile * QB)]
        for va in Vaug:
            P.memset("gpsimd", va[:, :, 64:65], 1.0, writes=[va])
        Qt = [sb(f"Qt{h}", [96, TT], BF16) for h in range(2)]
        cq = sb("cq", [128, 2, TT]); sq = sb("sq", [128, 2, TT]); rstd = sb("rstd", [128, TT])
        cqn = sb("cqn", [128, 2, TT], BF16)
        ckv = sb("ckv", [128, TT]); sqk = sb("sqk", [128, TT]); rstdk = sb("rstdk", [128, TT]); ckvn = sb("ckvn", [128, TT], BF16)
        kr1 = dtmp; kr2 = sb("kr2", [128, TT])
        qr1 = dtmp; qr2 = kr2
        Pt = [sb(f"Pt{i}", [128, TT], BF16) for i in range(3)]
        osb = sb("osb", [128, QB, 64]); orec = sb("orec", [128, QB, 1])
        pctr = [0]

    if do_rwkv:
        f2 = lambda n: sb(n, [128, SUB])
        LW = f2("LW"); AS = f2("AS"); GF = f2("GF"); KK = f2("KK"); D2 = f2("D2"); D3 = f2("D3"); KM = f2("KM"); BV = f2("BV")
        CU = f2("CU"); E1 = f2("E1"); E2 = f2("E2"); RT = f2("RT"); KT_ = f2("KT_"); AT = f2("AT"); BT = f2("BT")
        KHF = f2("KHF"); BHF = f2("BHF"); RK = f2("RK")
        gC = sb("gC", [128, 4])
        BD = [[sb(f"BD{q}_{r}", [128, 2, 64]) for r in range(1)] for q in range(9)]
        AM = [sb(f"AM{r}", [128, 640]) for r in range(1)]
        TM = [sb(f"TM{r}", [128, 640]) for r in range(1)]
        XS = [sb(f"XS{r}", [128, 256]) for r in range(3)]
        PS_ = [sb(f"PS{r}", [128, 256]) for r in range(3)]
        RH = [sb(f"RH{r}", [128, 128]) for r in range(1)]
        GT = [sb(f"GT{r}", [128, 128]) for r in range(1)]
        YH = [sb(f"YH{r}", [128, 256]) for r in range(5)]
        P.memset("vector", YH[4][:], 0.0, writes=[YH[4]])
        Vc = sb("Vc", [128, 4, 64]); Gc = sb("Gc", [128, 4, 64]); Yc = sb("Yc", [128, 4, 64]); Ysq = sb("Ysq", [128, 4, 64])
        coef = sb("coef", [128, 4])
        st1 = sb("st1", [128, 4]); st2 = sb("st2", [128, 4]); st3 = sb("st3", [128, 4])
        yo = sb("yo", [128, 4, 64])
        hstate = [4]

    for it in range(ntile):
        c0 = it * TT
        if ctx is None or ctx.get("xload") is None:
            P.dma("gpsimd", xbf[:], xT_d.ap.rearrange("(k p) t -> p k t", p=128)[:, :, c0:c0 + TT], writes=[xbf])
        else:
            ctx["xload"](xbf, c0)
        if do_mla:
            P.dma("sync", cct[64:96, :], rope_d[0, :, c0:c0 + TT], writes=[cct])
            P.dma("sync", sst[64:96, :], rope_d[1, :, c0:c0 + TT], writes=[sst])
        for m in range(11):
            if m < 5 and not do_rwkv:
                continue
            if m in (5, 6, 7, 8, 10) and not do_mla:
                continue
            if m == 9 and not do_s5:
                continue
            pb = gen_ps()
            for k in range(8):
                P.mm(pb[:, 0:TT], wmix[:, k, m * 128:(m + 1) * 128], xbf[:, k, :], start=(k == 0), stop=(k == 7), reads=[wmix, xbf], writes=[pb])
            if m < 5:
                pr = praw[m]
                P.act(pr[:, 1:TT + 1], pb[:, 0:TT], AF.Copy, reads=[pb], writes=[pr])
                P.tt("vector", dtmp[:], pr[:, 0:TT], pr[:, 1:TT + 1], ALU.subtract, reads=[pr], writes=[dtmp])
                P.copy("vector", pr[:, 0:1], pr[:, TT:TT + 1], reads=[pr], writes=[pr])
                P.stt("vector", pr[:, 1:TT + 1], dtmp[:], V(m), pr[:, 1:TT + 1], ALU.mult, ALU.add, reads=[dtmp, vecs, pr], writes=[pr])
            elif m in (5, 6):
                j = m - 5
                P.act(cq[:, j, :], pb[:, 0:TT], AF.Copy, reads=[pb], writes=[cq])
                P.act(sq[:, j, :], pb[:, 0:TT], AF.Square, reads=[pb], writes=[sq])
            elif m == 7:
                P.act(ckv[:], pb[:, 0:TT], AF.Copy, reads=[pb], writes=[ckv])
                P.act(sqk[:], pb[:, 0:TT], AF.Square, reads=[pb], writes=[sqk])
            elif m == 8:
                P.tt("vector", kr1[64:96, :], pb[64:96, 0:TT], cct[64:96, :], ALU.mult, reads=[pb, cct], writes=[kr1])
            elif m == 10:
                P.tt("vector", kr2[64:96, :], pb[64:96, 0:TT], sst[64:96, :], ALU.mult, reads=[pb, sst], writes=[kr2])
                P.tt("vector", Kt[0][it][64:96, :], kr1[64:96, :], kr2[64:96, :], ALU.add, reads=[kr1, kr2], writes=[Kt[0][it]])
                P.tt("vector", Kt[1][it][64:96, :], kr1[64:96, :], kr2[64:96, :], ALU.add, reads=[kr1, kr2], writes=[Kt[1][it]])
            elif m == 9:
                P.act(ubuf[:], pb[:, 0:TT], AF.Copy, reads=[pb], writes=[ubuf])

        if do_rwkv:
            for s in range(TT // SUB):
                o = 1 + s * SUB
                r_ = praw[0][:, o:o + SUB]; k_ = praw[1][:, o:o + SUB]; v_ = praw[2][:, o:o + SUB]
                dwa = praw[3]; dg = praw[4]
                P.act(dwa[0:64, o:o + SUB], dwa[0:64, o:o + SUB], AF.Tanh, reads=[dwa], writes=[dwa])
                pb = gen_ps()
                P.mm(pb[:, 0:SUB], w2a2[0:64, :], dwa[0:64, o:o + SUB], reads=[w2a2, dwa], writes=[pb])
                P.act(LW[:], pb[:, 0:SUB], AF.Sigmoid, bias=V(5), reads=[pb, vecs], writes=[LW])
                P.ts("vector", LW[:], LW[:], NEG_EXPM05, None, ALU.mult, reads=[LW], writes=[LW])
                pb = gen_ps()
                P.mm(pb[:, 0:SUB], w2a2[64:128, :], dwa[64:128, o:o + SUB], reads=[w2a2, dwa], writes=[pb])
                P.act(AS[:], pb[:, 0:SUB], AF.Sigmoid, bias=V(6), reads=[pb, vecs], writes=[AS])
                P.act(dg[:, o:o + SUB], dg[:, o:o + SUB], AF.Sigmoid, reads=[dg], writes=[dg])
                pb = gen_ps()
                P.mm(pb[:, 0:SUB], g2[:, :], dg[:, o:o + SUB], reads=[g2, dg], writes=[pb])
                P.act(GF[:], pb[:, 0:SUB], AF.Copy, reads=[pb], writes=[GF])
                P.ts("vector", KK[:], k_, V(7), None, ALU.mult, reads=[praw[1], vecs], writes=[KK])
                P.tt("vector", D2[:], KK[:], KK[:], ALU.mult, reads=[KK], writes=[D2])
                pb = gen_ps()
                P.mm(pb[:, 0:SUB], blk[:, :], D2[:], reads=[blk, D2], writes=[pb])
                P.ts("vector", D2[:], pb[:, 0:SUB], 1e-12, None, ALU.max, reads=[pb], writes=[D2])
                P.act(D2[:], D2[:], AF.Sqrt, reads=[D2], writes=[D2])
                P.op("vector", lambda e: e.reciprocal(D2[:], D2[:]), reads=[D2], writes=[D2])
                P.tt("vector", KK[:], KK[:], D2[:], ALU.mult, reads=[KK, D2], writes=[KK])
                P.ts("vector", D2[:], AS[:], 1.0, V(8), ALU.subtract, ALU.mult, reads=[AS, vecs], writes=[D2])
                P.stt("vector", KM[:], D2[:], 1.0, k_, ALU.add, ALU.mult, reads=[D2, praw[1]], writes=[KM])
                P.tt("vector", BV[:], KK[:], AS[:], ALU.mult, reads=[KK, AS], writes=[BV])
                P.stt("vector", RK[:], r_, V(9), KM[:], ALU.mult, ALU.mult, reads=[praw[0], vecs, KM], writes=[RK])
                P.op("vector", lambda e: e.tensor_tensor_scan(CU[:], rmask[:], LW[:], 0.0, ALU.mult, ALU.add), reads=[rmask, LW], writes=[CU])
                cu3 = CU[:].rearrange("p (c t) -> p c t", t=CH)
                P.act(gC[:], CU[:].rearrange("p (c t) -> p c t", t=CH)[:, :, CH - 1], AF.Exp, reads=[CU], writes=[gC])
                P.act(E1[:], CU[:], AF.Exp, reads=[CU], writes=[E1])
                P.tt("vector", RT[:], r_, E1[:], ALU.mult, reads=[praw[0], E1], writes=[RT])
                P.act(E2[:], CU[:], AF.Exp, scale=-1.0, reads=[CU], writes=[E2])
                P.tt("vector", KT_[:], KM[:], E2[:], ALU.mult, reads=[KM, E2], writes=[KT_])
                P.tt("vector", BT[:], BV[:], E2[:], ALU.mult, reads=[BV, E2], writes=[BT])
                P.tt("vector", D3[:], CU[:], LW[:], ALU.subtract, reads=[CU, LW], writes=[D3])
                P.act(E1[:], D3[:], AF.Exp, reads=[D3], writes=[E1])
                P.stt("vector", AT[:], KK[:], -1.0, E1[:], ALU.mult, ALU.mult, reads=[KK, E1], writes=[AT])
                P.tt("vector", D3[:].rearrange("p (c t) -> p c t", t=CH), cu3[:, :, CH - 1:CH].to_broadcast([128, SUB // CH, CH]), cu3, ALU.subtract, reads=[CU], writes=[D3])
                P.act(E2[:], D3[:], AF.Exp, reads=[D3], writes=[E2])
                P.tt("vector", KHF[:], KM[:], E2[:], ALU.mult, reads=[KM, E2], writes=[KHF])
                P.tt("vector", BHF[:], BV[:], E2[:], ALU.mult, reads=[BV, E2], writes=[BHF])
                if RSTOP < 2:
                    continue
                srcs = [(RT, None), (KT_, None), (AT, None), (BT, None), (KHF, None), (BHF, None), (praw[2], v_), (GF, None), (RK, None)]
                for c in range(SUB // CH):
                    cs = slice(c * CH, (c + 1) * CH)
                    rot = 0
                    bd = []
                    for q, (tl, view) in enumerate(srcs):
                        src = (view if view is not None else tl[:])[:, cs]
                        dst = BD[q][rot]
                        P.tt("gpsimd", dst[:], src.unsqueeze(1).to_broadcast([128, 2, CH]), bdm[:, 0:2].unsqueeze(2).to_broadcast([128, 2, CH]),
                             ALU.mult, reads=[tl, bdm], writes=[dst])
                        bd.append(dst)
                    f = lambda t: t[:].rearrange("p a b -> p (a b)")
                    Rb, Kb, Ab, Bb, KHb, BHb, Vb, Gb, RKb = bd
                    am = AM[rot]; tm = TM[rot]
                    pb = gen_ps()
                    P.mm(pb[:, 0:128], f(Bb), f(Ab), reads=[Bb, Ab], writes=[pb])
                    P.mm(pb[:, 128:256], f(Ab), f(Bb), reads=[Bb, Ab], writes=[pb])
                    P.mm(pb[:, 256:384], f(Kb), f(Ab), reads=[Kb, Ab], writes=[pb])
                    P.mm(pb[:, 384:512], f(Bb), f(Rb), reads=[Bb, Rb], writes=[pb])
                    P.tt("vector", am[:, 0:512], pb[:, 0:512], maskb[:, 0:512], ALU.mult, reads=[pb, maskb], writes=[am])
                    if CSTOP <= 1:
                        continue
                    pb = gen_ps()
                    if G2V != 'B':
                        P.mm(pb[:, 0:128], f(Kb), f(Rb), reads=[Kb, Rb], writes=[pb])
                    if G2V != 'A':
                        P.tr(pb[:, 128:256], f(KHb), ident[:], reads=[KHb, ident], writes=[pb])
                        P.tr(pb[:, 256:384], f(BHb), ident[:], reads=[BHb, ident], writes=[pb])
                        P.tr(pb[:, 384:512], f(Vb), ident[:], reads=[Vb, ident], writes=[pb])
                    if G2V != 'B':
                        P.tt("vector", am[:, 512:640], pb[:, 0:128], maskb[:, 512:640], ALU.mult, reads=[pb, maskb], writes=[am])
                    if G2V != 'A':
                        if True:
                            P.copy("vector", tm[:, 0:384], pb[:, 128:512], reads=[pb], writes=[tm])
                        else:
                            P.act(tm[:, 0:384], pb[:, 128:512], AF.Copy, reads=[pb], writes=[tm])
                    NTm = am[:, 0:128]; Nm = am[:, 128:256]; AakT = am[:, 256:384]; ArbT = am[:, 384:512]; ArkT = am[:, 512:640]
                    Kh = tm[:, 0:128]; Bh = tm[:, 128:256]; Vt = tm[:, 256:384]
                    if CSTOP <= 2:
                        continue
                    pb = gen_ps()
                    P.tr(pb[:, 0:128], f(Ab), ident[:], reads=[Ab, ident], writes=[pb])
                    P.tr(pb[:, 256:384], f(Gb), ident[:], reads=[Gb, ident], writes=[pb])
                    P.mm(pb[:, 128:256], AakT, Vt, reads=[am, tm], writes=[pb])
                    xs = XS[0]
                    P.act(xs[:], pb[:, 0:256], AF.Copy, reads=[pb], writes=[xs])
                    P.act(tm[:, 512:640], pb[:, 256:384], AF.Copy, reads=[pb], writes=[tm])
                    if CSTOP <= 3:
                        continue
                    pbc = gen_ps()
                    P.mm(pbc[:, 0:1], f(RKb), ones[:, 0:1], reads=[RKb, ones], writes=[pbc])
                    P.act(coef[:, c:c + 1], pbc[:, 0:1], AF.Copy, reads=[pbc], writes=[coef])
                    P.tt("gpsimd", Vc[:, c, :], tm[:, 256:320], tm[:, 320:384], ALU.add, reads=[tm], writes=[Vc])
                    P.tt("gpsimd", Gc[:, c, :], tm[:, 512:576], tm[:, 576:640], ALU.add, reads=[tm], writes=[Gc])
                    if CSTOP <= 4:
                        continue
                    pn_ap, pt_ap, pT = Nm, NTm, am
                    xi = 0
                    for i in range(6):
                        pb = gen_ps()
                        xcur = XS[xi % 3]; xnew = XS[(xi + 1) % 3]
                        P.mm(pb[:, 0:256], pt_ap, xcur[:], reads=[pT, xcur], writes=[pb])
                        if i < 5:
                            P.mm(pb[:, 256:384], pt_ap, pn_ap, reads=[pT], writes=[pb])
                            P.mm(pb[:, 384:512], pn_ap, pt_ap, reads=[pT], writes=[pb])
                        P.tt("vector", xnew[:], xcur[:], pb[:, 0:256], ALU.add, reads=[xcur, pb], writes=[xnew])
                        if i < 5:
                            pnew = PS_[i % 3]
                            P.copy("vector", pnew[:], pb[:, 256:512], reads=[pb], writes=[pnew])
                            pn_ap, pt_ap, pT = pnew[:, 0:128], pnew[:, 128:256], pnew
                        xi += 1
                    X = XS[xi % 3]
                    Ah = X[:, 0:128]; U0 = X[:, 128:256]
                    if CSTOP <= 5:
                        continue
                    pb = gen_ps()
                    P.mm(pb[:, 0:128], Ah, ArbT, reads=[X, am], writes=[pb])
                    P.mm(pb[:, 128:256], Ah, Bh, reads=[X, tm], writes=[pb])
                    rh = RH[rot]; gt = GT[rot]
                    P.tt("vector", rh[:], pb[:, 0:128], f(Rb), ALU.add, reads=[pb, Rb], writes=[rh])
                    P.stt("vector", gt[:], ident[:], gC[:, c:c + 1], pb[:, 128:256], ALU.mult, ALU.add, reads=[ident, gC, pb], writes=[gt])
                    if CSTOP <= 6:
                        continue
                    hcur = YH[hstate[0]]
                    hn_i = (hstate[0] + 1) % 5
                    yh = YH[hn_i]
                    pb = gen_ps()
                    P.mm(pb[:, 0:128], ArbT, U0, start=True, stop=False, reads=[am, X], writes=[pb])
                    P.mm(pb[:, 0:128], ArkT, Vt, start=False, stop=False, reads=[am, tm], writes=[pb])
                    P.mm(pb[:, 0:128], rh[:], hcur[:, 128:256], start=False, stop=True, reads=[rh, hcur], writes=[pb])
                    P.mm(pb[:, 128:256], Bh, U0, start=True, stop=False, reads=[tm, X], writes=[pb])
                    P.mm(pb[:, 128:256], Kh, Vt, start=False, stop=False, reads=[tm], writes=[pb])
                    P.mm(pb[:, 128:256], gt[:], hcur[:, 128:256], start=False, stop=True, reads=[gt, hcur], writes=[pb])
                    P.act(yh[:], pb[:, 0:256], AF.Copy, reads=[pb], writes=[yh])
                    hstate[0] = hn_i
                    P.tt("gpsimd", Yc[:, c, :], yh[:, 0:64], yh[:, 64:128], ALU.add, reads=[yh], writes=[Yc])
                if RSTOP < 3:
                    continue
                nchk = SUB // CH
                P.op("vector", lambda e: e.tensor_reduce(st1[:], Yc[:], AX.X, ALU.add), reads=[Yc], writes=[st1])
                P.tt("gpsimd", Ysq[:], Yc[:], Yc[:], ALU.mult, reads=[Yc], writes=[Ysq])
                P.op("vector", lambda e: e.tensor_reduce(st2[:], Ysq[:], AX.X, ALU.add), reads=[Ysq], writes=[st2])
                P.ts("vector", st1[:], st1[:], 1.0 / 64, None, ALU.mult, reads=[st1], writes=[st1])
                P.tt("vector", st3[:], st1[:], st1[:], ALU.mult, reads=[st1], writes=[st3])
                P.stt("vector", st2[:], st2[:], 1.0 / 64, st3[:], ALU.mult, ALU.subtract, reads=[st2, st3], writes=[st2])
                P.act(st2[:], st2[:], AF.Sqrt, bias=64e-5, reads=[st2], writes=[st2])
                P.op("vector", lambda e: e.reciprocal(st2[:], st2[:]), reads=[st2], writes=[st2])
                bc = lambda t: t[:, :].unsqueeze(2).to_broadcast([128, nchk, 64])
                bg = lambda t: t[:, :].unsqueeze(1).to_broadcast([128, nchk, 64])
                P.tt("vector", yo[:], Yc[:], bc(st1), ALU.subtract, reads=[Yc, st1], writes=[yo])
                P.tt("vector", yo[:], yo[:], bc(st2), ALU.mult, reads=[yo, st2], writes=[yo])
                P.tt("vector", yo[:], yo[:], bg(gng), ALU.mult, reads=[yo, gng], writes=[yo])
                P.tt("vector", yo[:], yo[:], bg(gnb), ALU.add, reads=[yo, gnb], writes=[yo])
                P.tt("vector", Ysq[:], Vc[:], bc(coef), ALU.mult, reads=[Vc, coef], writes=[Ysq])
                P.tt("vector", yo[:], yo[:], Ysq[:], ALU.add, reads=[yo, Ysq], writes=[yo])
                P.tt("vector", yo[:], yo[:], Gc[:], ALU.mult, reads=[yo, Gc], writes=[yo])
                t0 = c0 + s * SUB
                for h in range(2 if RSTOP >= 4 else 0):
                    P.dma("sync", oap("orw", t0, SUB)[:, h * 64:(h + 1) * 64].rearrange("(c t) v -> t c v", t=CH),
                          yo[h * 64:(h + 1) * 64, :, :], reads=[yo], writes=[orw_d])

        if do_s5:
            for s in range(TT // SUB):
                us = ubuf[:, s * SUB:(s + 1) * SUB]
                pby = ps[0]
                for g in range(8):
                    pb = gen_ps()
                    P.mm(pb[:, 0:SUB], W1[:, g, :], us, reads=[W1, ubuf], writes=[pb])
                    P.mm(pb[:, SUB:2 * SUB], W2[:, g, :], us, reads=[W2, ubuf], writes=[pb])
                    x1, x2, xt, ht = s5x
                    P.tt("vector", x1[:], pb[:, 0:SUB], ctab[:, g, :], ALU.mult, reads=[pb, ctab], writes=[x1])
                    P.tt("vector", x2[:], pb[:, SUB:2 * SUB], stab[:, g, :], ALU.mult, reads=[pb, stab], writes=[x2])
                    P.tt("gpsimd", xt[:], x1[:], x2[:], ALU.add, reads=[x1, x2], writes=[xt])
                    P.op("vector", lambda e, g=g, xt=xt, ht=ht: e.tensor_tensor_scan(ht[:], mag_sp[:, g:g + 1].to_broadcast([128, SUB]), xt[:], h0[:, g:g + 1], ALU.mult, ALU.add),
                         reads=[mag_sp, xt, h0], writes=[ht])
                    P.tt("gpsimd", x1[:], ht[:], ctab[:, g, :], ALU.mult, reads=[ht, ctab], writes=[x1])
                    P.tt("vector", x2[:], ht[:], stab[:, g, :], ALU.mult, reads=[ht, stab], writes=[x2])
                    P.mm(pby[:, 0:SUB], Wc1g[:, g, :], x1[:], start=(g == 0), stop=False, reads=[Wc1g, x1], writes=[pby])
                    P.mm(pby[:, 0:SUB], Wc2g[:, g, :], x2[:], start=False, stop=(g == 7), reads=[Wc2g, x2], writes=[pby] )
                    pbh = gen_ps()
                    P.mm(pbh[:, 0:1], ident[:], x1[:, SUB - 1:SUB], start=True, stop=False, reads=[ident, x1], writes=[pbh])
                    P.mm(pbh[:, 0:1], jm[:], x2[:, SUB - 1:SUB], start=False, stop=True, reads=[jm, x2], writes=[pbh])
                    P.act(h0[:, g:g + 1], pbh[:, 0:1], AF.Copy, reads=[pbh], writes=[h0])
                P.stt("vector", s5y[:], us, V(13), pby[:, 0:SUB], ALU.mult, ALU.add, reads=[ubuf, vecs, pby], writes=[s5y])
                P.act(s5o[:], s5y[:], AF.Gelu_apprx_tanh, reads=[s5y], writes=[s5o])
                t0 = c0 + s * SUB
                P.dma("sync", oap("os5", t0, SUB), s5o[:], reads=[s5o], writes=[os5_d])

        if do_mla:
            pb = gen_ps()
            P.mm(pb[:, 0:TT], ones[:, :], sq[:, 0, :], start=True, stop=False, reads=[ones, sq], writes=[pb])
            P.mm(pb[:, 0:TT], ones[:, :], sq[:, 1, :], start=False, stop=True, reads=[ones, sq], writes=[pb])
            P.act(rstd[:], pb[:, 0:TT], AF.Sqrt, scale=1.0 / 256, bias=1e-6, reads=[pb], writes=[rstd])
            P.op("vector", lambda e: e.reciprocal(rstd[:], rstd[:]), reads=[rstd], writes=[rstd])
            for j in range(2):
                P.stt("vector", cqn[:, j, :], cq[:, j, :], V(10 + j), rstd[:], ALU.mult, ALU.mult, reads=[cq, vecs, rstd], writes=[cqn])
            pb = gen_ps()
            P.mm(pb[:, 0:TT], ones[:, :], sqk[:], reads=[ones, sqk], writes=[pb])
            P.act(rstdk[:], pb[:, 0:TT], AF.Sqrt, scale=1.0 / 128, bias=1e-6, reads=[pb], writes=[rstdk])
            P.op("vector", lambda e: e.reciprocal(rstdk[:], rstdk[:]), reads=[rstdk], writes=[rstdk])
            P.stt("vector", ckvn[:], ckv[:], V(12), rstdk[:], ALU.mult, ALU.mult, reads=[ckv, vecs, rstdk], writes=[ckvn])
            for h in range(2):
                pb = gen_ps()
                P.mm(pb[0:64, 0:TT], kvupk[:, h * 64:(h + 1) * 64], ckvn[:], reads=[kvupk, ckvn], writes=[pb])
                P.act(Kt[h][it][0:64, :], pb[0:64, 0:TT], AF.Copy, reads=[pb], writes=[Kt[h][it]])
            pb = gen_ps()
            for kb in range(QB):
                P.mm(pb[:, kb * 128:(kb + 1) * 128], ckvn[:, kb * 128:(kb + 1) * 128], kvupv[:, :], reads=[ckvn, kvupv], writes=[pb])
            for kb in range(QB):
                va = Vaug[it * QB + kb]
                P.copy("vector", va[:, :, 0:64], pb[:, kb * 128:(kb + 1) * 128].rearrange("p (h v) -> p h v", h=2), reads=[pb], writes=[va])
            for h in range(2):
                pb = gen_ps()
                for j in range(2):
                    P.mm(pb[0:96, 0:TT], qup[:, j, h * 192:h * 192 + 96], cqn[:, j, :], start=(j == 0), stop=(j == 1), reads=[qup, cqn], writes=[pb])
                pb2 = gen_ps()
                for j in range(2):
                    P.mm(pb2[0:96, 0:TT], qup[:, j, h * 192 + 96:h * 192 + 192], cqn[:, j, :], start=(j == 0), stop=(j == 1), reads=[qup, cqn], writes=[pb2])
                P.ts("vector", Qt[h][0:64, :], pb[0:64, 0:TT], ATTN_SCALE, None, ALU.mult, reads=[pb], writes=[Qt[h]])
                P.stt("vector", qr1[64:96, :], pb[64:96, 0:TT], ATTN_SCALE, cct[64:96, :], ALU.mult, ALU.mult, reads=[pb, cct], writes=[qr1])
                P.stt("vector", qr2[64:96, :], pb2[64:96, 0:TT], ATTN_SCALE, sst[64:96, :], ALU.mult, ALU.mult, reads=[pb2, sst], writes=[qr2])
                P.tt("vector", Qt[h][64:96, :], qr1[64:96, :], qr2[64:96, :], ALU.add, reads=[qr1, qr2], writes=[Qt[h]])
            for h in range(2):
                nkb = QB * it + QB
                for kb in range(nkb):
                    kt = Kt[h][kb // QB]
                    kcol = (kb % QB) * 128
                    sbk = ps[2 + kb % 2]
                    diag = kb >= QB * it
                    qb0 = kb - QB * it if diag else 0
                    q0 = qb0 * 128
                    P.mm(sbk[:, q0:TT], kt[:, kcol:kcol + 128], Qt[h][:, q0:TT], reads=[kt, Qt[h]], writes=[sbk])
                    pt = Pt[pctr[0] % 3]; pctr[0] += 1
                    P.act(pt[:, q0:TT], sbk[:, q0:TT], AF.Exp, reads=[sbk], writes=[pt])
                    if diag:
                        P.tt("vector", pt[:, q0:q0 + 128], pt[:, q0:q0 + 128], tri[:, :], ALU.mult, reads=[pt, tri], writes=[pt])
                    va = Vaug[kb]
                    for qb in range(qb0, QB):
                        last = (kb == QB * it + qb)
                        ob = ps[qb]
                        P.mm(ob[:, 0:65], pt[:, qb * 128:(qb + 1) * 128], va[:, h, :], start=(kb == 0), stop=last,
                             reads=[pt, va], writes=[ob])
                for qb in range(QB):
                    ob = ps[qb]
                    P.op("vector", lambda e, ob=ob, qb=qb: e.reciprocal(orec[:, qb, :], ob[:, 64:65]), reads=[ob], writes=[orec])
                    P.tt("vector", osb[:, qb, :], ob[:, 0:64], orec[:, qb, 0:1].to_broadcast([128, 64]), ALU.mult, reads=[ob, orec], writes=[osb])
                P.dma("sync", oap("omla", c0, TT)[:, h * 64:(h + 1) * 64].rearrange("(q p) v -> p q v", p=128), osb[:], reads=[osb], writes=[omla_d])

    if ctx is not None:
        return
    P.wait_tiles("sync", [orw_d, omla_d, os5_d])
    P.wait_all_dma("sync")
    P.emit()
    return nc, P


def consts():
    su = np.triu(np.ones((64, 64), np.float32), 1)
    uu = np.triu(np.ones((64, 64), np.float32), 0)
    z = np.zeros((64, 64), np.float32)
    bd = lambda m: np.block([[m, z], [z, m]])
    maskb = np.concatenate([bd(su), bd(su.T), bd(su), bd(uu), bd(uu)], axis=1)
    bdm = np.zeros((128, 2), np.float32); bdm[:64, 0] = 1; bdm[64:, 1] = 1
    blk = bd(np.ones((64, 64), np.float32))
    tri = np.triu(np.ones((128, 128), np.float32), 0)
    rmask = np.ones((128, SUB), np.float32); rmask[:, ::CH] = 0
    jm = np.zeros((128, 128), np.float32)
    for p in range(64):
        jm[64 + p, p] = -1.0
        jm[p, 64 + p] = 1.0
    gmask = np.zeros((128, 8), np.float32)
    for g in range(8):
        gmask[g * 16:(g + 1) * 16, g] = 1
    sign1 = np.ones((128, 1), np.float32); sign1[64:] = -1
    tidx = np.tile(np.arange(1, SUB + 1, dtype=np.float32)[None, :], (128, 1))
    pos = np.arange(8192, dtype=np.float32)
    inv_freq = (np.float32(10000.0) ** (-np.arange(0, 32, 2, dtype=np.float32) / np.float32(32))).astype(np.float32)
    ang = (pos[:, None] * inv_freq[None, :]).astype(np.float32)
    cos = np.cos(ang.astype(np.float64)).astype(np.float32).T
    sin = np.sin(ang.astype(np.float64)).astype(np.float32).T
    rope = np.stack([np.concatenate([cos, cos], 0), np.concatenate([-sin, sin], 0)], 0)
    return dict(ident=np.eye(128, dtype=np.float32), maskb=maskb, bdm=bdm, blkones=blk, tri=tri, rmask=rmask, jm=jm,
                gmask=gmask, sign1=sign1, tidx=tidx, rope=np.ascontiguousarray(rope))


def mixer_inputs(I, l, j, xT, C=None):
    C = C or consts()
    w_in = I['w_in'][l]
    hs = slice(j * 128, (j + 1) * 128)
    oM = 1792; oS = oM + 416
    z64 = np.zeros((1024, 64), np.float32); z32 = np.zeros((1024, 32), np.float32)
    kr = w_in[:, oM + 384:oM + 416]
    kr_sw = np.concatenate([kr[:, 16:32], kr[:, 0:16]], 1)
    cols = [w_in[:, 0:512][:, hs], w_in[:, 512:1024][:, hs], w_in[:, 1024:1536][:, hs],
            w_in[:, 1536:1664], w_in[:, 1664:1792],
            w_in[:, oM:oM + 256], w_in[:, oM + 256:oM + 384],
            np.concatenate([z64, kr, z32], 1),
            w_in[:, oS:oS + 512][:, hs],
            np.concatenate([z64, kr_sw, z32], 1)]
    wmix = np.concatenate(cols, 1)
    assert wmix.shape == (1024, 1408), wmix.shape
    mu = I['rwkv_mu'][l]
    vecs = np.zeros((128, 16), np.float32)
    vecs[:, 0] = mu[0:512][hs]; vecs[:, 1] = mu[512:1024][hs]; vecs[:, 2] = mu[1024:1536][hs]
    vecs[:, 3] = mu[1536:1664]; vecs[:, 4] = mu[1664:1792]
    vecs[:, 5] = I['rwkv_w0'][l][hs]; vecs[:, 6] = I['rwkv_a0'][l][hs]
    vecs[:, 7] = I['rwkv_k_k'][l][hs]; vecs[:, 8] = I['rwkv_k_a'][l][hs]
    vecs[:, 9] = I['rwkv_r_k'][l].reshape(512)[hs]
    vecs[:, 10] = I['mla_q_norm'][l][0:128]; vecs[:, 11] = I['mla_q_norm'][l][128:256]
    vecs[:, 12] = I['mla_kv_norm'][l]
    vecs[:, 13] = I['s5_d'][l][hs]
    w2a2 = np.concatenate([I['rwkv_w2'][l][:, hs], I['rwkv_a2'][l][:, hs]], 0)
    g2 = I['rwkv_g2'][l][:, hs]
    gng = np.repeat(I['rwkv_gn_g'][l][hs].reshape(2, 1, 64), 64, axis=1).reshape(128, 64)
    gnb = np.repeat(I['rwkv_gn_b'][l][hs].reshape(2, 1, 64), 64, axis=1).reshape(128, 64)
    qu = I['mla_q_up'][l].reshape(256, 8, 96)
    z = np.zeros((256, 64), np.float32)
    qparts = []
    for h in (2 * j, 2 * j + 1):
        main = qu[:, h, :]
        sw = np.concatenate([z, qu[:, h, 80:96], qu[:, h, 64:80]], 1)
        qparts += [main, sw]
    qup = np.concatenate(qparts, 1)
    kvu = I['mla_kv_up'][l].reshape(128, 8, 128)
    kvupk = np.concatenate([kvu[:, 2 * j, 0:64], kvu[:, 2 * j + 1, 0:64]], 1)
    kvupv = np.concatenate([kvu[:, 2 * j, 64:128], kvu[:, 2 * j + 1, 64:128]], 1)
    gs = slice(8 * j, 8 * j + 8)
    lre = I['s5_lambda_re'][l][gs]; lim = I['s5_lambda_im'][l][gs]; ls = I['s5_log_step'][l][gs]
    s5sp = np.concatenate([np.concatenate([lre.T, lre.T], 0), np.concatenate([lim.T, lim.T], 0), np.tile(ls[None, :], (128, 1))], 1)
    rep = lambda a: np.repeat(a, 16, axis=0)
    bre = I['s5_b_re'][l][gs].transpose(0, 2, 1).reshape(128, 64)
    bim = I['s5_b_im'][l][gs].transpose(0, 2, 1).reshape(128, 64)
    s5gc = np.concatenate([rep(lre), rep(lim), bre, bim, np.repeat(ls, 16)[:, None]], 1)
    cre = I['s5_c_re'][l][gs].reshape(128, 64).T
    cim = I['s5_c_im'][l][gs].reshape(128, 64).T
    cmain = np.concatenate([cre, cim], 0); cswap = np.concatenate([cim, cre], 0)
    d = dict(xT=xT, wmix=wmix, vecs=vecs, w2a2=w2a2, g2=g2, gng=gng, gnb=gnb, qup=qup, kvupk=kvupk, kvupv=kvupv,
             s5sp=s5sp, s5gc=s5gc, cmain=cmain, cswap=cswap)
    d.update(C)
    return {k: np.ascontiguousarray(v, dtype=np.float32) for k, v in d.items() if v is not None}


import math
import numpy as np

RT = 512
NROW = 2048
ALPHA = (2.0 * 2) ** 0.25
DFF = 2816


def build_row(ntile=NROW // RT, ctx=None):
    if ctx is None:
        nc = bass.Bass("TRN2", target_bir_lowering=False)
        P = Prog(nc)
        din = lambda n, s: P.dram(n, s, F32, kind="ExternalInput")
    else:
        P = ctx["P"]; din = ctx["din"]
    xT_d = din("xT", [1024, NROW])
    mo_d = [din("orwT", [512, NROW]), din("omlaT", [512, NROW]), din("os5T", [512, NROW])]
    wg_d = din("wgate", [1024, 3072])
    wo_d = [din("rwkv_out", [512, 1024]), din("mla_out", [512, 1024])]
    glu_d = din("s5_glu", [512, 2048])
    wout_d = din("w_out", [1024, 1024])
    w1_d = din("ffn_w1", [1024, DFF]); w3_d = din("ffn_w3", [1024, DFF]); w2_d = din("ffn_w2", [DFF, 1024])
    rv_d = din("rvecs", [128, 56])
    xo_d = P.dram("xout", [1024, NROW], F32, kind="ExternalOutput") if ctx is None else ctx["xout"]
    sb = P.sbuf
    rv = sb("rv", [128, 56]); P.dma("sync", rv[:], rv_d[:], writes=[rv])
    ones = sb("ones", [128, 128]); P.memset("vector", ones[:], 1.0, writes=[ones])
    xres = sb("xres", [128, 8, NROW])
    for k in range(8):
        P.dma("sync", xres[:, k, :], xT_d[k * 128:(k + 1) * 128, :], writes=[xres])
    xb = sb("xb", [128, 8, RT], BF16)
    mo = [sb(f"mo{i}", [128, 4, RT], BF16) for i in range(3)]
    merged = sb("merged", [128, 8, RT]); mb = sb("mb", [128, 8, RT], BF16)
    z = sb("z", [128, 8, RT])
    hb = sb("hb", [128, 22, RT], BF16)
    gt = sb("gt", [128, RT]); t1 = sb("t1", [128, RT]); t2 = sb("t2", [128, RT])
    mean = sb("mean", [128, RT]); rstd = sb("rstd", [128, RT])
    wbuf = [sb(f"wbuf{i}", [128, 4096], BF16) for i in range(3)]
    wctr = [0]
    ps = [P.psum(f"ps{i}", [128, 512]) for i in range(8)] if ctx is None else ctx["ps"]
    pctr = [0]

    def gen_ps():
        pctr[0] += 1
        return ps[pctr[0] % 8]

    def wload(d, r0, nk, c0, ncols):
        wb = wbuf[wctr[0] % 3]; wctr[0] += 1
        view = wb[:, 0:nk * ncols].rearrange("p (k c) -> p k c", k=nk)
        P.dma("gpsimd", view, d.ap[r0:r0 + nk * 128, c0:c0 + ncols].rearrange("(k p) c -> p k c", p=128), writes=[wb])
        return wb, view

    def layer_norm(gcol, bcol, cs):
        pS = gen_ps(); pQ = gen_ps()
        for m in range(8):
            P.mm(pS[:, 0:RT], ones[:, :], z[:, m, :], start=(m == 0), stop=(m == 7), reads=[ones, z], writes=[pS])
        for m in range(8):
            P.act(t1[:], z[:, m, :], AF.Square, reads=[z], writes=[t1])
            P.mm(pQ[:, 0:RT], ones[:, :], t1[:], start=(m == 0), stop=(m == 7), reads=[ones, t1], writes=[pQ])
        P.ts("vector", mean[:], pS[:, 0:RT], 1.0 / 1024, None, ALU.mult, reads=[pS], writes=[mean])
        P.tt("vector", t2[:], mean[:], mean[:], ALU.mult, reads=[mean], writes=[t2])
        P.stt("vector", rstd[:], pQ[:, 0:RT], 1.0 / 1024, t2[:], ALU.mult, ALU.subtract, reads=[pQ, t2], writes=[rstd])
        P.act(rstd[:], rstd[:], AF.Sqrt, bias=1e-5, reads=[rstd], writes=[rstd])
        P.op("vector", lambda e: e.reciprocal(rstd[:], rstd[:]), reads=[rstd], writes=[rstd])
        for m in range(8):
            P.tt("vector", t2[:], z[:, m, :], mean[:], ALU.subtract, reads=[z, mean], writes=[t2])
            P.tt("gpsimd", t2[:], t2[:], rstd[:], ALU.mult, reads=[t2, rstd], writes=[t2])
            P.ts("vector", xres[:, m, cs], t2[:], rv[:, gcol + m:gcol + m + 1], rv[:, bcol + m:bcol + m + 1], ALU.mult, ALU.add,
                 reads=[t2, rv], writes=[xres])

    for it in range(ntile):
        cs = slice(it * RT, (it + 1) * RT)
        P.copy("vector", xb[:], xres[:, :, cs], reads=[xres], writes=[xb])
        if ctx is None:
            for i in range(3):
                P.dma("gpsimd", mo[i][:], mo_d[i].ap[:, cs].rearrange("(k p) t -> p k t", p=128), writes=[mo[i]])
        else:
            ctx["moload"](mo, it, gen_ps)
        for br in range(3):
            for mc in range(2):
                wgt, wgv = wload(wg_d, 0, 8, br * 1024 + mc * 512, 512)
                if br < 2:
                    wyt, wyv = wload(wo_d[br], 0, 4, mc * 512, 512)
                else:
                    wyt, wyv = wload(glu_d, 0, 4, mc * 512, 512)
                    wy2t, wy2v = wload(glu_d, 0, 4, 1024 + mc * 512, 512)
                for mi in range(4):
                    m = mc * 4 + mi
                    pg = gen_ps()
                    for k in range(8):
                        P.mm(pg[:, 0:RT], wgv[:, k, mi * 128:(mi + 1) * 128], xb[:, k, :], start=(k == 0), stop=(k == 7), reads=[wgt, xb], writes=[pg])
                    P.act(gt[:], pg[:, 0:RT], AF.Sigmoid, bias=rv[:, br * 8 + m:br * 8 + m + 1], reads=[pg, rv], writes=[gt])
                    py = gen_ps()
                    for k in range(4):
                        P.mm(py[:, 0:RT], wyv[:, k, mi * 128:(mi + 1) * 128], mo[br][:, k, :], start=(k == 0), stop=(k == 3), reads=[wyt, mo[br]], writes=[py])
                    if br == 2:
                        py2 = gen_ps()
                        for k in range(4):
                            P.mm(py2[:, 0:RT], wy2v[:, k, mi * 128:(mi + 1) * 128], mo[br][:, k, :], start=(k == 0), stop=(k == 3), reads=[wy2t, mo[br]], writes=[py2])
                        P.act(t1[:], py2[:, 0:RT], AF.Sigmoid, reads=[py2], writes=[t1])
                        P.tt("vector", t1[:], py[:, 0:RT], t1[:], ALU.mult, reads=[py, t1], writes=[t1])
                        P.tt("vector", t1[:], t1[:], gt[:], ALU.mult, reads=[t1, gt], writes=[t1])
                        P.tt("gpsimd", merged[:, m, :], merged[:, m, :], t1[:], ALU.add, reads=[merged, t1], writes=[merged])
                    elif br == 0:
                        P.tt("vector", merged[:, m, :], py[:, 0:RT], gt[:], ALU.mult, reads=[py, gt], writes=[merged])
                    else:
                        P.tt("vector", t1[:], py[:, 0:RT], gt[:], ALU.mult, reads=[py, gt], writes=[t1])
                        P.tt("gpsimd", merged[:, m, :], merged[:, m, :], t1[:], ALU.add, reads=[merged, t1], writes=[merged])
        P.copy("vector", mb[:], merged[:], reads=[merged], writes=[mb])
        for mc in range(2):
            wt, wv = wload(wout_d, 0, 8, mc * 512, 512)
            for mi in range(4):
                m = mc * 4 + mi
                pz = gen_ps()
                for k in range(8):
                    P.mm(pz[:, 0:RT], wv[:, k, mi * 128:(mi + 1) * 128], mb[:, k, :], start=(k == 0), stop=(k == 7), reads=[wt, mb], writes=[pz])
                P.stt("vector", z[:, m, :], xres[:, m, cs], ALPHA, pz[:, 0:RT], ALU.mult, ALU.add, reads=[xres, pz], writes=[z])
        layer_norm(24, 32, cs)
        P.copy("vector", xb[:], xres[:, :, cs], reads=[xres], writes=[xb])
        for mc in range(6):
            ncols = 512 if mc < 5 else 256
            w1t, w1v = wload(w1_d, 0, 8, mc * 512, ncols)
            w3t, w3v = wload(w3_d, 0, 8, mc * 512, ncols)
            for mi in range(ncols // 128):
                m = mc * 4 + mi
                p1 = gen_ps(); p3 = gen_ps()
                for k in range(8):
                    P.mm(p1[:, 0:RT], w1v[:, k, mi * 128:(mi + 1) * 128], xb[:, k, :], start=(k == 0), stop=(k == 7), reads=[w1t, xb], writes=[p1])
                for k in range(8):
                    P.mm(p3[:, 0:RT], w3v[:, k, mi * 128:(mi + 1) * 128], xb[:, k, :], start=(k == 0), stop=(k == 7), reads=[w3t, xb], writes=[p3])
                P.act(t1[:], p1[:, 0:RT], AF.Silu, reads=[p1], writes=[t1])
                P.tt("vector", hb[:, m, :], p3[:, 0:RT], t1[:], ALU.mult, reads=[p3, t1], writes=[hb])
        for m in range(8):
            wt, wv = wload(w2_d, 0, 22, m * 128, 128)
            pz = gen_ps()
            for k in range(22):
                P.mm(pz[:, 0:RT], wv[:, k, :], hb[:, k, :], start=(k == 0), stop=(k == 21), reads=[wt, hb], writes=[pz])
            P.stt("vector", z[:, m, :], xres[:, m, cs], ALPHA, pz[:, 0:RT], ALU.mult, ALU.add, reads=[xres, pz], writes=[z])
        layer_norm(40, 48, cs)
    for k in range(8):
        P.dma("sync", xo_d[k * 128:(k + 1) * 128, :], xres[:, k, :], reads=[xres], writes=[xo_d])
    if ctx is not None:
        if ctx.get("xbf_out") is not None:
            for k in range(8):
                P.dma("gpsimd", ctx["xbf_out"][k * 128:(k + 1) * 128, :], xres[:, k, :], reads=[xres], writes=[ctx["xbf_out"]])
        return
    P.wait_tiles("sync", [xo_d])
    P.wait_all_dma("sync")
    P.emit()
    return nc, P


def row_inputs(I, l, xT_own, orwT, omlaT, os5T):
    rv = np.zeros((128, 56), np.float32)
    rv[:, 0:24] = I['gate_b'][l].reshape(24, 128).T
    rv[:, 24:32] = I['ln1_g'][l].reshape(8, 128).T
    rv[:, 32:40] = I['ln1_b'][l].reshape(8, 128).T
    rv[:, 40:48] = I['ln2_g'][l].reshape(8, 128).T
    rv[:, 48:56] = I['ln2_b'][l].reshape(8, 128).T
    d = dict(xT=xT_own, orwT=orwT, omlaT=omlaT, os5T=os5T, wgate=I['w_in'][l][:, 2720:5792],
             rwkv_out=I['rwkv_out'][l], mla_out=I['mla_out'][l], s5_glu=I['s5_glu'][l], w_out=I['w_out'][l],
             ffn_w1=I['ffn_w1'][l], ffn_w3=I['ffn_w3'][l], ffn_w2=I['ffn_w2'][l], rvecs=rv)
    return {k: np.ascontiguousarray(v, dtype=np.float32) for k, v in d.items() if v is not None}


import numpy as np

CONST_NAMES = ("ident", "maskb", "bdm", "blkones", "tri", "rmask", "jm", "gmask", "sign1", "tidx", "rope")
RG = [[0, 1, 2, 3], [4, 5, 6, 7]]


import os
FSKIP = os.environ.get('FSKIP', '')


def build_fused(nlayers=2):
    nc = bass.Bass("TRN2", target_bir_lowering=False)
    P = Prog(nc)
    P.use_arena()
    ext = {}
    qc = {}

    def getq(e):
        return e.partition_id() % 4

    def ext_in(name, shape):
        if name not in ext:
            ext[name] = P.dram(name, shape, F32, kind="ExternalInput")
        return ext[name]

    ps = [P.psum(f"ps{i}", [128, 512]) for i in range(8)]
    QW = 2048 * 128
    mixout = P.dram("mixout", [12 * 2048, 128], F32)
    gath = P.dram("gath", [12 * 4 * 2048, 128], F32)
    mine = P.dram("mine", [3 * 4 * 2048, 128], F32)
    xqbf = P.dram("xqbf", [1024, 2048], BF16)
    xg = P.dram("xg", [8 * 4 * 128, 2048], BF16)
    xres_d = P.dram("xres_d", [1024, 2048], F32)
    xout = P.dram("xout", [1024, 2048], F32, kind="ExternalOutput")

    for l in range(nlayers):
        last = (l == nlayers - 1)
        P.arena_reset()
        mo_t = T("mixout_t", mixout.ap)
        outs = (mo_t, mo_t, mo_t)

        def oap(kind, t0, n):
            q = t0 // 2048; lt = t0 % 2048
            br = {"orw": 0, "omla": 1, "os5": 2}[kind]
            blk = mixout.ap[(q * 3 + br) * 2048:(q * 3 + br + 1) * 2048, :]
            if br < 2:
                return blk[lt:lt + n, :]
            return blk.rearrange("(f a) c -> f (a c)", f=128)[:, lt:lt + n]

        def din_m(n, s, l=l):
            if n in CONST_NAMES or n == "xT":
                return ext_in(n, s)
            return ext_in(f"{n}_m{l}", s)

        def xload(xbf, c0):
            r = c0 // 2048; col = c0 % 2048
            P.dma("sync", xbf[:], xg.ap.rearrange("(k r p) t -> r p k t", k=8, r=4)[r][:, :, col:col + TT], reads=[xg], writes=[xbf])

        build_mixer(ctx=dict(P=P, din=din_m, ps=ps, outs=outs, oap=oap, xload=(None if l == 0 else xload)))
        if 'ag' not in FSKIP:
            for i in range(12):
                P.op("gpsimd", lambda e, i=i: e.collective_compute("AllGather", ALU.bypass, replica_groups=RG, ins=[mixout.ap[i * 2048:(i + 1) * 2048, :]],
                                                                    outs=[gath.ap[i * 8192:(i + 1) * 8192, :]]),
                     reads=[mo_t], writes=[gath], dma=True, amt=1)

        def cp_mine(e):
            q = getq(e)
            gv = gath.ap.rearrange("(q x) f -> q (x f)", q=4)
            return e.dma_start(out=mine.ap.rearrange("(o x) f -> o (x f)", o=1), in_=gv[bass.ds(q, 1), :])
        if 'cp' not in FSKIP:
            P.op("sync", cp_mine, reads=[gath], writes=[mine], dma=True)
        if 'row' in FSKIP:
            break
        P.barrier()
        P.arena_reset()

        def din_r(n, s, l=l):
            if n in ("orwT", "omlaT", "os5T"):
                return None
            if n == "xT":
                return ext_in("xTq", s) if l == 0 else xres_d
            return ext_in(f"{n}_r{l}", s)

        st = {}

        def moload(mo, it, gen_ps):
            if "tmt" not in st:
                st["tmt"] = [P.sbuf(f"tmt{i}", [128, 4, 128]) for i in range(2)]
                st["s5st"] = [P.sbuf(f"s5st{i}", [128, RT]) for i in range(2)]
                st["identf"] = P.sbuf("identf", [128, 128])
                st["n"] = 0
                P.dma("sync", st["identf"][:], ext_in("ident", [128, 128])[:], writes=[st["identf"]])
            identf = st["identf"]
            t0 = it * RT
            for br in range(2):
                for r in range(4):
                    tmt = st["tmt"][st["n"] % 2]; st["n"] += 1
                    P.dma("sync", tmt[:], mine.ap[(br * 4 + r) * 2048 + t0:(br * 4 + r) * 2048 + t0 + RT, :].rearrange("(b p) f -> p b f", p=128), reads=[mine], writes=[tmt])
                    pb = gen_ps()
                    for b4 in range(4):
                        P.mm(pb[:, b4 * 128:(b4 + 1) * 128], tmt[:, b4, :], identf[:], reads=[tmt, identf], writes=[pb])
                    P.copy("vector", mo[br][:, r, :], pb[:, 0:512], reads=[pb], writes=[mo[br]])
            for r in range(4):
                P.dma("gpsimd", mo[2][:, r, :], mine.ap[(8 + r) * 2048:(9 + r) * 2048, :].rearrange("(f a) c -> f (a c)", f=128)[:, t0:t0 + RT], reads=[mine], writes=[mo[2]])

        build_row(ctx=dict(P=P, din=din_r, ps=ps, xout=(xout if last else xres_d), xbf_out=(None if last else xqbf), moload=moload))
        if not last:
            for k in range(8):
                P.op("gpsimd", lambda e, k=k: e.collective_compute("AllGather", ALU.bypass, replica_groups=RG, ins=[xqbf.ap[k * 128:(k + 1) * 128, :]],
                                                                    outs=[xg.ap[k * 512:(k + 1) * 512, :]]),
                     reads=[xqbf], writes=[xg], dma=True, amt=1)
            P.barrier()
    P.wait_tiles("sync", [xout])
    P.wait_all_dma("sync")
    P.emit()
    return nc, P


def fused_inputs(I, b, j, C, nlayers=2):
    x = I['x']
    d = dict(C)
    d["xT"] = x[b].T
    d["xTq"] = x[b, j * 2048:(j + 1) * 2048].T
    for l in range(nlayers):
        mi = mixer_inputs(I, l, j, None, C)
        for k, v in mi.items():
            if k in C or k == "xT":
                continue
            d[f"{k}_m{l}"] = v
        ri = row_inputs(I, l, None, None, None, None)
        for k, v in ri.items():
            if k in ("xT", "orwT", "omlaT", "os5T"):
                continue
            d[f"{k}_r{l}"] = v
    return {k: np.ascontiguousarray(v, dtype=np.float32) for k, v in d.items()}


from concourse.bass_utils import run_bass_kernel_spmd


def kernel(**inputs):
    I = {k: np.asarray(v, dtype=np.float32) for k, v in inputs.items()}
    C = consts()
    nc, P = build_fused(2)
    in_maps = [fused_inputs(I, b, j, C, 2) for b in range(2) for j in range(4)]
    res = run_bass_kernel_spmd(nc, in_maps, core_ids=list(range(8))).results
    P.close()
    out = np.stack([np.concatenate([res[b * 4 + q]["xout"] for q in range(4)], 1).T for b in range(2)], 0)
    return np.ascontiguousarray(out.astype(np.float32))
```

```python
import contextlib
import numpy as np
import concourse.bass as bass
import concourse.mybir as mybir

F32 = mybir.dt.float32
BF16 = mybir.dt.bfloat16
ALU = mybir.AluOpType
AF = mybir.ActivationFunctionType
AX = mybir.AxisListType


class T:
    def __init__(self, name, ap):
        self.name = name
        self.ap = ap
        self.w = None
        self.r = {}

    def __getitem__(self, k):
        return self.ap[k]


class Prog:
    NDMA = 48

    def __init__(self, nc):
        self.nc = nc
        self.st = contextlib.ExitStack()
        self.engs = ["tensor", "vector", "scalar", "gpsimd", "sync"]
        self.ops = {e: [] for e in self.engs}
        self.cnt = {e: 0 for e in self.engs}
        self.known = {e: {} for e in self.engs}
        self.esem = {e: self.st.enter_context(nc.semaphore("es_" + e)) for e in self.engs}
        self.dsem = [self.st.enter_context(nc.semaphore(f"ds{i}")) for i in range(self.NDMA)]
        self.dcnt = [0] * self.NDMA
        self.csem = self.st.enter_context(nc.semaphore("cc_sem"))
        self.ccnt = 0
        self.dma_i = 0
        self.nuniq = 0

    ARENA_WORDS = 52800

    def use_arena(self):
        self.arena = self.st.enter_context(self.nc.sbuf_tensor("arena", [128, self.ARENA_WORDS], F32))
        self.aoff = 0

    def arena_reset(self):
        self.aoff = 0

    def sbuf(self, name, shape, dt=F32):
        if getattr(self, "arena", None) is None:
            t = self.st.enter_context(self.nc.sbuf_tensor(name, list(shape), dt))
            return T(name, t)
        shape = list(shape)
        nelem = 1
        for d in shape[1:]:
            nelem *= d
        four = dt in (F32, mybir.dt.int32)
        words = nelem if four else (nelem + 1) // 2
        assert self.aoff + words <= self.ARENA_WORDS, ("arena overflow", name, self.aoff, words)
        ap = self.arena[0:shape[0], self.aoff:self.aoff + words]
        self.aoff += words
        if dt != F32:
            ap = ap.bitcast(dt)
            if not four:
                ap = ap[:, 0:nelem]
        if len(shape) == 3:
            ap = ap.rearrange("p (a b) -> p a b", a=shape[1])
        elif len(shape) == 4:
            ap = ap.rearrange("p (a b c) -> p a b c", a=shape[1], b=shape[2])
        return T(name, ap)

    def barrier(self):
        for eng in self.engs:
            waits = []
            kn = self.known[eng]
            for e2 in self.engs:
                if e2 == eng or self.cnt[e2] == 0:
                    continue
                k = ("e", e2)
                if kn.get(k, 0) < self.cnt[e2]:
                    kn[k] = self.cnt[e2]
                    waits.append((k, self.cnt[e2]))
            for slot in range(self.NDMA):
                k = ("d", slot)
                if self.dcnt[slot] > 0 and kn.get(k, 0) < self.dcnt[slot]:
                    kn[k] = self.dcnt[slot]
                    waits.append((k, self.dcnt[slot]))
            if self.ccnt > 0 and kn.get(("c", 0), 0) < self.ccnt:
                kn[("c", 0)] = self.ccnt
                waits.append((("c", 0), self.ccnt))
            self.ops[eng].append((None, waits, None))

    def psum(self, name, shape, dt=F32):
        t = self.st.enter_context(self.nc.psum_tensor(name, list(shape), dt))
        return T(name, t)

    def dram(self, name, shape, dt, kind="Internal"):
        t = self.nc.dram_tensor(name, list(shape), dt, kind=kind)
        return T(name, t.ap())

    def sub(self, t, name, key):
        return T(name, t.ap[key])

    def _tokkey(self, tok):
        return (tok[0], tok[1])

    def op(self, eng, fn, reads=(), writes=(), inc=True, dma=False, touch=(), amt=16):
        deps = {}

        def add(tok):
            if tok is None:
                return
            k = self._tokkey(tok)
            if deps.get(k, 0) < tok[2]:
                deps[k] = tok[2]

        for t in reads:
            add(t.w)
        for t in list(writes) + list(touch):
            add(t.w)
            for k, v in t.r.items():
                add((k[0], k[1], v))
        if dma and amt == 1:
            self.ccnt += 1
            tok = ("c", 0, self.ccnt, 1)
        elif dma:
            slot = self.dma_i % self.NDMA
            self.dma_i += 1
            if self.dcnt[slot] > 0:
                add(("d", slot, self.dcnt[slot]))
            self.dcnt[slot] += amt
            tok = ("d", slot, self.dcnt[slot], amt)
        else:
            tok = ("e", eng, self.cnt[eng] + 1)
            if inc:
                self.cnt[eng] += 1
        waits = []
        kn = self.known[eng]
        for k, v in deps.items():
            if k[0] == "e" and k[1] == eng:
                if eng == "tensor":
                    continue
                assert v <= self.cnt[eng] or (v == tok[2] and False), (eng, v, self.cnt[eng])
            if kn.get(k, 0) >= v:
                continue
            kn[k] = v
            waits.append((k, v))
        self.ops[eng].append((fn, waits, tok if (inc or dma) else None))
        for t in reads:
            k = self._tokkey(tok)
            if t.r.get(k, 0) < tok[2]:
                t.r[k] = tok[2]
        for t in writes:
            t.w = tok
            t.r = {}
        return tok

    def wait_tiles(self, eng, tiles):
        self.op(eng, None, reads=tiles, inc=False)

    def wait_all_dma(self, eng="sync"):
        waits = []
        for slot in range(self.NDMA):
            if self.dcnt[slot] > 0 and self.known[eng].get(("d", slot), 0) < self.dcnt[slot]:
                waits.append((("d", slot), self.dcnt[slot]))
                self.known[eng][("d", slot)] = self.dcnt[slot]
        if self.ccnt > 0 and self.known[eng].get(("c", 0), 0) < self.ccnt:
            self.known[eng][("c", 0)] = self.ccnt
            waits.append((("c", 0), self.ccnt))
        self.ops[eng].append((None, waits, None))

    def _sem(self, k):
        if k[0] == "c":
            return self.csem
        return self.esem[k[1]] if k[0] == "e" else self.dsem[k[1]]

    def emit(self):
        nc = self.nc
        with nc.Block() as block:
            def mk(name):
                def body(e):
                    for fn, waits, tok in self.ops[name]:
                        for k, v in waits:
                            e.wait_ge(self._sem(k), v)
                        if fn is None:
                            continue
                        ins = fn(e)
                        if tok is not None:
                            if tok[0] == "c":
                                ins.then_inc(self.csem, 1)
                            elif tok[0] == "d":
                                ins.then_inc(self.dsem[tok[1]], tok[3])
                            else:
                                ins.then_inc(self.esem[name], 1)
                return body
            block.tensor(mk("tensor"))
            block.vector(mk("vector"))
            block.scalar(mk("scalar"))
            block.gpsimd(mk("gpsimd"))
            block.sync(mk("sync"))

    def close(self):
        self.st.close()

    def dma(self, eng, out_ap, in_ap, reads=(), writes=()):
        return self.op(eng, lambda e: e.dma_start(out=out_ap, in_=in_ap), reads=reads, writes=writes, dma=True)

    def mm(self, out_ap, lhsT, rhs, start=True, stop=True, reads=(), writes=(), **kw):
        return self.op("tensor", lambda e: e.matmul(out_ap, lhsT, rhs, start=start, stop=stop, **kw),
                       reads=reads, writes=writes if stop else (), touch=() if stop else writes, inc=True)

    def tr(self, out_ap, in_ap, ident_ap, reads=(), writes=(), inc=True):
        return self.op("tensor", lambda e: e.matmul(out_ap, in_ap, ident_ap, start=True, stop=True), reads=reads, writes=writes if inc else (), inc=inc)

    def act(self, out_ap, in_ap, func, reads=(), writes=(), eng="scalar", **kw):
        return self.op(eng, lambda e: e.activation(out_ap, in_ap, func, **kw), reads=reads, writes=writes)

    def tt(self, eng, out_ap, a, b, op, reads=(), writes=()):
        return self.op(eng, lambda e: e.tensor_tensor(out_ap, a, b, op), reads=reads, writes=writes)

    def ts(self, eng, out_ap, a, s1, s2, op0, op1=None, reads=(), writes=()):
        if op1 is None:
            return self.op(eng, lambda e: e.tensor_scalar(out_ap, a, s1, None, op0), reads=reads, writes=writes)
        return self.op(eng, lambda e: e.tensor_scalar(out_ap, a, s1, s2, op0, op1), reads=reads, writes=writes)

    def stt(self, eng, out_ap, in0, scalar, in1, op0, op1, reads=(), writes=()):
        return self.op(eng, lambda e: e.scalar_tensor_tensor(out_ap, in0, scalar, in1, op0, op1), reads=reads, writes=writes)

    def copy(self, eng, out_ap, in_ap, reads=(), writes=()):
        if eng == "scalar":
            return self.op(eng, lambda e: e.copy(out_ap, in_ap), reads=reads, writes=writes)
        return self.op(eng, lambda e: e.tensor_copy(out_ap, in_ap), reads=reads, writes=writes)

    def memset(self, eng, ap, val, writes=()):
        return self.op(eng, lambda e: e.memset(ap, val), writes=writes)


import math, os
RSTOP = int(os.environ.get('RSTOP', '9'))
CSTOP = int(os.environ.get('CSTOP', '99'))
G2V = os.environ.get('G2V', '')
import numpy as np

TT = 256
NTILE = 32
QB = TT // 128
SUB = 256
CH = 64
ATTN_SCALE = 1.0 / math.sqrt(96.0)
NEG_EXPM05 = -math.exp(-0.5)


def build_mixer(ntile=NTILE, do_rwkv=True, do_mla=True, do_s5=True, ctx=None):
    if ctx is None:
        nc = bass.Bass("TRN2", target_bir_lowering=False)
        P = Prog(nc)
        din = lambda n, s: P.dram(n, s, F32, kind="ExternalInput")
    else:
        P = ctx["P"]; din = ctx["din"]
    xT_d = din("xT", [1024, 8192])
    wmix_d = din("wmix", [1024, 1408])
    vecs_d = din("vecs", [128, 16])
    w2a2_d = din("w2a2", [128, 128])
    g2_d = din("g2", [128, 128])
    gng_d = din("gng", [128, 64])
    gnb_d = din("gnb", [128, 64])
    qup_d = din("qup", [256, 4 * 96])
    kvupk_d = din("kvupk", [128, 128])
    kvupv_d = din("kvupv", [128, 128])
    rope_d = din("rope", [2, 32, 8192])
    ident_d = din("ident", [128, 128])
    maskb_d = din("maskb", [128, 640])
    bdm_d = din("bdm", [128, 2])
    blk_d = din("blkones", [128, 128])
    tri_d = din("tri", [128, 128])
    rmask_d = din("rmask", [128, SUB])
    jm_d = din("jm", [128, 128])
    s5sp_d = din("s5sp", [128, 24])
    s5gc_d = din("s5gc", [128, 4 * 64 + 1])
    gmask_d = din("gmask", [128, 8])
    cmain_d = din("cmain", [128, 128])
    cswap_d = din("cswap", [128, 128])
    sign1_d = din("sign1", [128, 1])
    tidx_d = din("tidx", [128, SUB])
    if ctx is None:
        orw_d = P.dram("orw", [8192, 128], F32, kind="ExternalOutput")
        omla_d = P.dram("omla", [8192, 128], F32, kind="ExternalOutput")
        os5_d = P.dram("os5", [128, 8192], F32, kind="ExternalOutput")
    else:
        orw_d, omla_d, os5_d = ctx["outs"]
    if ctx is not None and ctx.get("oap") is not None:
        oap = ctx["oap"]
    else:
        def oap(kind, t0, n):
            if kind == "orw":
                return orw_d.ap[t0:t0 + n, :]
            if kind == "omla":
                return omla_d.ap[t0:t0 + n, :]
            return os5_d.ap[:, t0:t0 + n]

    sb = P.sbuf
    def load(name, d, shape, dt=F32, eng="sync"):
        t = sb(name, shape, dt)
        P.dma("gpsimd" if dt != F32 else eng, t[:], d[:], writes=[t])
        return t
    ident = load("ident_s", ident_d, [128, 128])
    maskb = load("maskb_s", maskb_d, [128, 640])
    bdm = load("bdm_s", bdm_d, [128, 2])
    blk = load("blk_s", blk_d, [128, 128])
    tri = load("tri_s", tri_d, [128, 128], BF16)
    rmask = load("rmask_s", rmask_d, [128, SUB])
    jm = load("jm_s", jm_d, [128, 128])
    vecs = load("vecs_s", vecs_d, [128, 16])
    w2a2 = load("w2a2_s", w2a2_d, [128, 128])
    g2 = load("g2_s", g2_d, [128, 128])
    gng = load("gng_s", gng_d, [128, 64])
    gnb = load("gnb_s", gnb_d, [128, 64])
    kvupk = load("kvupk_s", kvupk_d, [128, 128], BF16)
    kvupv = load("kvupv_s", kvupv_d, [128, 128], BF16)
    qup = sb("qup_s", [128, 2, 384], BF16)
    P.dma("gpsimd", qup[:], qup_d.ap.rearrange("(k p) m -> p k m", p=128), writes=[qup])
    wmix = sb("wmix_s", [128, 8, 1408], BF16)
    for k in range(8):
        P.dma("gpsimd", wmix[:, k, :], wmix_d[k * 128:(k + 1) * 128, :], writes=[wmix])
    ones = sb("ones_s", [128, 128])
    P.memset("vector", ones[:], 1.0, writes=[ones])

    ps = [P.psum(f"ps{i}", [128, 512]) for i in range(8)] if ctx is None else ctx["ps"]
    gctr = [0]

    def gen_ps():
        gctr[0] += 1
        return ps[5 + gctr[0] % 3]

    V = lambda c: vecs[:, c:c + 1]

    xbf = sb("xbf", [128, 8, TT], BF16)
    praw = [sb(f"praw{m}", [128, TT + 1]) for m in range(5)]
    for m in range(5):
        P.memset("vector", praw[m][:, 0:1], 0.0, writes=[praw[m]])
    dtmp = sb("dtmp", [128, TT])
    cct = sb("cct", [128, TT])
    sst = sb("sst", [128, TT])

    if do_s5:
        s5sp = load("s5sp_s", s5sp_d, [128, 24])
        s5gc = load("s5gc_s", s5gc_d, [128, 257])
        gmask = load("gmask_s", gmask_d, [128, 8])
        cmain = load("cmain_s", cmain_d, [128, 128])
        cswap = load("cswap_s", cswap_d, [128, 128])
        sign1 = load("sign1_s", sign1_d, [128, 1])
        tidx = load("tidx_s", tidx_d, [128, SUB])
        TWO_PI = 2.0 * math.pi

        scs = {}

        def sincos(name, ang, shape, want_cos, out=None):
            key = tuple(shape)
            if key not in scs:
                scs[key] = (sb(f"scf{len(scs)}", shape), sb(f"sci{len(scs)}", shape, mybir.dt.int32), sb(f"scg{len(scs)}", shape))
            f, fi, g = scs[key]
            o = sb(name + "_o", shape) if out is None else out
            P.ts("vector", f[:], ang[:], 1.0 / TWO_PI, 0.25 if want_cos else 0.0, ALU.mult, ALU.add, reads=[ang], writes=[f])
            P.copy("vector", fi[:], f[:], reads=[f], writes=[fi])
            P.copy("vector", g[:], fi[:], reads=[fi], writes=[g])
            P.tt("vector", f[:], f[:], g[:], ALU.subtract, reads=[f, g], writes=[f])
            P.ts("vector", g[:], f[:], 0.5, None, ALU.is_ge, reads=[f], writes=[g])
            P.tt("vector", f[:], f[:], g[:], ALU.subtract, reads=[f, g], writes=[f])
            P.ts("vector", g[:], f[:], -0.5, None, ALU.is_lt, reads=[f], writes=[g])
            P.tt("vector", f[:], f[:], g[:], ALU.add, reads=[f, g], writes=[f])
            oap = o[:] if out is None else out_ap[0]
            P.act(oap, f[:], AF.Sin, scale=TWO_PI, reads=[f], writes=[o])
            return o

        step_sp = sb("step_sp", [128, 8])
        P.act(step_sp[:], s5sp[:, 16:24], AF.Exp, reads=[s5sp], writes=[step_sp])
        lre_sp = sb("lre_sp", [128, 8])
        P.ts("vector", lre_sp[:], s5sp[:, 0:8], -1e-4, None, ALU.min, reads=[s5sp], writes=[lre_sp])
        P.tt("vector", lre_sp[:], lre_sp[:], step_sp[:], ALU.mult, reads=[lre_sp, step_sp], writes=[lre_sp])
        mag_sp = sb("mag_sp", [128, 8])
        P.act(mag_sp[:], lre_sp[:], AF.Exp, reads=[lre_sp], writes=[mag_sp])
        th_sp = sb("th_sp", [128, 8])
        P.tt("vector", th_sp[:], s5sp[:, 8:16], step_sp[:], ALU.mult, reads=[s5sp, step_sp], writes=[th_sp])
        thf = sb("thf", [128, 8]); thi = sb("thi", [128, 8], mybir.dt.int32); thg = sb("thg", [128, 8])
        P.ts("vector", thf[:], th_sp[:], 1.0 / TWO_PI, None, ALU.mult, reads=[th_sp], writes=[thf])
        P.copy("vector", thi[:], thf[:], reads=[thf], writes=[thi])
        P.copy("vector", thg[:], thi[:], reads=[thi], writes=[thg])
        P.tt("vector", thf[:], thf[:], thg[:], ALU.subtract, reads=[thf, thg], writes=[thf])
        ctab = sb("ctab", [128, 8, SUB]); stab = sb("stab", [128, 8, SUB])
        angt = sb("angt", [128, SUB])
        for g in range(8):
            P.ts("vector", angt[:], tidx[:], thf[:, g:g + 1], TWO_PI, ALU.mult, ALU.mult, reads=[tidx, thf], writes=[angt])
            P.ts("vector", angt[:], angt[:], TWO_PI * (SUB + 1), None, ALU.add, reads=[angt], writes=[angt])
            out_ap = [stab[:, g, :]]
            sincos(f"sc_s{g}", angt, [128, SUB], False, out=stab)
            out_ap = [ctab[:, g, :]]
            sincos(f"sc_c{g}", angt, [128, SUB], True, out=ctab)
        step_gc = sb("step_gc", [128, 1])
        P.act(step_gc[:], s5gc[:, 256:257], AF.Exp, reads=[s5gc], writes=[step_gc])
        lre = sb("lre_gc", [128, 64])
        P.ts("vector", lre[:], s5gc[:, 0:64], -1e-4, None, ALU.min, reads=[s5gc], writes=[lre])
        lim = s5gc
        magg = sb("mag_gc", [128, 64])
        P.act(magg[:], lre[:], AF.Exp, scale=step_gc[:, 0:1], reads=[lre, step_gc], writes=[magg])
        angg = sb("ang_gc", [128, 64])
        P.ts("vector", angg[:], s5gc[:, 64:128], step_gc[:, 0:1], None, ALU.mult, reads=[s5gc, step_gc], writes=[angg])
        sing = sincos("sg", angg, [128, 64], False)
        cosg = sincos("cg", angg, [128, 64], True)
        lbre = sb("lbre", [128, 64]); lbim = sb("lbim", [128, 64])
        P.tt("vector", lbre[:], magg[:], cosg[:], ALU.mult, reads=[magg, cosg], writes=[lbre])
        P.tt("vector", lbim[:], magg[:], sing[:], ALU.mult, reads=[magg, sing], writes=[lbim])
        den = sb("den", [128, 64]); t1 = sb("s5t1", [128, 64]); t2 = sb("s5t2", [128, 64])
        P.tt("vector", den[:], lre[:], lre[:], ALU.mult, reads=[lre], writes=[den])
        P.tt("vector", t1[:], s5gc[:, 64:128], s5gc[:, 64:128], ALU.mult, reads=[s5gc], writes=[t1])
        P.tt("vector", den[:], den[:], t1[:], ALU.add, reads=[den, t1], writes=[den])
        P.op("vector", lambda e: e.reciprocal(den[:], den[:]), reads=[den], writes=[den])
        nre = sb("nre", [128, 64])
        P.ts("vector", nre[:], lbre[:], -1.0, None, ALU.add, reads=[lbre], writes=[nre])
        fre = sb("fre", [128, 64]); fim = sb("fim", [128, 64])
        P.tt("vector", t1[:], nre[:], lre[:], ALU.mult, reads=[nre, lre], writes=[t1])
        P.tt("vector", t2[:], lbim[:], s5gc[:, 64:128], ALU.mult, reads=[lbim, s5gc], writes=[t2])
        P.tt("vector", t1[:], t1[:], t2[:], ALU.add, reads=[t1, t2], writes=[t1])
        P.tt("vector", fre[:], t1[:], den[:], ALU.mult, reads=[t1, den], writes=[fre])
        P.tt("vector", t1[:], lbim[:], lre[:], ALU.mult, reads=[lbim, lre], writes=[t1])
        P.tt("vector", t2[:], nre[:], s5gc[:, 64:128], ALU.mult, reads=[nre, s5gc], writes=[t2])
        P.tt("vector", t1[:], t1[:], t2[:], ALU.subtract, reads=[t1, t2], writes=[t1])
        P.tt("vector", fim[:], t1[:], den[:], ALU.mult, reads=[t1, den], writes=[fim])
        bbre = sb("bbre", [128, 64]); bbim = sb("bbim", [128, 64])
        bre = s5gc[:, 128:192]; bim = s5gc[:, 192:256]
        P.tt("vector", t1[:], fre[:], bre, ALU.mult, reads=[fre, s5gc], writes=[t1])
        P.tt("vector", t2[:], fim[:], bim, ALU.mult, reads=[fim, s5gc], writes=[t2])
        P.tt("vector", bbre[:], t1[:], t2[:], ALU.subtract, reads=[t1, t2], writes=[bbre])
        P.tt("vector", t1[:], fre[:], bim, ALU.mult, reads=[fre, s5gc], writes=[t1])
        P.tt("vector", t2[:], fim[:], bre, ALU.mult, reads=[fim, s5gc], writes=[t2])
        P.tt("vector", bbim[:], t1[:], t2[:], ALU.add, reads=[t1, t2], writes=[bbim])
        W1 = sb("s5W1", [128, 8, 128]); W2 = sb("s5W2", [128, 8, 128])
        ngm = sb("ngmask", [128, 8])
        P.ts("vector", ngm[:], gmask[:], -1.0, None, ALU.mult, reads=[gmask], writes=[ngm])
        for g in range(8):
            P.ts("vector", W1[:, g, 0:64], bbre[:], gmask[:, g:g + 1], None, ALU.mult, reads=[bbre, gmask], writes=[W1])
            P.ts("vector", W1[:, g, 64:128], bbim[:], gmask[:, g:g + 1], None, ALU.mult, reads=[bbim, gmask], writes=[W1])
            P.ts("vector", W2[:, g, 0:64], bbim[:], gmask[:, g:g + 1], None, ALU.mult, reads=[bbim, gmask], writes=[W2])
            P.ts("vector", W2[:, g, 64:128], bbre[:], ngm[:, g:g + 1], None, ALU.mult, reads=[bbre, ngm], writes=[W2])
        wc1 = sb("wc1", [128, 128]); wc2 = sb("wc2", [128, 128])
        P.ts("vector", wc1[:], cmain[:], sign1[:, 0:1], None, ALU.mult, reads=[cmain, sign1], writes=[wc1])
        P.ts("vector", wc2[:], cswap[:], -1.0, None, ALU.mult, reads=[cswap], writes=[wc2])
        Wc1g = sb("Wc1g", [128, 8, 128]); Wc2g = sb("Wc2g", [128, 8, 128])
        P.memset("vector", Wc1g[:], 0.0, writes=[Wc1g]); P.memset("vector", Wc2g[:], 0.0, writes=[Wc2g])
        for g in range(8):
            P.copy("vector", Wc1g[:, g, g * 16:(g + 1) * 16], wc1[:, g * 16:(g + 1) * 16], reads=[wc1], writes=[Wc1g])
            P.copy("vector", Wc2g[:, g, g * 16:(g + 1) * 16], wc2[:, g * 16:(g + 1) * 16], reads=[wc2], writes=[Wc2g])
        h0 = sb("s5h0", [128, 8])
        P.memset("vector", h0[:], 0.0, writes=[h0])
        ubuf = sb("s5u", [128, TT])
        s5x = [sb(f"s5x{i}", [128, SUB]) for i in range(4)]
        s5y = sb("s5y", [128, SUB]); s5y2 = sb("s5y2", [128, SUB]); s5o = sb("s5o", [128, SUB])

    if do_mla:
        Kt = [[P.sbuf(f"Kt{h}_{i}", [96, TT], BF16) for i in range(ntile)] for h in range(2)]
        Vaug = [P.sbuf(f"Va{i}", [128, 2, 65], BF16) for i in range(ntile * QB)]
        for va in Vaug:
            P.memset("gpsimd", va[:, :, 64:65], 1.0, writes=[va])
        Qt = [sb(f"Qt{h}", [96, TT], BF16) for h in range(2)]
        cq = sb("cq", [128, 2, TT]); sq = sb("sq", [128, 2, TT]); rstd = sb("rstd", [128, TT])
        cqn = sb("cqn", [128, 2, TT], BF16)
        ckv = sb("ckv", [128, TT]); sqk = sb("sqk", [128, TT]); rstdk = sb("rstdk", [128, TT]); ckvn = sb("ckvn", [128, TT], BF16)
        kr1 = dtmp; kr2 = sb("kr2", [128, TT])
        qr1 = dtmp; qr2 = kr2
        Pt = [sb(f"Pt{i}", [128, TT], BF16) for i in range(3)]
        osb = sb("osb", [128, QB, 64]); orec = sb("orec", [128, QB, 1])
        pctr = [0]

    if do_rwkv:
        f2 = lambda n: sb(n, [128, SUB])
        LW = f2("LW"); AS = f2("AS"); GF = f2("GF"); KK = f2("KK"); D2 = f2("D2"); D3 = f2("D3"); KM = f2("KM"); BV = f2("BV")
        CU = f2("CU"); E1 = f2("E1"); E2 = f2("E2"); RT = f2("RT"); KT_ = f2("KT_"); AT = f2("AT"); BT = f2("BT")
        KHF = f2("KHF"); BHF = f2("BHF"); RK = f2("RK")
        gC = sb("gC", [128, 4])
        BD = [[sb(f"BD{q}_{r}", [128, 2, 64]) for r in range(1)] for q in range(9)]
        AM = [sb(f"AM{r}", [128, 640]) for r in range(1)]
        TM = [sb(f"TM{r}", [128, 640]) for r in range(1)]
        XS = [sb(f"XS{r}", [128, 256]) for r in range(3)]
        PS_ = [sb(f"PS{r}", [128, 256]) for r in range(3)]
        RH = [sb(f"RH{r}", [128, 128]) for r in range(1)]
        GT = [sb(f"GT{r}", [128, 128]) for r in range(1)]
        YH = [sb(f"YH{r}", [128, 256]) for r in range(5)]
        P.memset("vector", YH[4][:], 0.0, writes=[YH[4]])
        Vc = sb("Vc", [128, 4, 64]); Gc = sb("Gc", [128, 4, 64]); Yc = sb("Yc", [128, 4, 64]); Ysq = sb("Ysq", [128, 4, 64])
        coef = sb("coef", [128, 4])
        st1 = sb("st1", [128, 4]); st2 = sb("st2", [128, 4]); st3 = sb("st3", [128, 4])
        yo = sb("yo", [128, 4, 64])
        hstate = [4]

    for it in range(ntile):
        c0 = it * TT
        if ctx is None or ctx.get("xload") is None:
            P.dma("gpsimd", xbf[:], xT_d.ap.rearrange("(k p) t -> p k t", p=128)[:, :, c0:c0 + TT], writes=[xbf])
        else:
            ctx["xload"](xbf, c0)
        if do_mla:
            P.dma("sync", cct[64:96, :], rope_d[0, :, c0:c0 + TT], writes=[cct])
            P.dma("sync", sst[64:96, :], rope_d[1, :, c0:c0 + TT], writes=[sst])
        for m in range(11):
            if m < 5 and not do_rwkv:
                continue
            if m in (5, 6, 7, 8, 10) and not do_mla:
                continue
            if m == 9 and not do_s5:
                continue
            pb = gen_ps()
            for k in range(8):
                P.mm(pb[:, 0:TT], wmix[:, k, m * 128:(m + 1) * 128], xbf[:, k, :], start=(k == 0), stop=(k == 7), reads=[wmix, xbf], writes=[pb])
            if m < 5:
                pr = praw[m]
                P.act(pr[:, 1:TT + 1], pb[:, 0:TT], AF.Copy, reads=[pb], writes=[pr])
                P.tt("vector", dtmp[:], pr[:, 0:TT], pr[:, 1:TT + 1], ALU.subtract, reads=[pr], writes=[dtmp])
                P.copy("vector", pr[:, 0:1], pr[:, TT:TT + 1], reads=[pr], writes=[pr])
                P.stt("vector", pr[:, 1:TT + 1], dtmp[:], V(m), pr[:, 1:TT + 1], ALU.mult, ALU.add, reads=[dtmp, vecs, pr], writes=[pr])
            elif m in (5, 6):
                j = m - 5
                P.act(cq[:, j, :], pb[:, 0:TT], AF.Copy, reads=[pb], writes=[cq])
                P.act(sq[:, j, :], pb[:, 0:TT], AF.Square, reads=[pb], writes=[sq])
            elif m == 7:
                P.act(ckv[:], pb[:, 0:TT], AF.Copy, reads=[pb], writes=[ckv])
                P.act(sqk[:], pb[:, 0:TT], AF.Square, reads=[pb], writes=[sqk])
            elif m == 8:
                P.tt("vector", kr1[64:96, :], pb[64:96, 0:TT], cct[64:96, :], ALU.mult, reads=[pb, cct], writes=[kr1])
            elif m == 10:
                P.tt("vector", kr2[64:96, :], pb[64:96, 0:TT], sst[64:96, :], ALU.mult, reads=[pb, sst], writes=[kr2])
                P.tt("vector", Kt[0][it][64:96, :], kr1[64:96, :], kr2[64:96, :], ALU.add, reads=[kr1, kr2], writes=[Kt[0][it]])
                P.tt("vector", Kt[1][it][64:96, :], kr1[64:96, :], kr2[64:96, :], ALU.add, reads=[kr1, kr2], writes=[Kt[1][it]])
            elif m == 9:
                P.act(ubuf[:], pb[:, 0:TT], AF.Copy, reads=[pb], writes=[ubuf])

        def _g_rwkv():
            yield
            if do_rwkv:
                for s in range(TT // SUB):
                    o = 1 + s * SUB
                    r_ = praw[0][:, o:o + SUB]; k_ = praw[1][:, o:o + SUB]; v_ = praw[2][:, o:o + SUB]
                    dwa = praw[3]; dg = praw[4]
                    P.act(dwa[0:64, o:o + SUB], dwa[0:64, o:o + SUB], AF.Tanh, reads=[dwa], writes=[dwa])
                    pb = gen_ps()
                    P.mm(pb[:, 0:SUB], w2a2[0:64, :], dwa[0:64, o:o + SUB], reads=[w2a2, dwa], writes=[pb])
                    P.act(LW[:], pb[:, 0:SUB], AF.Sigmoid, bias=V(5), reads=[pb, vecs], writes=[LW])
                    P.ts("vector", LW[:], LW[:], NEG_EXPM05, None, ALU.mult, reads=[LW], writes=[LW])
                    pb = gen_ps()
                    P.mm(pb[:, 0:SUB], w2a2[64:128, :], dwa[64:128, o:o + SUB], reads=[w2a2, dwa], writes=[pb])
                    P.act(AS[:], pb[:, 0:SUB], AF.Sigmoid, bias=V(6), reads=[pb, vecs], writes=[AS])
                    P.act(dg[:, o:o + SUB], dg[:, o:o + SUB], AF.Sigmoid, reads=[dg], writes=[dg])
                    pb = gen_ps()
                    P.mm(pb[:, 0:SUB], g2[:, :], dg[:, o:o + SUB], reads=[g2, dg], writes=[pb])
                    P.act(GF[:], pb[:, 0:SUB], AF.Copy, reads=[pb], writes=[GF])
                    yield
                    P.ts("vector", KK[:], k_, V(7), None, ALU.mult, reads=[praw[1], vecs], writes=[KK])
                    P.tt("vector", D2[:], KK[:], KK[:], ALU.mult, reads=[KK], writes=[D2])
                    pb = gen_ps()
                    P.mm(pb[:, 0:SUB], blk[:, :], D2[:], reads=[blk, D2], writes=[pb])
                    P.ts("vector", D2[:], pb[:, 0:SUB], 1e-12, None, ALU.max, reads=[pb], writes=[D2])
                    P.act(D2[:], D2[:], AF.Sqrt, reads=[D2], writes=[D2])
                    P.op("vector", lambda e: e.reciprocal(D2[:], D2[:]), reads=[D2], writes=[D2])
                    P.tt("vector", KK[:], KK[:], D2[:], ALU.mult, reads=[KK, D2], writes=[KK])
                    P.ts("vector", D2[:], AS[:], 1.0, V(8), ALU.subtract, ALU.mult, reads=[AS, vecs], writes=[D2])
                    P.stt("vector", KM[:], D2[:], 1.0, k_, ALU.add, ALU.mult, reads=[D2, praw[1]], writes=[KM])
                    P.tt("vector", BV[:], KK[:], AS[:], ALU.mult, reads=[KK, AS], writes=[BV])
                    P.stt("vector", RK[:], r_, V(9), KM[:], ALU.mult, ALU.mult, reads=[praw[0], vecs, KM], writes=[RK])
                    yield
                    P.op("vector", lambda e: e.tensor_tensor_scan(CU[:], rmask[:], LW[:], 0.0, ALU.mult, ALU.add), reads=[rmask, LW], writes=[CU])
                    cu3 = CU[:].rearrange("p (c t) -> p c t", t=CH)
                    P.act(gC[:], CU[:].rearrange("p (c t) -> p c t", t=CH)[:, :, CH - 1], AF.Exp, reads=[CU], writes=[gC])
                    P.act(E1[:], CU[:], AF.Exp, reads=[CU], writes=[E1])
                    P.tt("vector", RT[:], r_, E1[:], ALU.mult, reads=[praw[0], E1], writes=[RT])
                    P.act(E2[:], CU[:], AF.Exp, scale=-1.0, reads=[CU], writes=[E2])
                    P.tt("vector", KT_[:], KM[:], E2[:], ALU.mult, reads=[KM, E2], writes=[KT_])
                    P.tt("vector", BT[:], BV[:], E2[:], ALU.mult, reads=[BV, E2], writes=[BT])
                    P.tt("vector", D3[:], CU[:], LW[:], ALU.subtract, reads=[CU, LW], writes=[D3])
                    P.act(E1[:], D3[:], AF.Exp, reads=[D3], writes=[E1])
                    P.stt("vector", AT[:], KK[:], -1.0, E1[:], ALU.mult, ALU.mult, reads=[KK, E1], writes=[AT])
                    P.tt("vector", D3[:].rearrange("p (c t) -> p c t", t=CH), cu3[:, :, CH - 1:CH].to_broadcast([128, SUB // CH, CH]), cu3, ALU.subtract, reads=[CU], writes=[D3])
                    P.act(E2[:], D3[:], AF.Exp, reads=[D3], writes=[E2])
                    P.tt("vector", KHF[:], KM[:], E2[:], ALU.mult, reads=[KM, E2], writes=[KHF])
                    P.tt("vector", BHF[:], BV[:], E2[:], ALU.mult, reads=[BV, E2], writes=[BHF])
                    if RSTOP < 2:
                        continue
                    srcs = [(RT, None), (KT_, None), (AT, None), (BT, None), (KHF, None), (BHF, None), (praw[2], v_), (GF, None), (RK, None)]
                    for c in range(SUB // CH):
                        cs = slice(c * CH, (c + 1) * CH)
                        rot = 0
                        bd = []
                        for q, (tl, view) in enumerate(srcs):
                            src = (view if view is not None else tl[:])[:, cs]
                            dst = BD[q][rot]
                            P.tt("gpsimd", dst[:], src.unsqueeze(1).to_broadcast([128, 2, CH]), bdm[:, 0:2].unsqueeze(2).to_broadcast([128, 2, CH]),
                                 ALU.mult, reads=[tl, bdm], writes=[dst])
                            bd.append(dst)
                        f = lambda t: t[:].rearrange("p a b -> p (a b)")
                        Rb, Kb, Ab, Bb, KHb, BHb, Vb, Gb, RKb = bd
                        am = AM[rot]; tm = TM[rot]
                        pb = gen_ps()
                        P.mm(pb[:, 0:128], f(Bb), f(Ab), reads=[Bb, Ab], writes=[pb])
                        P.mm(pb[:, 128:256], f(Ab), f(Bb), reads=[Bb, Ab], writes=[pb])
                        P.mm(pb[:, 256:384], f(Kb), f(Ab), reads=[Kb, Ab], writes=[pb])
                        P.mm(pb[:, 384:512], f(Bb), f(Rb), reads=[Bb, Rb], writes=[pb])
                        P.tt("vector", am[:, 0:512], pb[:, 0:512], maskb[:, 0:512], ALU.mult, reads=[pb, maskb], writes=[am])
                        if CSTOP <= 1:
                            continue
                        yield
                        pb = gen_ps()
                        if G2V != 'B':
                            P.mm(pb[:, 0:128], f(Kb), f(Rb), reads=[Kb, Rb], writes=[pb])
                        if G2V != 'A':
                            P.tr(pb[:, 128:256], f(KHb), ident[:], reads=[KHb, ident], writes=[pb])
                            P.tr(pb[:, 256:384], f(BHb), ident[:], reads=[BHb, ident], writes=[pb])
                            P.tr(pb[:, 384:512], f(Vb), ident[:], reads=[Vb, ident], writes=[pb])
                        if G2V != 'B':
                            P.tt("vector", am[:, 512:640], pb[:, 0:128], maskb[:, 512:640], ALU.mult, reads=[pb, maskb], writes=[am])
                        if G2V != 'A':
                            if True:
                                P.copy("vector", tm[:, 0:384], pb[:, 128:512], reads=[pb], writes=[tm])
                            else:
                                P.act(tm[:, 0:384], pb[:, 128:512], AF.Copy, reads=[pb], writes=[tm])
                        NTm = am[:, 0:128]; Nm = am[:, 128:256]; AakT = am[:, 256:384]; ArbT = am[:, 384:512]; ArkT = am[:, 512:640]
                        Kh = tm[:, 0:128]; Bh = tm[:, 128:256]; Vt = tm[:, 256:384]
                        if CSTOP <= 2:
                            continue
                        yield
                        pb = gen_ps()
                        P.tr(pb[:, 0:128], f(Ab), ident[:], reads=[Ab, ident], writes=[pb])
                        P.tr(pb[:, 256:384], f(Gb), ident[:], reads=[Gb, ident], writes=[pb])
                        P.mm(pb[:, 128:256], AakT, Vt, reads=[am, tm], writes=[pb])
                        xs = XS[0]
                        P.act(xs[:], pb[:, 0:256], AF.Copy, reads=[pb], writes=[xs])
                        P.act(tm[:, 512:640], pb[:, 256:384], AF.Copy, reads=[pb], writes=[tm])
                        if CSTOP <= 3:
                            continue
                        pbc = gen_ps()
                        P.mm(pbc[:, 0:1], f(RKb), ones[:, 0:1], reads=[RKb, ones], writes=[pbc])
                        P.act(coef[:, c:c + 1], pbc[:, 0:1], AF.Copy, reads=[pbc], writes=[coef])
                        P.tt("gpsimd", Vc[:, c, :], tm[:, 256:320], tm[:, 320:384], ALU.add, reads=[tm], writes=[Vc])
                        P.tt("gpsimd", Gc[:, c, :], tm[:, 512:576], tm[:, 576:640], ALU.add, reads=[tm], writes=[Gc])
                        if CSTOP <= 4:
                            continue
                        yield
                        pn_ap, pt_ap, pT = Nm, NTm, am
                        xi = 0
                        for i in range(6):
                            pb = gen_ps()
                            xcur = XS[xi % 3]; xnew = XS[(xi + 1) % 3]
                            P.mm(pb[:, 0:256], pt_ap, xcur[:], reads=[pT, xcur], writes=[pb])
                            if i < 5:
                                P.mm(pb[:, 256:384], pt_ap, pn_ap, reads=[pT], writes=[pb])
                                P.mm(pb[:, 384:512], pn_ap, pt_ap, reads=[pT], writes=[pb])
                            P.tt("vector", xnew[:], xcur[:], pb[:, 0:256], ALU.add, reads=[xcur, pb], writes=[xnew])
                            if i < 5:
                                pnew = PS_[i % 3]
                                P.copy("vector", pnew[:], pb[:, 256:512], reads=[pb], writes=[pnew])
                                pn_ap, pt_ap, pT = pnew[:, 0:128], pnew[:, 128:256], pnew
                            xi += 1
                            yield
                        X = XS[xi % 3]
                        Ah = X[:, 0:128]; U0 = X[:, 128:256]
                        if CSTOP <= 5:
                            continue
                        yield
                        pb = gen_ps()
                        P.mm(pb[:, 0:128], Ah, ArbT, reads=[X, am], writes=[pb])
                        P.mm(pb[:, 128:256], Ah, Bh, reads=[X, tm], writes=[pb])
                        rh = RH[rot]; gt = GT[rot]
                        P.tt("vector", rh[:], pb[:, 0:128], f(Rb), ALU.add, reads=[pb, Rb], writes=[rh])
                        P.stt("vector", gt[:], ident[:], gC[:, c:c + 1], pb[:, 128:256], ALU.mult, ALU.add, reads=[ident, gC, pb], writes=[gt])
                        if CSTOP <= 6:
                            continue
                        yield
                        hcur = YH[hstate[0]]
                        hn_i = (hstate[0] + 1) % 5
                        yh = YH[hn_i]
                        pb = gen_ps()
                        P.mm(pb[:, 0:128], ArbT, U0, start=True, stop=False, reads=[am, X], writes=[pb])
                        P.mm(pb[:, 0:128], ArkT, Vt, start=False, stop=False, reads=[am, tm], writes=[pb])
                        P.mm(pb[:, 0:128], rh[:], hcur[:, 128:256], start=False, stop=True, reads=[rh, hcur], writes=[pb])
                        P.mm(pb[:, 128:256], Bh, U0, start=True, stop=False, reads=[tm, X], writes=[pb])
                        P.mm(pb[:, 128:256], Kh, Vt, start=False, stop=False, reads=[tm], writes=[pb])
                        P.mm(pb[:, 128:256], gt[:], hcur[:, 128:256], start=False, stop=True, reads=[gt, hcur], writes=[pb])
                        P.act(yh[:], pb[:, 0:256], AF.Copy, reads=[pb], writes=[yh])
                        hstate[0] = hn_i
                        yield
                        P.tt("gpsimd", Yc[:, c, :], yh[:, 0:64], yh[:, 64:128], ALU.add, reads=[yh], writes=[Yc])
                    if RSTOP < 3:
                        continue
                    yield
                    nchk = SUB // CH
                    P.op("vector", lambda e: e.tensor_reduce(st1[:], Yc[:], AX.X, ALU.add), reads=[Yc], writes=[st1])
                    P.tt("gpsimd", Ysq[:], Yc[:], Yc[:], ALU.mult, reads=[Yc], writes=[Ysq])
                    P.op("vector", lambda e: e.tensor_reduce(st2[:], Ysq[:], AX.X, ALU.add), reads=[Ysq], writes=[st2])
                    P.ts("vector", st1[:], st1[:], 1.0 / 64, None, ALU.mult, reads=[st1], writes=[st1])
                    P.tt("vector", st3[:], st1[:], st1[:], ALU.mult, reads=[st1], writes=[st3])
                    P.stt("vector", st2[:], st2[:], 1.0 / 64, st3[:], ALU.mult, ALU.subtract, reads=[st2, st3], writes=[st2])
                    P.act(st2[:], st2[:], AF.Sqrt, bias=64e-5, reads=[st2], writes=[st2])
                    P.op("vector", lambda e: e.reciprocal(st2[:], st2[:]), reads=[st2], writes=[st2])
                    bc = lambda t: t[:, :].unsqueeze(2).to_broadcast([128, nchk, 64])
                    bg = lambda t: t[:, :].unsqueeze(1).to_broadcast([128, nchk, 64])
                    P.tt("vector", yo[:], Yc[:], bc(st1), ALU.subtract, reads=[Yc, st1], writes=[yo])
                    P.tt("vector", yo[:], yo[:], bc(st2), ALU.mult, reads=[yo, st2], writes=[yo])
                    P.tt("vector", yo[:], yo[:], bg(gng), ALU.mult, reads=[yo, gng], writes=[yo])
                    P.tt("vector", yo[:], yo[:], bg(gnb), ALU.add, reads=[yo, gnb], writes=[yo])
                    P.tt("vector", Ysq[:], Vc[:], bc(coef), ALU.mult, reads=[Vc, coef], writes=[Ysq])
                    P.tt("vector", yo[:], yo[:], Ysq[:], ALU.add, reads=[yo, Ysq], writes=[yo])
                    P.tt("vector", yo[:], yo[:], Gc[:], ALU.mult, reads=[yo, Gc], writes=[yo])
                    t0 = c0 + s * SUB
                    for h in range(2 if RSTOP >= 4 else 0):
                        P.dma("sync", oap("orw", t0, SUB)[:, h * 64:(h + 1) * 64].rearrange("(c t) v -> t c v", t=CH),
                              yo[h * 64:(h + 1) * 64, :, :], reads=[yo], writes=[orw_d])


        def _g_s5():
            yield
            if do_s5:
                for s in range(TT // SUB):
                    us = ubuf[:, s * SUB:(s + 1) * SUB]
                    pby = ps[4]
                    for g in range(8):
                        pb = gen_ps()
                        P.mm(pb[:, 0:SUB], W1[:, g, :], us, reads=[W1, ubuf], writes=[pb])
                        P.mm(pb[:, SUB:2 * SUB], W2[:, g, :], us, reads=[W2, ubuf], writes=[pb])
                        x1, x2, xt, ht = s5x
                        P.tt("vector", x1[:], pb[:, 0:SUB], ctab[:, g, :], ALU.mult, reads=[pb, ctab], writes=[x1])
                        P.tt("vector", x2[:], pb[:, SUB:2 * SUB], stab[:, g, :], ALU.mult, reads=[pb, stab], writes=[x2])
                        P.tt("gpsimd", xt[:], x1[:], x2[:], ALU.add, reads=[x1, x2], writes=[xt])
                        P.op("vector", lambda e, g=g, xt=xt, ht=ht: e.tensor_tensor_scan(ht[:], mag_sp[:, g:g + 1].to_broadcast([128, SUB]), xt[:], h0[:, g:g + 1], ALU.mult, ALU.add),
                             reads=[mag_sp, xt, h0], writes=[ht])
                        P.tt("gpsimd", x1[:], ht[:], ctab[:, g, :], ALU.mult, reads=[ht, ctab], writes=[x1])
                        P.tt("vector", x2[:], ht[:], stab[:, g, :], ALU.mult, reads=[ht, stab], writes=[x2])
                        P.mm(pby[:, 0:SUB], Wc1g[:, g, :], x1[:], start=(g == 0), stop=False, reads=[Wc1g, x1], writes=[pby])
                        P.mm(pby[:, 0:SUB], Wc2g[:, g, :], x2[:], start=False, stop=(g == 7), reads=[Wc2g, x2], writes=[pby] )
                        pbh = gen_ps()
                        P.mm(pbh[:, 0:1], ident[:], x1[:, SUB - 1:SUB], start=True, stop=False, reads=[ident, x1], writes=[pbh])
                        P.mm(pbh[:, 0:1], jm[:], x2[:, SUB - 1:SUB], start=False, stop=True, reads=[jm, x2], writes=[pbh])
                        P.act(h0[:, g:g + 1], pbh[:, 0:1], AF.Copy, reads=[pbh], writes=[h0])
                        yield
                    P.stt("vector", s5y[:], us, V(13), pby[:, 0:SUB], ALU.mult, ALU.add, reads=[ubuf, vecs, pby], writes=[s5y])
                    P.act(s5o[:], s5y[:], AF.Gelu_apprx_tanh, reads=[s5y], writes=[s5o])
                    t0 = c0 + s * SUB
                    P.dma("sync", oap("os5", t0, SUB), s5o[:], reads=[s5o], writes=[os5_d])


        def _g_mla():
            yield
            if do_mla:
                pb = gen_ps()
                P.mm(pb[:, 0:TT], ones[:, :], sq[:, 0, :], start=True, stop=False, reads=[ones, sq], writes=[pb])
                P.mm(pb[:, 0:TT], ones[:, :], sq[:, 1, :], start=False, stop=True, reads=[ones, sq], writes=[pb])
                P.act(rstd[:], pb[:, 0:TT], AF.Sqrt, scale=1.0 / 256, bias=1e-6, reads=[pb], writes=[rstd])
                P.op("vector", lambda e: e.reciprocal(rstd[:], rstd[:]), reads=[rstd], writes=[rstd])
                for j in range(2):
                    P.stt("vector", cqn[:, j, :], cq[:, j, :], V(10 + j), rstd[:], ALU.mult, ALU.mult, reads=[cq, vecs, rstd], writes=[cqn])
                pb = gen_ps()
                P.mm(pb[:, 0:TT], ones[:, :], sqk[:], reads=[ones, sqk], writes=[pb])
                P.act(rstdk[:], pb[:, 0:TT], AF.Sqrt, scale=1.0 / 128, bias=1e-6, reads=[pb], writes=[rstdk])
                P.op("vector", lambda e: e.reciprocal(rstdk[:], rstdk[:]), reads=[rstdk], writes=[rstdk])
                P.stt("vector", ckvn[:], ckv[:], V(12), rstdk[:], ALU.mult, ALU.mult, reads=[ckv, vecs, rstdk], writes=[ckvn])
                yield
                for h in range(2):
                    pb = gen_ps()
                    P.mm(pb[0:64, 0:TT], kvupk[:, h * 64:(h + 1) * 64], ckvn[:], reads=[kvupk, ckvn], writes=[pb])
                    P.act(Kt[h][it][0:64, :], pb[0:64, 0:TT], AF.Copy, reads=[pb], writes=[Kt[h][it]])
                pb = gen_ps()
                for kb in range(QB):
                    P.mm(pb[:, kb * 128:(kb + 1) * 128], ckvn[:, kb * 128:(kb + 1) * 128], kvupv[:, :], reads=[ckvn, kvupv], writes=[pb])
                for kb in range(QB):
                    va = Vaug[it * QB + kb]
                    P.copy("vector", va[:, :, 0:64], pb[:, kb * 128:(kb + 1) * 128].rearrange("p (h v) -> p h v", h=2), reads=[pb], writes=[va])
                yield
                for h in range(2):
                    pb = gen_ps()
                    for j in range(2):
                        P.mm(pb[0:96, 0:TT], qup[:, j, h * 192:h * 192 + 96], cqn[:, j, :], start=(j == 0), stop=(j == 1), reads=[qup, cqn], writes=[pb])
                    pb2 = gen_ps()
                    for j in range(2):
                        P.mm(pb2[0:96, 0:TT], qup[:, j, h * 192 + 96:h * 192 + 192], cqn[:, j, :], start=(j == 0), stop=(j == 1), reads=[qup, cqn], writes=[pb2])
                    P.ts("vector", Qt[h][0:64, :], pb[0:64, 0:TT], ATTN_SCALE, None, ALU.mult, reads=[pb], writes=[Qt[h]])
                    P.stt("vector", qr1[64:96, :], pb[64:96, 0:TT], ATTN_SCALE, cct[64:96, :], ALU.mult, ALU.mult, reads=[pb, cct], writes=[qr1])
                    P.stt("vector", qr2[64:96, :], pb2[64:96, 0:TT], ATTN_SCALE, sst[64:96, :], ALU.mult, ALU.mult, reads=[pb2, sst], writes=[qr2])
                    P.tt("vector", Qt[h][64:96, :], qr1[64:96, :], qr2[64:96, :], ALU.add, reads=[qr1, qr2], writes=[Qt[h]])
                yield
                for h in range(2):
                    nkb = QB * it + QB
                    for kb in range(nkb):
                        kt = Kt[h][kb // QB]
                        kcol = (kb % QB) * 128
                        sbk = ps[2 + kb % 2]
                        diag = kb >= QB * it
                        qb0 = kb - QB * it if diag else 0
                        q0 = qb0 * 128
                        P.mm(sbk[:, q0:TT], kt[:, kcol:kcol + 128], Qt[h][:, q0:TT], reads=[kt, Qt[h]], writes=[sbk])
                        pt = Pt[pctr[0] % 3]; pctr[0] += 1
                        P.act(pt[:, q0:TT], sbk[:, q0:TT], AF.Exp, reads=[sbk], writes=[pt])
                        if diag:
                            P.tt("vector", pt[:, q0:q0 + 128], pt[:, q0:q0 + 128], tri[:, :], ALU.mult, reads=[pt, tri], writes=[pt])
                        va = Vaug[kb]
                        yield
                        for qb in range(qb0, QB):
                            last = (kb == QB * it + qb)
                            ob = ps[qb]
                            P.mm(ob[:, 0:65], pt[:, qb * 128:(qb + 1) * 128], va[:, h, :], start=(kb == 0), stop=last,
                                 reads=[pt, va], writes=[ob])
                    for qb in range(QB):
                        ob = ps[qb]
                        P.op("vector", lambda e, ob=ob, qb=qb: e.reciprocal(orec[:, qb, :], ob[:, 64:65]), reads=[ob], writes=[orec])
                        P.tt("vector", osb[:, qb, :], ob[:, 0:64], orec[:, qb, 0:1].to_broadcast([128, 64]), ALU.mult, reads=[ob, orec], writes=[osb])
                    P.dma("sync", oap("omla", c0, TT)[:, h * 64:(h + 1) * 64].rearrange("(q p) v -> p q v", p=128), osb[:], reads=[osb], writes=[omla_d])


        _gens = [_g_rwkv(), _g_s5(), _g_mla()]
        while _gens:
            for _g in list(_gens):
                try:
                    next(_g)
                except StopIteration:
                    _gens.remove(_g)

    if ctx is not None:
        return
    P.wait_tiles("sync", [orw_d, omla_d, os5_d])
    P.wait_all_dma("sync")
    P.emit()
    return nc, P


def consts():
    su = np.triu(np.ones((64, 64), np.float32), 1)
    uu = np.triu(np.ones((64, 64), np.float32), 0)
    z = np.zeros((64, 64), np.float32)
    bd = lambda m: np.block([[m, z], [z, m]])
    maskb = np.concatenate([bd(su), bd(su.T), bd(su), bd(uu), bd(uu)], axis=1)
    bdm = np.zeros((128, 2), np.float32); bdm[:64, 0] = 1; bdm[64:, 1] = 1
    blk = bd(np.ones((64, 64), np.float32))
    tri = np.triu(np.ones((128, 128), np.float32), 0)
    rmask = np.ones((128, SUB), np.float32); rmask[:, ::CH] = 0
    jm = np.zeros((128, 128), np.float32)
    for p in range(64):
        jm[64 + p, p] = -1.0
        jm[p, 64 + p] = 1.0
    gmask = np.zeros((128, 8), np.float32)
    for g in range(8):
        gmask[g * 16:(g + 1) * 16, g] = 1
    sign1 = np.ones((128, 1), np.float32); sign1[64:] = -1
    tidx = np.tile(np.arange(1, SUB + 1, dtype=np.float32)[None, :], (128, 1))
    pos = np.arange(8192, dtype=np.float32)
    inv_freq = (np.float32(10000.0) ** (-np.arange(0, 32, 2, dtype=np.float32) / np.float32(32))).astype(np.float32)
    ang = (pos[:, None] * inv_freq[None, :]).astype(np.float32)
    cos = np.cos(ang.astype(np.float64)).astype(np.float32).T
    sin = np.sin(ang.astype(np.float64)).astype(np.float32).T
    rope = np.stack([np.concatenate([cos, cos], 0), np.concatenate([-sin, sin], 0)], 0)
    return dict(ident=np.eye(128, dtype=np.float32), maskb=maskb, bdm=bdm, blkones=blk, tri=tri, rmask=rmask, jm=jm,
                gmask=gmask, sign1=sign1, tidx=tidx, rope=np.ascontiguousarray(rope))


def mixer_inputs(I, l, j, xT, C=None):
    C = C or consts()
    w_in = I['w_in'][l]
    hs = slice(j * 128, (j + 1) * 128)
    oM = 1792; oS = oM + 416
    z64 = np.zeros((1024, 64), np.float32); z32 = np.zeros((1024, 32), np.float32)
    kr = w_in[:, oM + 384:oM + 416]
    kr_sw = np.concatenate([kr[:, 16:32], kr[:, 0:16]], 1)
    cols = [w_in[:, 0:512][:, hs], w_in[:, 512:1024][:, hs], w_in[:, 1024:1536][:, hs],
            w_in[:, 1536:1664], w_in[:, 1664:1792],
            w_in[:, oM:oM + 256], w_in[:, oM + 256:oM + 384],
            np.concatenate([z64, kr, z32], 1),
            w_in[:, oS:oS + 512][:, hs],
            np.concatenate([z64, kr_sw, z32], 1)]
    wmix = np.concatenate(cols, 1)
    assert wmix.shape == (1024, 1408), wmix.shape
    mu = I['rwkv_mu'][l]
    vecs = np.zeros((128, 16), np.float32)
    vecs[:, 0] = mu[0:512][hs]; vecs[:, 1] = mu[512:1024][hs]; vecs[:, 2] = mu[1024:1536][hs]
    vecs[:, 3] = mu[1536:1664]; vecs[:, 4] = mu[1664:1792]
    vecs[:, 5] = I['rwkv_w0'][l][hs]; vecs[:, 6] = I['rwkv_a0'][l][hs]
    vecs[:, 7] = I['rwkv_k_k'][l][hs]; vecs[:, 8] = I['rwkv_k_a'][l][hs]
    vecs[:, 9] = I['rwkv_r_k'][l].reshape(512)[hs]
    vecs[:, 10] = I['mla_q_norm'][l][0:128]; vecs[:, 11] = I['mla_q_norm'][l][128:256]
    vecs[:, 12] = I['mla_kv_norm'][l]
    vecs[:, 13] = I['s5_d'][l][hs]
    w2a2 = np.concatenate([I['rwkv_w2'][l][:, hs], I['rwkv_a2'][l][:, hs]], 0)
    g2 = I['rwkv_g2'][l][:, hs]
    gng = np.repeat(I['rwkv_gn_g'][l][hs].reshape(2, 1, 64), 64, axis=1).reshape(128, 64)
    gnb = np.repeat(I['rwkv_gn_b'][l][hs].reshape(2, 1, 64), 64, axis=1).reshape(128, 64)
    qu = I['mla_q_up'][l].reshape(256, 8, 96)
    z = np.zeros((256, 64), np.float32)
    qparts = []
    for h in (2 * j, 2 * j + 1):
        main = qu[:, h, :]
        sw = np.concatenate([z, qu[:, h, 80:96], qu[:, h, 64:80]], 1)
        qparts += [main, sw]
    qup = np.concatenate(qparts, 1)
    kvu = I['mla_kv_up'][l].reshape(128, 8, 128)
    kvupk = np.concatenate([kvu[:, 2 * j, 0:64], kvu[:, 2 * j + 1, 0:64]], 1)
    kvupv = np.concatenate([kvu[:, 2 * j, 64:128], kvu[:, 2 * j + 1, 64:128]], 1)
    gs = slice(8 * j, 8 * j + 8)
    lre = I['s5_lambda_re'][l][gs]; lim = I['s5_lambda_im'][l][gs]; ls = I['s5_log_step'][l][gs]
    s5sp = np.concatenate([np.concatenate([lre.T, lre.T], 0), np.concatenate([lim.T, lim.T], 0), np.tile(ls[None, :], (128, 1))], 1)
    rep = lambda a: np.repeat(a, 16, axis=0)
    bre = I['s5_b_re'][l][gs].transpose(0, 2, 1).reshape(128, 64)
    bim = I['s5_b_im'][l][gs].transpose(0, 2, 1).reshape(128, 64)
    s5gc = np.concatenate([rep(lre), rep(lim), bre, bim, np.repeat(ls, 16)[:, None]], 1)
    cre = I['s5_c_re'][l][gs].reshape(128, 64).T
    cim = I['s5_c_im'][l][gs].reshape(128, 64).T
    cmain = np.concatenate([cre, cim], 0); cswap = np.concatenate([cim, cre], 0)
    d = dict(xT=xT, wmix=wmix, vecs=vecs, w2a2=w2a2, g2=g2, gng=gng, gnb=gnb, qup=qup, kvupk=kvupk, kvupv=kvupv,
             s5sp=s5sp, s5gc=s5gc, cmain=cmain, cswap=cswap)
    d.update(C)
    return {k: np.ascontiguousarray(v, dtype=np.float32) for k, v in d.items() if v is not None}


import math
import numpy as np

RT = 512
NROW = 2048
ALPHA = (2.0 * 2) ** 0.25
DFF = 2816


def build_row(ntile=NROW // RT, ctx=None):
    if ctx is None:
        nc = bass.Bass("TRN2", target_bir_lowering=False)
        P = Prog(nc)
        din = lambda n, s: P.dram(n, s, F32, kind="ExternalInput")
    else:
        P = ctx["P"]; din = ctx["din"]
    xT_d = din("xT", [1024, NROW])
    mo_d = [din("orwT", [512, NROW]), din("omlaT", [512, NROW]), din("os5T", [512, NROW])]
    wg_d = din("wgate", [1024, 3072])
    wo_d = [din("rwkv_out", [512, 1024]), din("mla_out", [512, 1024])]
    glu_d = din("s5_glu", [512, 2048])
    wout_d = din("w_out", [1024, 1024])
    w1_d = din("ffn_w1", [1024, DFF]); w3_d = din("ffn_w3", [1024, DFF]); w2_d = din("ffn_w2", [DFF, 1024])
    rv_d = din("rvecs", [128, 56])
    xo_d = P.dram("xout", [1024, NROW], F32, kind="ExternalOutput") if ctx is None else ctx["xout"]
    sb = P.sbuf
    rv = sb("rv", [128, 56]); P.dma("sync", rv[:], rv_d[:], writes=[rv])
    ones = sb("ones", [128, 128]); P.memset("vector", ones[:], 1.0, writes=[ones])
    xres = sb("xres", [128, 8, NROW])
    for k in range(8):
        P.dma("sync", xres[:, k, :], xT_d[k * 128:(k + 1) * 128, :], writes=[xres])
    xb = sb("xb", [128, 8, RT], BF16)
    mo = [sb(f"mo{i}", [128, 4, RT], BF16) for i in range(3)]
    merged = sb("merged", [128, 8, RT]); mb = sb("mb", [128, 8, RT], BF16)
    z = sb("z", [128, 8, RT])
    hb = sb("hb", [128, 22, RT], BF16)
    gt = sb("gt", [128, RT]); t1 = sb("t1", [128, RT]); t2 = sb("t2", [128, RT])
    mean = sb("mean", [128, RT]); rstd = sb("rstd", [128, RT])
    wbuf = [sb(f"wbuf{i}", [128, 4096], BF16) for i in range(3)]
    wctr = [0]
    ps = [P.psum(f"ps{i}", [128, 512]) for i in range(8)] if ctx is None else ctx["ps"]
    pctr = [0]

    def gen_ps():
        pctr[0] += 1
        return ps[pctr[0] % 8]

    def wload(d, r0, nk, c0, ncols):
        wb = wbuf[wctr[0] % 3]; wctr[0] += 1
        view = wb[:, 0:nk * ncols].rearrange("p (k c) -> p k c", k=nk)
        P.dma("gpsimd", view, d.ap[r0:r0 + nk * 128, c0:c0 + ncols].rearrange("(k p) c -> p k c", p=128), writes=[wb])
        return wb, view

    def layer_norm(gcol, bcol, cs):
        pS = gen_ps(); pQ = gen_ps()
        for m in range(8):
            P.mm(pS[:, 0:RT], ones[:, :], z[:, m, :], start=(m == 0), stop=(m == 7), reads=[ones, z], writes=[pS])
        for m in range(8):
            P.act(t1[:], z[:, m, :], AF.Square, reads=[z], writes=[t1])
            P.mm(pQ[:, 0:RT], ones[:, :], t1[:], start=(m == 0), stop=(m == 7), reads=[ones, t1], writes=[pQ])
        P.ts("vector", mean[:], pS[:, 0:RT], 1.0 / 1024, None, ALU.mult, reads=[pS], writes=[mean])
        P.tt("vector", t2[:], mean[:], mean[:], ALU.mult, reads=[mean], writes=[t2])
        P.stt("vector", rstd[:], pQ[:, 0:RT], 1.0 / 1024, t2[:], ALU.mult, ALU.subtract, reads=[pQ, t2], writes=[rstd])
        P.act(rstd[:], rstd[:], AF.Sqrt, bias=1e-5, reads=[rstd], writes=[rstd])
        P.op("vector", lambda e: e.reciprocal(rstd[:], rstd[:]), reads=[rstd], writes=[rstd])
        for m in range(8):
            P.tt("vector", t2[:], z[:, m, :], mean[:], ALU.subtract, reads=[z, mean], writes=[t2])
            P.tt("gpsimd", t2[:], t2[:], rstd[:], ALU.mult, reads=[t2, rstd], writes=[t2])
            P.ts("vector", xres[:, m, cs], t2[:], rv[:, gcol + m:gcol + m + 1], rv[:, bcol + m:bcol + m + 1], ALU.mult, ALU.add,
                 reads=[t2, rv], writes=[xres])

    for it in range(ntile):
        cs = slice(it * RT, (it + 1) * RT)
        P.copy("vector", xb[:], xres[:, :, cs], reads=[xres], writes=[xb])
        if ctx is None:
            for i in range(3):
                P.dma("gpsimd", mo[i][:], mo_d[i].ap[:, cs].rearrange("(k p) t -> p k t", p=128), writes=[mo[i]])
        else:
            ctx["moload"](mo, it, gen_ps)
        for br in range(3):
            for mc in range(2):
                wgt, wgv = wload(wg_d, 0, 8, br * 1024 + mc * 512, 512)
                if br < 2:
                    wyt, wyv = wload(wo_d[br], 0, 4, mc * 512, 512)
                else:
                    wyt, wyv = wload(glu_d, 0, 4, mc * 512, 512)
                    wy2t, wy2v = wload(glu_d, 0, 4, 1024 + mc * 512, 512)
                for mi in range(4):
                    m = mc * 4 + mi
                    pg = gen_ps()
                    for k in range(8):
                        P.mm(pg[:, 0:RT], wgv[:, k, mi * 128:(mi + 1) * 128], xb[:, k, :], start=(k == 0), stop=(k == 7), reads=[wgt, xb], writes=[pg])
                    P.act(gt[:], pg[:, 0:RT], AF.Sigmoid, bias=rv[:, br * 8 + m:br * 8 + m + 1], reads=[pg, rv], writes=[gt])
                    py = gen_ps()
                    for k in range(4):
                        P.mm(py[:, 0:RT], wyv[:, k, mi * 128:(mi + 1) * 128], mo[br][:, k, :], start=(k == 0), stop=(k == 3), reads=[wyt, mo[br]], writes=[py])
                    if br == 2:
                        py2 = gen_ps()
                        for k in range(4):
                            P.mm(py2[:, 0:RT], wy2v[:, k, mi * 128:(mi + 1) * 128], mo[br][:, k, :], start=(k == 0), stop=(k == 3), reads=[wy2t, mo[br]], writes=[py2])
                        P.act(t1[:], py2[:, 0:RT], AF.Sigmoid, reads=[py2], writes=[t1])
                        P.tt("vector", t1[:], py[:, 0:RT], t1[:], ALU.mult, reads=[py, t1], writes=[t1])
                        P.tt("vector", t1[:], t1[:], gt[:], ALU.mult, reads=[t1, gt], writes=[t1])
                        P.tt("gpsimd", merged[:, m, :], merged[:, m, :], t1[:], ALU.add, reads=[merged, t1], writes=[merged])
                    elif br == 0:
                        P.tt("vector", merged[:, m, :], py[:, 0:RT], gt[:], ALU.mult, reads=[py, gt], writes=[merged])
                    else:
                        P.tt("vector", t1[:], py[:, 0:RT], gt[:], ALU.mult, reads=[py, gt], writes=[t1])
                        P.tt("gpsimd", merged[:, m, :], merged[:, m, :], t1[:], ALU.add, reads=[merged, t1], writes=[merged])
        P.copy("vector", mb[:], merged[:], reads=[merged], writes=[mb])
        for mc in range(2):
            wt, wv = wload(wout_d, 0, 8, mc * 512, 512)
            for mi in range(4):
                m = mc * 4 + mi
                pz = gen_ps()
                for k in range(8):
                    P.mm(pz[:, 0:RT], wv[:, k, mi * 128:(mi + 1) * 128], mb[:, k, :], start=(k == 0), stop=(k == 7), reads=[wt, mb], writes=[pz])
                P.stt("vector", z[:, m, :], xres[:, m, cs], ALPHA, pz[:, 0:RT], ALU.mult, ALU.add, reads=[xres, pz], writes=[z])
        layer_norm(24, 32, cs)
        P.copy("vector", xb[:], xres[:, :, cs], reads=[xres], writes=[xb])
        for mc in range(6):
            ncols = 512 if mc < 5 else 256
            w1t, w1v = wload(w1_d, 0, 8, mc * 512, ncols)
            w3t, w3v = wload(w3_d, 0, 8, mc * 512, ncols)
            for mi in range(ncols // 128):
                m = mc * 4 + mi
                p1 = gen_ps(); p3 = gen_ps()
                for k in range(8):
                    P.mm(p1[:, 0:RT], w1v[:, k, mi * 128:(mi + 1) * 128], xb[:, k, :], start=(k == 0), stop=(k == 7), reads=[w1t, xb], writes=[p1])
                for k in range(8):
                    P.mm(p3[:, 0:RT], w3v[:, k, mi * 128:(mi + 1) * 128], xb[:, k, :], start=(k == 0), stop=(k == 7), reads=[w3t, xb], writes=[p3])
                P.act(t1[:], p1[:, 0:RT], AF.Silu, reads=[p1], writes=[t1])
                P.tt("vector", hb[:, m, :], p3[:, 0:RT], t1[:], ALU.mult, reads=[p3, t1], writes=[hb])
        for m in range(8):
            wt, wv = wload(w2_d, 0, 22, m * 128, 128)
            pz = gen_ps()
            for k in range(22):
                P.mm(pz[:, 0:RT], wv[:, k, :], hb[:, k, :], start=(k == 0), stop=(k == 21), reads=[wt, hb], writes=[pz])
            P.stt("vector", z[:, m, :], xres[:, m, cs], ALPHA, pz[:, 0:RT], ALU.mult, ALU.add, reads=[xres, pz], writes=[z])
        layer_norm(40, 48, cs)
    for k in range(8):
        P.dma("sync", xo_d[k * 128:(k + 1) * 128, :], xres[:, k, :], reads=[xres], writes=[xo_d])
    if ctx is not None:
        if ctx.get("xbf_out") is not None:
            for k in range(8):
                P.dma("gpsimd", ctx["xbf_out"][k * 128:(k + 1) * 128, :], xres[:, k, :], reads=[xres], writes=[ctx["xbf_out"]])
        return
    P.wait_tiles("sync", [xo_d])
    P.wait_all_dma("sync")
    P.emit()
    return nc, P


def row_inputs(I, l, xT_own, orwT, omlaT, os5T):
    rv = np.zeros((128, 56), np.float32)
    rv[:, 0:24] = I['gate_b'][l].reshape(24, 128).T
    rv[:, 24:32] = I['ln1_g'][l].reshape(8, 128).T
    rv[:, 32:40] = I['ln1_b'][l].reshape(8, 128).T
    rv[:, 40:48] = I['ln2_g'][l].reshape(8, 128).T
    rv[:, 48:56] = I['ln2_b'][l].reshape(8, 128).T
    d = dict(xT=xT_own, orwT=orwT, omlaT=omlaT, os5T=os5T, wgate=I['w_in'][l][:, 2720:5792],
             rwkv_out=I['rwkv_out'][l], mla_out=I['mla_out'][l], s5_glu=I['s5_glu'][l], w_out=I['w_out'][l],
             ffn_w1=I['ffn_w1'][l], ffn_w3=I['ffn_w3'][l], ffn_w2=I['ffn_w2'][l], rvecs=rv)
    return {k: np.ascontiguousarray(v, dtype=np.float32) for k, v in d.items() if v is not None}


import numpy as np

CONST_NAMES = ("ident", "maskb", "bdm", "blkones", "tri", "rmask", "jm", "gmask", "sign1", "tidx", "rope")
RG = [[0, 1, 2, 3], [4, 5, 6, 7]]


import os
FSKIP = os.environ.get('FSKIP', '')


def build_fused(nlayers=2):
    nc = bass.Bass("TRN2", target_bir_lowering=False)
    P = Prog(nc)
    P.use_arena()
    ext = {}
    qc = {}

    def getq(e):
        return e.partition_id() % 4

    def ext_in(name, shape):
        if name not in ext:
            ext[name] = P.dram(name, shape, F32, kind="ExternalInput")
        return ext[name]

    ps = [P.psum(f"ps{i}", [128, 512]) for i in range(8)]
    QW = 2048 * 128
    mixout = P.dram("mixout", [12 * 2048, 128], F32)
    gath = P.dram("gath", [12 * 4 * 2048, 128], F32)
    mine = P.dram("mine", [3 * 4 * 2048, 128], F32)
    xqbf = P.dram("xqbf", [1024, 2048], BF16)
    xg = P.dram("xg", [8 * 4 * 128, 2048], BF16)
    xres_d = P.dram("xres_d", [1024, 2048], F32)
    xout = P.dram("xout", [1024, 2048], F32, kind="ExternalOutput")

    for l in range(nlayers):
        last = (l == nlayers - 1)
        P.arena_reset()
        mo_t = T("mixout_t", mixout.ap)
        outs = (mo_t, mo_t, mo_t)

        def oap(kind, t0, n):
            q = t0 // 2048; lt = t0 % 2048
            br = {"orw": 0, "omla": 1, "os5": 2}[kind]
            blk = mixout.ap[(q * 3 + br) * 2048:(q * 3 + br + 1) * 2048, :]
            if br < 2:
                return blk[lt:lt + n, :]
            return blk.rearrange("(f a) c -> f (a c)", f=128)[:, lt:lt + n]

        def din_m(n, s, l=l):
            if n in CONST_NAMES or n == "xT":
                return ext_in(n, s)
            return ext_in(f"{n}_m{l}", s)

        def xload(xbf, c0):
            r = c0 // 2048; col = c0 % 2048
            P.dma("sync", xbf[:], xg.ap.rearrange("(k r p) t -> r p k t", k=8, r=4)[r][:, :, col:col + TT], reads=[xg], writes=[xbf])

        build_mixer(ctx=dict(P=P, din=din_m, ps=ps, outs=outs, oap=oap, xload=(None if l == 0 else xload)))
        if 'ag' not in FSKIP:
            for i in range(12):
                P.op("gpsimd", lambda e, i=i: e.collective_compute("AllGather", ALU.bypass, replica_groups=RG, ins=[mixout.ap[i * 2048:(i + 1) * 2048, :]],
                                                                    outs=[gath.ap[i * 8192:(i + 1) * 8192, :]]),
                     reads=[mo_t], writes=[gath], dma=True, amt=1)

        def cp_mine(e):
            q = getq(e)
            gv = gath.ap.rearrange("(q x) f -> q (x f)", q=4)
            return e.dma_start(out=mine.ap.rearrange("(o x) f -> o (x f)", o=1), in_=gv[bass.ds(q, 1), :])
        if 'cp' not in FSKIP:
            P.op("sync", cp_mine, reads=[gath], writes=[mine], dma=True)
        if 'row' in FSKIP:
            break
        P.barrier()
        P.arena_reset()

        def din_r(n, s, l=l):
            if n in ("orwT", "omlaT", "os5T"):
                return None
            if n == "xT":
                return ext_in("xTq", s) if l == 0 else xres_d
            return ext_in(f"{n}_r{l}", s)

        st = {}

        def moload(mo, it, gen_ps):
            if "tmt" not in st:
                st["tmt"] = [P.sbuf(f"tmt{i}", [128, 4, 128]) for i in range(2)]
                st["s5st"] = [P.sbuf(f"s5st{i}", [128, RT]) for i in range(2)]
                st["identf"] = P.sbuf("identf", [128, 128])
                st["n"] = 0
                P.dma("sync", st["identf"][:], ext_in("ident", [128, 128])[:], writes=[st["identf"]])
            identf = st["identf"]
            t0 = it * RT
            for br in range(2):
                for r in range(4):
                    tmt = st["tmt"][st["n"] % 2]; st["n"] += 1
                    P.dma("sync", tmt[:], mine.ap[(br * 4 + r) * 2048 + t0:(br * 4 + r) * 2048 + t0 + RT, :].rearrange("(b p) f -> p b f", p=128), reads=[mine], writes=[tmt])
                    pb = gen_ps()
                    for b4 in range(4):
                        P.mm(pb[:, b4 * 128:(b4 + 1) * 128], tmt[:, b4, :], identf[:], reads=[tmt, identf], writes=[pb])
                    P.copy("vector", mo[br][:, r, :], pb[:, 0:512], reads=[pb], writes=[mo[br]])
            for r in range(4):
                P.dma("gpsimd", mo[2][:, r, :], mine.ap[(8 + r) * 2048:(9 + r) * 2048, :].rearrange("(f a) c -> f (a c)", f=128)[:, t0:t0 + RT], reads=[mine], writes=[mo[2]])

        build_row(ctx=dict(P=P, din=din_r, ps=ps, xout=(xout if last else xres_d), xbf_out=(None if last else xqbf), moload=moload))
        if not last:
            for k in range(8):
                P.op("gpsimd", lambda e, k=k: e.collective_compute("AllGather", ALU.bypass, replica_groups=RG, ins=[xqbf.ap[k * 128:(k + 1) * 128, :]],
                                                                    outs=[xg.ap[k * 512:(k + 1) * 512, :]]),
                     reads=[xqbf], writes=[xg], dma=True, amt=1)
            P.barrier()
    P.wait_tiles("sync", [xout])
    P.wait_all_dma("sync")
    P.emit()
    return nc, P


def fused_inputs(I, b, j, C, nlayers=2):
    x = I['x']
    d = dict(C)
    d["xT"] = x[b].T
    d["xTq"] = x[b, j * 2048:(j + 1) * 2048].T
    for l in range(nlayers):
        mi = mixer_inputs(I, l, j, None, C)
        for k, v in mi.items():
            if k in C or k == "xT":
                continue
            d[f"{k}_m{l}"] = v
        ri = row_inputs(I, l, None, None, None, None)
        for k, v in ri.items():
            if k in ("xT", "orwT", "omlaT", "os5T"):
                continue
            d[f"{k}_r{l}"] = v
    return {k: np.ascontiguousarray(v, dtype=np.float32) for k, v in d.items()}


from concourse.bass_utils import run_bass_kernel_spmd


def kernel(**inputs):
    I = {k: np.asarray(v, dtype=np.float32) for k, v in inputs.items()}
    C = consts()
    nc, P = build_fused(2)
    in_maps = [fused_inputs(I, b, j, C, 2) for b in range(2) for j in range(4)]
    res = run_bass_kernel_spmd(nc, in_maps, core_ids=list(range(8))).results
    P.close()
    out = np.stack([np.concatenate([res[b * 4 + q]["xout"] for q in range(4)], 1).T for b in range(2)], 0)
    return np.ascontiguousarray(out.astype(np.float32))
```

```python
import contextlib
import numpy as np
import concourse.bass as bass
import concourse.mybir as mybir

F32 = mybir.dt.float32
BF16 = mybir.dt.bfloat16
ALU = mybir.AluOpType
AF = mybir.ActivationFunctionType
AX = mybir.AxisListType


class T:
    def __init__(self, name, ap):
        self.name = name
        self.ap = ap
        self.w = None
        self.r = {}

    def __getitem__(self, k):
        return self.ap[k]


class Prog:
    NDMA = 48

    def __init__(self, nc):
        self.nc = nc
        self.st = contextlib.ExitStack()
        self.engs = ["tensor", "vector", "scalar", "gpsimd", "sync"]
        self.ops = {e: [] for e in self.engs}
        self.cnt = {e: 0 for e in self.engs}
        self.known = {e: {} for e in self.engs}
        self.esem = {e: self.st.enter_context(nc.semaphore("es_" + e)) for e in self.engs}
        self.dsem = [self.st.enter_context(nc.semaphore(f"ds{i}")) for i in range(self.NDMA)]
        self.dcnt = [0] * self.NDMA
        self.csem = self.st.enter_context(nc.semaphore("cc_sem"))
        self.ccnt = 0
        self.dma_i = 0
        self.nuniq = 0

    ARENA_WORDS = 52800

    ARENA_R_WORDS = 14600

    def use_arena(self):
        self.ARENA_WORDS = 52800 - self.ARENA_R_WORDS
        self.arena = self.st.enter_context(self.nc.sbuf_tensor("arena", [128, self.ARENA_WORDS], F32))
        self.arena_r = self.st.enter_context(self.nc.sbuf_tensor("arena_r", [128, self.ARENA_R_WORDS], F32))
        self.aoff = 0
        self.atop = 0

    def arena_reset(self):
        self.aoff = 0
        self.atop = 0

    def sbuf(self, name, shape, dt=F32, top=False):
        if getattr(self, "arena", None) is None:
            t = self.st.enter_context(self.nc.sbuf_tensor(name, list(shape), dt))
            return T(name, t)
        shape = list(shape)
        nelem = 1
        for d in shape[1:]:
            nelem *= d
        four = dt in (F32, mybir.dt.int32)
        words = nelem if four else (nelem + 1) // 2
        if top:
            off = self.atop
            self.atop += words
            assert self.atop <= self.ARENA_R_WORDS, ("arena_r overflow", name, self.atop, words)
            ap = self.arena_r[0:shape[0], off:off + words]
        else:
            off = self.aoff
            self.aoff += words
            assert self.aoff <= self.ARENA_WORDS, ("arena overflow", name, self.aoff, words)
            ap = self.arena[0:shape[0], off:off + words]
        if dt != F32:
            ap = ap.bitcast(dt)
            if not four:
                ap = ap[:, 0:nelem]
        if len(shape) == 3:
            ap = ap.rearrange("p (a b) -> p a b", a=shape[1])
        elif len(shape) == 4:
            ap = ap.rearrange("p (a b c) -> p a b c", a=shape[1], b=shape[2])
        return T(name, ap)

    def barrier(self):
        for eng in self.engs:
            waits = []
            kn = self.known[eng]
            for e2 in self.engs:
                if e2 == eng or self.cnt[e2] == 0:
                    continue
                k = ("e", e2)
                if kn.get(k, 0) < self.cnt[e2]:
                    kn[k] = self.cnt[e2]
                    waits.append((k, self.cnt[e2]))
            for slot in range(self.NDMA):
                k = ("d", slot)
                if self.dcnt[slot] > 0 and kn.get(k, 0) < self.dcnt[slot]:
                    kn[k] = self.dcnt[slot]
                    waits.append((k, self.dcnt[slot]))
            if self.ccnt > 0 and kn.get(("c", 0), 0) < self.ccnt:
                kn[("c", 0)] = self.ccnt
                waits.append((("c", 0), self.ccnt))
            self.ops[eng].append((None, waits, None))

    def psum(self, name, shape, dt=F32):
        t = self.st.enter_context(self.nc.psum_tensor(name, list(shape), dt))
        return T(name, t)

    def dram(self, name, shape, dt, kind="Internal"):
        t = self.nc.dram_tensor(name, list(shape), dt, kind=kind)
        return T(name, t.ap())

    def sub(self, t, name, key):
        return T(name, t.ap[key])

    def _tokkey(self, tok):
        return (tok[0], tok[1])

    def op(self, eng, fn, reads=(), writes=(), inc=True, dma=False, touch=(), amt=16):
        deps = {}

        def add(tok):
            if tok is None:
                return
            k = self._tokkey(tok)
            if deps.get(k, 0) < tok[2]:
                deps[k] = tok[2]

        for t in reads:
            add(t.w)
        for t in list(writes) + list(touch):
            add(t.w)
            for k, v in t.r.items():
                add((k[0], k[1], v))
        if dma and amt == 1:
            self.ccnt += 1
            tok = ("c", 0, self.ccnt, 1)
        elif dma:
            slot = self.dma_i % self.NDMA
            self.dma_i += 1
            if self.dcnt[slot] > 0:
                add(("d", slot, self.dcnt[slot]))
            self.dcnt[slot] += amt
            tok = ("d", slot, self.dcnt[slot], amt)
        else:
            tok = ("e", eng, self.cnt[eng] + 1)
            if inc:
                self.cnt[eng] += 1
        waits = []
        kn = self.known[eng]
        for k, v in deps.items():
            if k[0] == "e" and k[1] == eng:
                if eng == "tensor":
                    continue
                assert v <= self.cnt[eng] or (v == tok[2] and False), (eng, v, self.cnt[eng])
            if kn.get(k, 0) >= v:
                continue
            kn[k] = v
            waits.append((k, v))
        self.ops[eng].append((fn, waits, tok if (inc or dma) else None))
        for t in reads:
            k = self._tokkey(tok)
            if t.r.get(k, 0) < tok[2]:
                t.r[k] = tok[2]
        for t in writes:
            t.w = tok
            t.r = {}
        return tok

    def wait_tiles(self, eng, tiles):
        self.op(eng, None, reads=tiles, inc=False)

    def wait_all_dma(self, eng="sync"):
        waits = []
        for slot in range(self.NDMA):
            if self.dcnt[slot] > 0 and self.known[eng].get(("d", slot), 0) < self.dcnt[slot]:
                waits.append((("d", slot), self.dcnt[slot]))
                self.known[eng][("d", slot)] = self.dcnt[slot]
        if self.ccnt > 0 and self.known[eng].get(("c", 0), 0) < self.ccnt:
            self.known[eng][("c", 0)] = self.ccnt
            waits.append((("c", 0), self.ccnt))
        self.ops[eng].append((None, waits, None))

    def _sem(self, k):
        if k[0] == "c":
            return self.csem
        return self.esem[k[1]] if k[0] == "e" else self.dsem[k[1]]

    def emit(self):
        nc = self.nc
        with nc.Block() as block:
            def mk(name):
                def body(e):
                    for fn, waits, tok in self.ops[name]:
                        for k, v in waits:
                            e.wait_ge(self._sem(k), v)
                        if fn is None:
                            continue
                        ins = fn(e)
                        if tok is not None:
                            if tok[0] == "c":
                                ins.then_inc(self.csem, 1)
                            elif tok[0] == "d":
                                ins.then_inc(self.dsem[tok[1]], tok[3])
                            else:
                                ins.then_inc(self.esem[name], 1)
                return body
            block.tensor(mk("tensor"))
            block.vector(mk("vector"))
            block.scalar(mk("scalar"))
            block.gpsimd(mk("gpsimd"))
            block.sync(mk("sync"))

    def close(self):
        self.st.close()

    def dma(self, eng, out_ap, in_ap, reads=(), writes=()):
        return self.op(eng, lambda e: e.dma_start(out=out_ap, in_=in_ap), reads=reads, writes=writes, dma=True)

    def mm(self, out_ap, lhsT, rhs, start=True, stop=True, reads=(), writes=(), **kw):
        return self.op("tensor", lambda e: e.matmul(out_ap, lhsT, rhs, start=start, stop=stop, **kw),
                       reads=reads, writes=writes if stop else (), touch=() if stop else writes, inc=True)

    def tr(self, out_ap, in_ap, ident_ap, reads=(), writes=(), inc=True):
        return self.op("tensor", lambda e: e.matmul(out_ap, in_ap, ident_ap, start=True, stop=True), reads=reads, writes=writes if inc else (), inc=inc)

    def act(self, out_ap, in_ap, func, reads=(), writes=(), eng="scalar", **kw):
        return self.op(eng, lambda e: e.activation(out_ap, in_ap, func, **kw), reads=reads, writes=writes)

    def tt(self, eng, out_ap, a, b, op, reads=(), writes=()):
        return self.op(eng, lambda e: e.tensor_tensor(out_ap, a, b, op), reads=reads, writes=writes)

    def ts(self, eng, out_ap, a, s1, s2, op0, op1=None, reads=(), writes=()):
        if op1 is None:
            return self.op(eng, lambda e: e.tensor_scalar(out_ap, a, s1, None, op0), reads=reads, writes=writes)
        return self.op(eng, lambda e: e.tensor_scalar(out_ap, a, s1, s2, op0, op1), reads=reads, writes=writes)

    def stt(self, eng, out_ap, in0, scalar, in1, op0, op1, reads=(), writes=()):
        return self.op(eng, lambda e: e.scalar_tensor_tensor(out_ap, in0, scalar, in1, op0, op1), reads=reads, writes=writes)

    def copy(self, eng, out_ap, in_ap, reads=(), writes=()):
        if eng == "scalar":
            return self.op(eng, lambda e: e.copy(out_ap, in_ap), reads=reads, writes=writes)
        return self.op(eng, lambda e: e.tensor_copy(out_ap, in_ap), reads=reads, writes=writes)

    def memset(self, eng, ap, val, writes=()):
        return self.op(eng, lambda e: e.memset(ap, val), writes=writes)


import math, os
RSTOP = int(os.environ.get('RSTOP', '9'))
CSTOP = int(os.environ.get('CSTOP', '99'))
G2V = os.environ.get('G2V', '')
import numpy as np

TT = 256
NTILE = 32
QB = TT // 128
SUB = 256
S5SUB = 128
CH = 64
ATTN_SCALE = 1.0 / math.sqrt(96.0)
NEG_EXPM05 = -math.exp(-0.5)
USE_R = os.environ.get('USE_R', '1') == '1'
F32R = mybir.dt.float32r
R = (lambda ap: ap.bitcast(F32R)) if USE_R else (lambda ap: ap)


def build_mixer(ntile=NTILE, do_rwkv=True, do_mla=True, do_s5=True, ctx=None):
    if ctx is None:
        nc = bass.Bass("TRN2", target_bir_lowering=False)
        P = Prog(nc)
        din = lambda n, s: P.dram(n, s, F32, kind="ExternalInput")
    else:
        P = ctx["P"]; din = ctx["din"]
    xT_d = din("xT", [1024, 8192])
    wmix_d = din("wmix", [1024, 1408])
    vecs_d = din("vecs", [128, 16])
    w2a2_d = din("w2a2", [128, 128])
    g2_d = din("g2", [128, 128])
    gng_d = din("gng", [128, 64])
    gnb_d = din("gnb", [128, 64])
    qup_d = din("qup", [256, 4 * 96])
    kvupk_d = din("kvupk", [128, 128])
    kvupv_d = din("kvupv", [128, 128])
    rope_d = din("rope", [2, 32, 8192])
    ident_d = din("ident", [128, 128])
    maskb_d = din("maskb", [128, 640])
    bdm_d = din("bdm", [128, 2])
    blk_d = din("blkones", [128, 128])
    tri_d = din("tri", [128, 128])
    rmask_d = din("rmask", [128, SUB])
    jm_d = din("jm", [128, 128])
    s5sp_d = din("s5sp", [128, 24])
    s5gc_d = din("s5gc", [128, 4 * 64 + 1])
    gmask_d = din("gmask", [128, 8])
    cmain_d = din("cmain", [128, 128])
    cswap_d = din("cswap", [128, 128])
    sign1_d = din("sign1", [128, 1])
    tidx_d = din("tidx", [128, S5SUB])
    if ctx is None:
        orw_d = P.dram("orw", [8192, 128], F32, kind="ExternalOutput")
        omla_d = P.dram("omla", [8192, 128], F32, kind="ExternalOutput")
        os5_d = P.dram("os5", [128, 8192], F32, kind="ExternalOutput")
    else:
        orw_d, omla_d, os5_d = ctx["outs"]
    if ctx is not None and ctx.get("oap") is not None:
        oap = ctx["oap"]
    else:
        def oap(kind, t0, n):
            if kind == "orw":
                return orw_d.ap[t0:t0 + n, :]
            if kind == "omla":
                return omla_d.ap[t0:t0 + n, :]
            return os5_d.ap[:, t0:t0 + n]

    sb = P.sbuf
    sbt = lambda n, shp, dt=F32: P.sbuf(n, shp, dt, top=True)
    def load(name, d, shape, dt=F32, eng="sync"):
        t = sb(name, shape, dt)
        P.dma("gpsimd" if dt != F32 else eng, t[:], d[:], writes=[t])
        return t
    ident = load("ident_s", ident_d, [128, 128])
    maskb = load("maskb_s", maskb_d, [128, 640])
    bdm = load("bdm_s", bdm_d, [128, 2])
    blk = load("blk_s", blk_d, [128, 128])
    tri = load("tri_s", tri_d, [128, 128], BF16)
    rmask = load("rmask_s", rmask_d, [128, SUB])
    jm = load("jm_s", jm_d, [128, 128])
    vecs = load("vecs_s", vecs_d, [128, 16])
    w2a2 = load("w2a2_s", w2a2_d, [128, 128])
    g2 = load("g2_s", g2_d, [128, 128])
    gng = load("gng_s", gng_d, [128, 64])
    gnb = load("gnb_s", gnb_d, [128, 64])
    kvupk = load("kvupk_s", kvupk_d, [128, 128], BF16)
    kvupv = load("kvupv_s", kvupv_d, [128, 128], BF16)
    qup = sb("qup_s", [128, 2, 384], BF16)
    P.dma("gpsimd", qup[:], qup_d.ap.rearrange("(k p) m -> p k m", p=128), writes=[qup])
    wmix = sb("wmix_s", [128, 8, 1408], BF16)
    for k in range(8):
        P.dma("gpsimd", wmix[:, k, :], wmix_d[k * 128:(k + 1) * 128, :], writes=[wmix])
    ones = sb("ones_s", [128, 128])
    P.memset("vector", ones[:], 1.0, writes=[ones])
    identr = sbt("identr", [128, 128]); onesr = sbt("onesr", [128, 1]); jmr = sbt("jmr", [128, 128])
    P.copy("vector", R(identr[:]), ident[:], reads=[ident], writes=[identr])
    P.copy("vector", R(onesr[:]), ones[:, 0:1], reads=[ones], writes=[onesr])
    P.copy("vector", R(jmr[:]), jm[:], reads=[jm], writes=[jmr])

    ps = [P.psum(f"ps{i}", [128, 512]) for i in range(8)] if ctx is None else ctx["ps"]
    gctr = [0]

    def gen_ps():
        gctr[0] += 1
        return ps[5 + gctr[0] % 3]

    V = lambda c: vecs[:, c:c + 1]

    xbf = sb("xbf", [128, 8, TT], BF16)
    praw = [sb(f"praw{m}", [128, TT + 1]) for m in range(5)]
    for m in range(5):
        P.memset("vector", praw[m][:, 0:1], 0.0, writes=[praw[m]])
    dtmp = sb("dtmp", [128, TT])
    cct = sb("cct", [128, TT])
    sst = sb("sst", [128, TT])

    if do_s5:
        s5sp = load("s5sp_s", s5sp_d, [128, 24])
        s5gc = load("s5gc_s", s5gc_d, [128, 257])
        gmask = load("gmask_s", gmask_d, [128, 8])
        cmain = load("cmain_s", cmain_d, [128, 128])
        cswap = load("cswap_s", cswap_d, [128, 128])
        sign1 = load("sign1_s", sign1_d, [128, 1])
        tidx = load("tidx_s", tidx_d, [128, S5SUB])
        TWO_PI = 2.0 * math.pi

        scs = {}

        def sincos(name, ang, shape, want_cos, out=None):
            key = tuple(shape)
            if key not in scs:
                scs[key] = (sb(f"scf{len(scs)}", shape), sb(f"sci{len(scs)}", shape, mybir.dt.int32), sb(f"scg{len(scs)}", shape))
            f, fi, g = scs[key]
            o = sb(name + "_o", shape) if out is None else out
            P.ts("vector", f[:], ang[:], 1.0 / TWO_PI, 0.25 if want_cos else 0.0, ALU.mult, ALU.add, reads=[ang], writes=[f])
            P.copy("vector", fi[:], f[:], reads=[f], writes=[fi])
            P.copy("vector", g[:], fi[:], reads=[fi], writes=[g])
            P.tt("vector", f[:], f[:], g[:], ALU.subtract, reads=[f, g], writes=[f])
            P.ts("vector", g[:], f[:], 0.5, None, ALU.is_ge, reads=[f], writes=[g])
            P.tt("vector", f[:], f[:], g[:], ALU.subtract, reads=[f, g], writes=[f])
            P.ts("vector", g[:], f[:], -0.5, None, ALU.is_lt, reads=[f], writes=[g])
            P.tt("vector", f[:], f[:], g[:], ALU.add, reads=[f, g], writes=[f])
            oap = o[:] if out is None else out_ap[0]
            P.act(oap, f[:], AF.Sin, scale=TWO_PI, reads=[f], writes=[o])
            return o

        step_sp = sb("step_sp", [128, 8])
        P.act(step_sp[:], s5sp[:, 16:24], AF.Exp, reads=[s5sp], writes=[step_sp])
        lre_sp = sb("lre_sp", [128, 8])
        P.ts("vector", lre_sp[:], s5sp[:, 0:8], -1e-4, None, ALU.min, reads=[s5sp], writes=[lre_sp])
        P.tt("vector", lre_sp[:], lre_sp[:], step_sp[:], ALU.mult, reads=[lre_sp, step_sp], writes=[lre_sp])
        mag_sp = sb("mag_sp", [128, 8])
        P.act(mag_sp[:], lre_sp[:], AF.Exp, reads=[lre_sp], writes=[mag_sp])
        th_sp = sb("th_sp", [128, 8])
        P.tt("vector", th_sp[:], s5sp[:, 8:16], step_sp[:], ALU.mult, reads=[s5sp, step_sp], writes=[th_sp])
        thf = sb("thf", [128, 8]); thi = sb("thi", [128, 8], mybir.dt.int32); thg = sb("thg", [128, 8])
        P.ts("vector", thf[:], th_sp[:], 1.0 / TWO_PI, None, ALU.mult, reads=[th_sp], writes=[thf])
        P.copy("vector", thi[:], thf[:], reads=[thf], writes=[thi])
        P.copy("vector", thg[:], thi[:], reads=[thi], writes=[thg])
        P.tt("vector", thf[:], thf[:], thg[:], ALU.subtract, reads=[thf, thg], writes=[thf])
        ctab = sb("ctab", [128, 8, S5SUB]); stab = sb("stab", [128, 8, S5SUB])
        angt = sb("angt", [128, S5SUB])
        for g in range(8):
            P.ts("vector", angt[:], tidx[:], thf[:, g:g + 1], TWO_PI, ALU.mult, ALU.mult, reads=[tidx, thf], writes=[angt])
            P.ts("vector", angt[:], angt[:], TWO_PI * (S5SUB + 1), None, ALU.add, reads=[angt], writes=[angt])
            out_ap = [stab[:, g, :]]
            sincos(f"sc_s{g}", angt, [128, S5SUB], False, out=stab)
            out_ap = [ctab[:, g, :]]
            sincos(f"sc_c{g}", angt, [128, S5SUB], True, out=ctab)
        step_gc = sb("step_gc", [128, 1])
        P.act(step_gc[:], s5gc[:, 256:257], AF.Exp, reads=[s5gc], writes=[step_gc])
        lre = sb("lre_gc", [128, 64])
        P.ts("vector", lre[:], s5gc[:, 0:64], -1e-4, None, ALU.min, reads=[s5gc], writes=[lre])
        lim = s5gc
        magg = sb("mag_gc", [128, 64])
        P.act(magg[:], lre[:], AF.Exp, scale=step_gc[:, 0:1], reads=[lre, step_gc], writes=[magg])
        angg = sb("ang_gc", [128, 64])
        P.ts("vector", angg[:], s5gc[:, 64:128], step_gc[:, 0:1], None, ALU.mult, reads=[s5gc, step_gc], writes=[angg])
        sing = sincos("sg", angg, [128, 64], False)
        cosg = sincos("cg", angg, [128, 64], True)
        lbre = sb("lbre", [128, 64]); lbim = sb("lbim", [128, 64])
        P.tt("vector", lbre[:], magg[:], cosg[:], ALU.mult, reads=[magg, cosg], writes=[lbre])
        P.tt("vector", lbim[:], magg[:], sing[:], ALU.mult, reads=[magg, sing], writes=[lbim])
        den = sb("den", [128, 64]); t1 = sb("s5t1", [128, 64]); t2 = sb("s5t2", [128, 64])
        P.tt("vector", den[:], lre[:], lre[:], ALU.mult, reads=[lre], writes=[den])
        P.tt("vector", t1[:], s5gc[:, 64:128], s5gc[:, 64:128], ALU.mult, reads=[s5gc], writes=[t1])
        P.tt("vector", den[:], den[:], t1[:], ALU.add, reads=[den, t1], writes=[den])
        P.op("vector", lambda e: e.reciprocal(den[:], den[:]), reads=[den], writes=[den])
        nre = sb("nre", [128, 64])
        P.ts("vector", nre[:], lbre[:], -1.0, None, ALU.add, reads=[lbre], writes=[nre])
        fre = sb("fre", [128, 64]); fim = sb("fim", [128, 64])
        P.tt("vector", t1[:], nre[:], lre[:], ALU.mult, reads=[nre, lre], writes=[t1])
        P.tt("vector", t2[:], lbim[:], s5gc[:, 64:128], ALU.mult, reads=[lbim, s5gc], writes=[t2])
        P.tt("vector", t1[:], t1[:], t2[:], ALU.add, reads=[t1, t2], writes=[t1])
        P.tt("vector", fre[:], t1[:], den[:], ALU.mult, reads=[t1, den], writes=[fre])
        P.tt("vector", t1[:], lbim[:], lre[:], ALU.mult, reads=[lbim, lre], writes=[t1])
        P.tt("vector", t2[:], nre[:], s5gc[:, 64:128], ALU.mult, reads=[nre, s5gc], writes=[t2])
        P.tt("vector", t1[:], t1[:], t2[:], ALU.subtract, reads=[t1, t2], writes=[t1])
        P.tt("vector", fim[:], t1[:], den[:], ALU.mult, reads=[t1, den], writes=[fim])
        bbre = sb("bbre", [128, 64]); bbim = sb("bbim", [128, 64])
        bre = s5gc[:, 128:192]; bim = s5gc[:, 192:256]
        P.tt("vector", t1[:], fre[:], bre, ALU.mult, reads=[fre, s5gc], writes=[t1])
        P.tt("vector", t2[:], fim[:], bim, ALU.mult, reads=[fim, s5gc], writes=[t2])
        P.tt("vector", bbre[:], t1[:], t2[:], ALU.subtract, reads=[t1, t2], writes=[bbre])
        P.tt("vector", t1[:], fre[:], bim, ALU.mult, reads=[fre, s5gc], writes=[t1])
        P.tt("vector", t2[:], fim[:], bre, ALU.mult, reads=[fim, s5gc], writes=[t2])
        P.tt("vector", bbim[:], t1[:], t2[:], ALU.add, reads=[t1, t2], writes=[bbim])
        W1 = sbt("s5W1", [128, 8, 128]); W2 = sbt("s5W2", [128, 8, 128])
        ngm = sb("ngmask", [128, 8])
        P.ts("vector", ngm[:], gmask[:], -1.0, None, ALU.mult, reads=[gmask], writes=[ngm])
        for g in range(8):
            P.ts("vector", R(W1[:, g, 0:64]), bbre[:], gmask[:, g:g + 1], None, ALU.mult, reads=[bbre, gmask], writes=[W1])
            P.ts("vector", R(W1[:, g, 64:128]), bbim[:], gmask[:, g:g + 1], None, ALU.mult, reads=[bbim, gmask], writes=[W1])
            P.ts("vector", R(W2[:, g, 0:64]), bbim[:], gmask[:, g:g + 1], None, ALU.mult, reads=[bbim, gmask], writes=[W2])
            P.ts("vector", R(W2[:, g, 64:128]), bbre[:], ngm[:, g:g + 1], None, ALU.mult, reads=[bbre, ngm], writes=[W2])
        wc1 = sb("wc1", [128, 128]); wc2 = sb("wc2", [128, 128])
        P.ts("vector", wc1[:], cmain[:], sign1[:, 0:1], None, ALU.mult, reads=[cmain, sign1], writes=[wc1])
        P.ts("vector", wc2[:], cswap[:], -1.0, None, ALU.mult, reads=[cswap], writes=[wc2])
        Wc1g = sbt("Wc1g", [128, 8, 128]); Wc2g = sbt("Wc2g", [128, 8, 128])
        P.memset("vector", Wc1g[:], 0.0, writes=[Wc1g]); P.memset("vector", Wc2g[:], 0.0, writes=[Wc2g])
        P.copy("vector", R(Wc1g[:]), Wc1g[:], reads=[Wc1g], writes=[Wc1g]); P.copy("vector", R(Wc2g[:]), Wc2g[:], reads=[Wc2g], writes=[Wc2g])
        for g in range(8):
            P.copy("vector", R(Wc1g[:, g, g * 16:(g + 1) * 16]), wc1[:, g * 16:(g + 1) * 16], reads=[wc1], writes=[Wc1g])
            P.copy("vector", R(Wc2g[:, g, g * 16:(g + 1) * 16]), wc2[:, g * 16:(g + 1) * 16], reads=[wc2], writes=[Wc2g])
        h0 = sb("s5h0", [128, 8])
        P.memset("vector", h0[:], 0.0, writes=[h0])
        ubuf = sbt("s5u", [128, TT])
        s5x = [(sbt if i < 2 else sb)(f"s5x{i}", [128, S5SUB]) for i in range(4)]
        s5y = sb("s5y", [128, S5SUB]); s5y2 = sb("s5y2", [128, S5SUB]); s5o = sb("s5o", [128, S5SUB])

    if do_mla:
        Kt = [[P.sbuf(f"Kt{h}_{i}", [96, TT], BF16) for i in range(ntile)] for h in range(2)]
        Vaug = [P.sbuf(f"Va{i}", [128, 2, 65], BF16) for i in range(ntile * QB)]
        for va in Vaug:
            P.memset("gpsimd", va[:, :, 64:65], 1.0, writes=[va])
        Qt = [sb(f"Qt{h}", [96, TT], BF16) for h in range(2)]
        cq = sb("cq", [128, 2, TT]); sq = sb("sq", [128, 2, TT]); rstd = sb("rstd", [128, TT])
        cqn = sb("cqn", [128, 2, TT], BF16)
        ckv = sb("ckv", [128, TT]); sqk = sb("sqk", [128, TT]); rstdk = sb("rstdk", [128, TT]); ckvn = sb("ckvn", [128, TT], BF16)
        kr1 = dtmp; kr2 = sb("kr2", [128, TT])
        qr1 = dtmp; qr2 = kr2
        Pt = [sb(f"Pt{i}", [128, TT], BF16) for i in range(3)]
        osb = sb("osb", [128, QB, 64]); orec = sb("orec", [128, QB, 1])
        pctr = [0]

    if do_rwkv:
        f2 = lambda n: sb(n, [128, SUB])
        LW = f2("LW"); AS = f2("AS"); GF = f2("GF"); KK = f2("KK"); D2 = f2("D2"); D3 = f2("D3"); KM = f2("KM"); BV = f2("BV")
        CU = f2("CU"); E1 = f2("E1"); E2 = f2("E2"); RT = f2("RT"); KT_ = f2("KT_"); AT = f2("AT"); BT = f2("BT")
        KHF = f2("KHF"); BHF = f2("BHF"); RK = f2("RK")
        gC = sb("gC", [128, 4])
        BD = [[sbt(f"BD{q}_{r}", [128, 2, 64]) for r in range(2)] for q in range(9)]
        AM = [sbt(f"AM{r}", [128, 640]) for r in range(2)]
        TM = [sbt(f"TM{r}", [128, 640]) for r in range(2)]
        XS = [[sbt(f"XS{a}_{r}", [128, 256]) for r in range(3)] for a in range(2)]
        PS_ = [[sbt(f"PS{a}_{r}", [128, 256]) for r in range(3)] for a in range(2)]
        RH = [sbt(f"RH{r}", [128, 128]) for r in range(2)]
        GT = [sbt(f"GT{r}", [128, 128]) for r in range(2)]
        YH = [sbt(f"YH{r}", [128, 256]) for r in range(5)]
        P.memset("vector", YH[4][:], 0.0, writes=[YH[4]])
        P.copy("vector", R(YH[4][:]), YH[4][:], reads=[YH[4]], writes=[YH[4]])
        Vc = sb("Vc", [128, 4, 64]); Gc = sb("Gc", [128, 4, 64]); Yc = sb("Yc", [128, 4, 64]); Ysq = sb("Ysq", [128, 4, 64])
        coef = sb("coef", [128, 4])
        st1 = sb("st1", [128, 4]); st2 = sb("st2", [128, 4]); st3 = sb("st3", [128, 4])
        yo = sb("yo", [128, 4, 64])
        hstate = [4]

    for it in range(ntile):
        c0 = it * TT
        if ctx is None or ctx.get("xload") is None:
            P.dma("gpsimd", xbf[:], xT_d.ap.rearrange("(k p) t -> p k t", p=128)[:, :, c0:c0 + TT], writes=[xbf])
        else:
            ctx["xload"](xbf, c0)
        if do_mla:
            P.dma("sync", cct[64:96, :], rope_d[0, :, c0:c0 + TT], writes=[cct])
            P.dma("sync", sst[64:96, :], rope_d[1, :, c0:c0 + TT], writes=[sst])
        for m in range(11):
            if m < 5 and not do_rwkv:
                continue
            if m in (5, 6, 7, 8, 10) and not do_mla:
                continue
            if m == 9 and not do_s5:
                continue
            pb = gen_ps()
            for k in range(8):
                P.mm(pb[:, 0:TT], wmix[:, k, m * 128:(m + 1) * 128], xbf[:, k, :], start=(k == 0), stop=(k == 7), reads=[wmix, xbf], writes=[pb])
            if m < 5:
                pr = praw[m]
                P.act(pr[:, 1:TT + 1], pb[:, 0:TT], AF.Copy, reads=[pb], writes=[pr])
                P.tt("vector", dtmp[:], pr[:, 0:TT], pr[:, 1:TT + 1], ALU.subtract, reads=[pr], writes=[dtmp])
                P.copy("vector", pr[:, 0:1], pr[:, TT:TT + 1], reads=[pr], writes=[pr])
                P.stt("vector", pr[:, 1:TT + 1], dtmp[:], V(m), pr[:, 1:TT + 1], ALU.mult, ALU.add, reads=[dtmp, vecs, pr], writes=[pr])
            elif m in (5, 6):
                j = m - 5
                P.act(cq[:, j, :], pb[:, 0:TT], AF.Copy, reads=[pb], writes=[cq])
                P.act(sq[:, j, :], pb[:, 0:TT], AF.Square, reads=[pb], writes=[sq])
            elif m == 7:
                P.act(ckv[:], pb[:, 0:TT], AF.Copy, reads=[pb], writes=[ckv])
                P.act(sqk[:], pb[:, 0:TT], AF.Square, reads=[pb], writes=[sqk])
            elif m == 8:
                P.tt("vector", kr1[64:96, :], pb[64:96, 0:TT], cct[64:96, :], ALU.mult, reads=[pb, cct], writes=[kr1])
            elif m == 10:
                P.tt("vector", kr2[64:96, :], pb[64:96, 0:TT], sst[64:96, :], ALU.mult, reads=[pb, sst], writes=[kr2])
                P.tt("vector", Kt[0][it][64:96, :], kr1[64:96, :], kr2[64:96, :], ALU.add, reads=[kr1, kr2], writes=[Kt[0][it]])
                P.tt("vector", Kt[1][it][64:96, :], kr1[64:96, :], kr2[64:96, :], ALU.add, reads=[kr1, kr2], writes=[Kt[1][it]])
            elif m == 9:
                P.act(R(ubuf[:]), pb[:, 0:TT], AF.Copy, reads=[pb], writes=[ubuf])

        def _g_rwkv():
            yield
            if do_rwkv:
                for s in range(TT // SUB):
                    o = 1 + s * SUB
                    r_ = praw[0][:, o:o + SUB]; k_ = praw[1][:, o:o + SUB]; v_ = praw[2][:, o:o + SUB]
                    dwa = praw[3]; dg = praw[4]
                    P.act(dwa[0:64, o:o + SUB], dwa[0:64, o:o + SUB], AF.Tanh, reads=[dwa], writes=[dwa])
                    pb = gen_ps()
                    P.mm(pb[:, 0:SUB], w2a2[0:64, :], dwa[0:64, o:o + SUB], reads=[w2a2, dwa], writes=[pb])
                    P.act(LW[:], pb[:, 0:SUB], AF.Sigmoid, bias=V(5), reads=[pb, vecs], writes=[LW])
                    P.ts("vector", LW[:], LW[:], NEG_EXPM05, None, ALU.mult, reads=[LW], writes=[LW])
                    pb = gen_ps()
                    P.mm(pb[:, 0:SUB], w2a2[64:128, :], dwa[64:128, o:o + SUB], reads=[w2a2, dwa], writes=[pb])
                    P.act(AS[:], pb[:, 0:SUB], AF.Sigmoid, bias=V(6), reads=[pb, vecs], writes=[AS])
                    P.act(dg[:, o:o + SUB], dg[:, o:o + SUB], AF.Sigmoid, reads=[dg], writes=[dg])
                    pb = gen_ps()
                    P.mm(pb[:, 0:SUB], g2[:, :], dg[:, o:o + SUB], reads=[g2, dg], writes=[pb])
                    P.act(GF[:], pb[:, 0:SUB], AF.Copy, reads=[pb], writes=[GF])
                    yield
                    P.ts("vector", KK[:], k_, V(7), None, ALU.mult, reads=[praw[1], vecs], writes=[KK])
                    P.tt("vector", D2[:], KK[:], KK[:], ALU.mult, reads=[KK], writes=[D2])
                    pb = gen_ps()
                    P.mm(pb[:, 0:SUB], blk[:, :], D2[:], reads=[blk, D2], writes=[pb])
                    P.ts("vector", D2[:], pb[:, 0:SUB], 1e-12, None, ALU.max, reads=[pb], writes=[D2])
                    P.act(D2[:], D2[:], AF.Sqrt, reads=[D2], writes=[D2])
                    P.op("vector", lambda e: e.reciprocal(D2[:], D2[:]), reads=[D2], writes=[D2])
                    P.tt("vector", KK[:], KK[:], D2[:], ALU.mult, reads=[KK, D2], writes=[KK])
                    P.ts("vector", D2[:], AS[:], 1.0, V(8), ALU.subtract, ALU.mult, reads=[AS, vecs], writes=[D2])
                    P.stt("vector", KM[:], D2[:], 1.0, k_, ALU.add, ALU.mult, reads=[D2, praw[1]], writes=[KM])
                    P.tt("vector", BV[:], KK[:], AS[:], ALU.mult, reads=[KK, AS], writes=[BV])
                    P.stt("vector", RK[:], r_, V(9), KM[:], ALU.mult, ALU.mult, reads=[praw[0], vecs, KM], writes=[RK])
                    yield
                    P.op("vector", lambda e: e.tensor_tensor_scan(CU[:], rmask[:], LW[:], 0.0, ALU.mult, ALU.add), reads=[rmask, LW], writes=[CU])
                    cu3 = CU[:].rearrange("p (c t) -> p c t", t=CH)
                    P.act(gC[:], CU[:].rearrange("p (c t) -> p c t", t=CH)[:, :, CH - 1], AF.Exp, reads=[CU], writes=[gC])
                    P.act(E1[:], CU[:], AF.Exp, reads=[CU], writes=[E1])
                    P.tt("vector", RT[:], r_, E1[:], ALU.mult, reads=[praw[0], E1], writes=[RT])
                    P.act(E2[:], CU[:], AF.Exp, scale=-1.0, reads=[CU], writes=[E2])
                    P.tt("vector", KT_[:], KM[:], E2[:], ALU.mult, reads=[KM, E2], writes=[KT_])
                    P.tt("vector", BT[:], BV[:], E2[:], ALU.mult, reads=[BV, E2], writes=[BT])
                    P.tt("vector", D3[:], CU[:], LW[:], ALU.subtract, reads=[CU, LW], writes=[D3])
                    P.act(E1[:], D3[:], AF.Exp, reads=[D3], writes=[E1])
                    P.stt("vector", AT[:], KK[:], -1.0, E1[:], ALU.mult, ALU.mult, reads=[KK, E1], writes=[AT])
                    P.tt("vector", D3[:].rearrange("p (c t) -> p c t", t=CH), cu3[:, :, CH - 1:CH].to_broadcast([128, SUB // CH, CH]), cu3, ALU.subtract, reads=[CU], writes=[D3])
                    P.act(E2[:], D3[:], AF.Exp, reads=[D3], writes=[E2])
                    P.tt("vector", KHF[:], KM[:], E2[:], ALU.mult, reads=[KM, E2], writes=[KHF])
                    P.tt("vector", BHF[:], BV[:], E2[:], ALU.mult, reads=[BV, E2], writes=[BHF])
                    if RSTOP < 2:
                        continue
                    srcs = [(RT, None), (KT_, None), (AT, None), (BT, None), (KHF, None), (BHF, None), (praw[2], v_), (GF, None), (RK, None)]
                    f = lambda t: t[:].rearrange("p a b -> p (a b)")
                    LS = 2
                    for cp in range(0, SUB // CH, LS):
                        chunks = list(range(cp, cp + LS))
                        S = {c: {} for c in chunks}
                        for c in chunks:
                            st = S[c]; rot = c % LS
                            cs = slice(c * CH, (c + 1) * CH)
                            bd = []
                            for q, (tl, view) in enumerate(srcs):
                                src = (view if view is not None else tl[:])[:, cs]
                                dst = BD[q][rot]
                                P.tt("gpsimd", R(dst[:]), src.unsqueeze(1).to_broadcast([128, 2, CH]), bdm[:, 0:2].unsqueeze(2).to_broadcast([128, 2, CH]),
                                     ALU.mult, reads=[tl, bdm], writes=[dst])
                                bd.append(dst)
                            st["bd"] = bd
                            Rb, Kb, Ab, Bb, KHb, BHb, Vb, Gb, RKb = bd
                            am = AM[rot]; tm = TM[rot]
                            st["am"] = am; st["tm"] = tm
                            pb = gen_ps()
                            P.mm(pb[:, 0:128], R(f(Bb)), R(f(Ab)), reads=[Bb, Ab], writes=[pb])
                            P.mm(pb[:, 128:256], R(f(Ab)), R(f(Bb)), reads=[Bb, Ab], writes=[pb])
                            P.mm(pb[:, 256:384], R(f(Kb)), R(f(Ab)), reads=[Kb, Ab], writes=[pb])
                            P.mm(pb[:, 384:512], R(f(Bb)), R(f(Rb)), reads=[Bb, Rb], writes=[pb])
                            P.tt("vector", R(am[:, 0:512]), pb[:, 0:512], maskb[:, 0:512], ALU.mult, reads=[pb, maskb], writes=[am])
                            yield
                        for c in chunks:
                            st = S[c]; am = st["am"]; tm = st["tm"]
                            Rb, Kb, Ab, Bb, KHb, BHb, Vb, Gb, RKb = st["bd"]
                            pb = gen_ps()
                            P.mm(pb[:, 0:128], R(f(Kb)), R(f(Rb)), reads=[Kb, Rb], writes=[pb])
                            P.tr(pb[:, 128:256], R(f(KHb)), R(identr[:]), reads=[KHb, ident], writes=[pb])
                            P.tr(pb[:, 256:384], R(f(BHb)), R(identr[:]), reads=[BHb, ident], writes=[pb])
                            P.tr(pb[:, 384:512], R(f(Vb)), R(identr[:]), reads=[Vb, ident], writes=[pb])
                            P.tt("vector", R(am[:, 512:640]), pb[:, 0:128], maskb[:, 512:640], ALU.mult, reads=[pb, maskb], writes=[am])
                            P.copy("vector", R(tm[:, 0:384]), pb[:, 128:512], reads=[pb], writes=[tm])
                            yield
                        for c in chunks:
                            st = S[c]; am = st["am"]; tm = st["tm"]; rot = c % LS
                            Rb, Kb, Ab, Bb, KHb, BHb, Vb, Gb, RKb = st["bd"]
                            AakT = am[:, 256:384]; Vt = tm[:, 256:384]
                            pb = gen_ps()
                            P.tr(pb[:, 0:128], R(f(Ab)), R(identr[:]), reads=[Ab, ident], writes=[pb])
                            P.tr(pb[:, 256:384], R(f(Gb)), R(identr[:]), reads=[Gb, ident], writes=[pb])
                            P.mm(pb[:, 128:256], R(AakT), R(Vt), reads=[am, tm], writes=[pb])
                            xs = XS[rot][0]
                            P.act(R(xs[:]), pb[:, 0:256], AF.Copy, reads=[pb], writes=[xs])
                            P.act(R(tm[:, 512:640]), pb[:, 256:384], AF.Copy, reads=[pb], writes=[tm])
                            pbc = gen_ps()
                            P.mm(pbc[:, 0:1], f(RKb), ones[:, 0:1], reads=[RKb, ones], writes=[pbc])
                            P.act(coef[:, c:c + 1], pbc[:, 0:1], AF.Copy, reads=[pbc], writes=[coef])
                            P.tt("gpsimd", Vc[:, c, :], tm[:, 256:320], tm[:, 320:384], ALU.add, reads=[tm], writes=[Vc])
                            P.tt("gpsimd", Gc[:, c, :], tm[:, 512:576], tm[:, 576:640], ALU.add, reads=[tm], writes=[Gc])
                            st["pn"] = am[:, 128:256]; st["pt"] = am[:, 0:128]; st["pT"] = am; st["xi"] = 0
                            yield
                        for i in range(6):
                            for c in chunks:
                                st = S[c]; rot = c % LS
                                pb = gen_ps()
                                xcur = XS[rot][st["xi"] % 3]; xnew = XS[rot][(st["xi"] + 1) % 3]
                                P.mm(pb[:, 0:256], R(st["pt"]), R(xcur[:]), reads=[st["pT"], xcur], writes=[pb])
                                if i < 5:
                                    P.mm(pb[:, 256:384], R(st["pt"]), R(st["pn"]), reads=[st["pT"]], writes=[pb])
                                    P.mm(pb[:, 384:512], R(st["pn"]), R(st["pt"]), reads=[st["pT"]], writes=[pb])
                                P.tt("vector", R(xnew[:]), xcur[:], pb[:, 0:256], ALU.add, reads=[xcur, pb], writes=[xnew])
                                if i < 5:
                                    pnew = PS_[rot][i % 3]
                                    P.copy("vector", R(pnew[:]), pb[:, 256:512], reads=[pb], writes=[pnew])
                                    st["pn"], st["pt"], st["pT"] = pnew[:, 0:128], pnew[:, 128:256], pnew
                                st["xi"] += 1
                                yield
                        for c in chunks:
                            st = S[c]; am = st["am"]; tm = st["tm"]; rot = c % LS
                            Rb = st["bd"][0]
                            X = XS[rot][st["xi"] % 3]; st["X"] = X
                            Ah = X[:, 0:128]
                            ArbT = am[:, 384:512]; Bh = tm[:, 128:256]
                            pb = gen_ps()
                            P.mm(pb[:, 0:128], R(Ah), R(ArbT), reads=[X, am], writes=[pb])
                            P.mm(pb[:, 128:256], R(Ah), R(Bh), reads=[X, tm], writes=[pb])
                            rh = RH[rot]; gt = GT[rot]
                            P.tt("vector", R(rh[:]), pb[:, 0:128], f(Rb), ALU.add, reads=[pb, Rb], writes=[rh])
                            P.stt("vector", R(gt[:]), ident[:], gC[:, c:c + 1], pb[:, 128:256], ALU.mult, ALU.add, reads=[ident, gC, pb], writes=[gt])
                            yield
                        for c in chunks:
                            st = S[c]; am = st["am"]; tm = st["tm"]; rot = c % LS
                            X = st["X"]; U0 = X[:, 128:256]
                            ArbT = am[:, 384:512]; ArkT = am[:, 512:640]
                            Kh = tm[:, 0:128]; Bh = tm[:, 128:256]; Vt = tm[:, 256:384]
                            rh = RH[rot]; gt = GT[rot]
                            hcur = YH[hstate[0]]
                            hn_i = (hstate[0] + 1) % 5
                            yh = YH[hn_i]
                            pb = gen_ps()
                            P.mm(pb[:, 0:128], R(ArbT), R(U0), start=True, stop=False, reads=[am, X], writes=[pb])
                            P.mm(pb[:, 0:128], R(ArkT), R(Vt), start=False, stop=False, reads=[am, tm], writes=[pb])
                            P.mm(pb[:, 0:128], R(rh[:]), R(hcur[:, 128:256]), start=False, stop=True, reads=[rh, hcur], writes=[pb])
                            P.mm(pb[:, 128:256], R(Bh), R(U0), start=True, stop=False, reads=[tm, X], writes=[pb])
                            P.mm(pb[:, 128:256], R(Kh), R(Vt), start=False, stop=False, reads=[tm], writes=[pb])
                            P.mm(pb[:, 128:256], R(gt[:]), R(hcur[:, 128:256]), start=False, stop=True, reads=[gt, hcur], writes=[pb])
                            P.act(R(yh[:]), pb[:, 0:256], AF.Copy, reads=[pb], writes=[yh])
                            hstate[0] = hn_i
                            P.tt("gpsimd", Yc[:, c, :], yh[:, 0:64], yh[:, 64:128], ALU.add, reads=[yh], writes=[Yc])
                            yield
                    if RSTOP < 3:
                        continue
                    yield
                    nchk = SUB // CH
                    P.op("vector", lambda e: e.tensor_reduce(st1[:], Yc[:], AX.X, ALU.add), reads=[Yc], writes=[st1])
                    P.tt("gpsimd", Ysq[:], Yc[:], Yc[:], ALU.mult, reads=[Yc], writes=[Ysq])
                    P.op("vector", lambda e: e.tensor_reduce(st2[:], Ysq[:], AX.X, ALU.add), reads=[Ysq], writes=[st2])
                    P.ts("vector", st1[:], st1[:], 1.0 / 64, None, ALU.mult, reads=[st1], writes=[st1])
                    P.tt("vector", st3[:], st1[:], st1[:], ALU.mult, reads=[st1], writes=[st3])
                    P.stt("vector", st2[:], st2[:], 1.0 / 64, st3[:], ALU.mult, ALU.subtract, reads=[st2, st3], writes=[st2])
                    P.act(st2[:], st2[:], AF.Sqrt, bias=64e-5, reads=[st2], writes=[st2])
                    P.op("vector", lambda e: e.reciprocal(st2[:], st2[:]), reads=[st2], writes=[st2])
                    bc = lambda t: t[:, :].unsqueeze(2).to_broadcast([128, nchk, 64])
                    bg = lambda t: t[:, :].unsqueeze(1).to_broadcast([128, nchk, 64])
                    P.tt("vector", yo[:], Yc[:], bc(st1), ALU.subtract, reads=[Yc, st1], writes=[yo])
                    P.tt("vector", yo[:], yo[:], bc(st2), ALU.mult, reads=[yo, st2], writes=[yo])
                    P.tt("vector", yo[:], yo[:], bg(gng), ALU.mult, reads=[yo, gng], writes=[yo])
                    P.tt("vector", yo[:], yo[:], bg(gnb), ALU.add, reads=[yo, gnb], writes=[yo])
                    P.tt("vector", Ysq[:], Vc[:], bc(coef), ALU.mult, reads=[Vc, coef], writes=[Ysq])
                    P.tt("vector", yo[:], yo[:], Ysq[:], ALU.add, reads=[yo, Ysq], writes=[yo])
                    P.tt("vector", yo[:], yo[:], Gc[:], ALU.mult, reads=[yo, Gc], writes=[yo])
                    t0 = c0 + s * SUB
                    for h in range(2 if RSTOP >= 4 else 0):
                        P.dma("sync", oap("orw", t0, SUB)[:, h * 64:(h + 1) * 64].rearrange("(c t) v -> t c v", t=CH),
                              yo[h * 64:(h + 1) * 64, :, :], reads=[yo], writes=[orw_d])


        def _g_s5():
            yield
            if do_s5:
                for s in range(TT // S5SUB):
                    us = ubuf[:, s * S5SUB:(s + 1) * S5SUB]
                    pby = ps[4]
                    for g in range(8):
                        pb = gen_ps()
                        P.mm(pb[:, 0:S5SUB], R(W1[:, g, :]), R(us), reads=[W1, ubuf], writes=[pb])
                        P.mm(pb[:, S5SUB:2 * S5SUB], R(W2[:, g, :]), R(us), reads=[W2, ubuf], writes=[pb])
                        x1, x2, xt, ht = s5x
                        P.tt("vector", R(x1[:]), pb[:, 0:S5SUB], ctab[:, g, :], ALU.mult, reads=[pb, ctab], writes=[x1])
                        P.tt("vector", R(x2[:]), pb[:, S5SUB:2 * S5SUB], stab[:, g, :], ALU.mult, reads=[pb, stab], writes=[x2])
                        P.tt("gpsimd", xt[:], x1[:], x2[:], ALU.add, reads=[x1, x2], writes=[xt])
                        P.op("vector", lambda e, g=g, xt=xt, ht=ht: e.tensor_tensor_scan(ht[:], mag_sp[:, g:g + 1].to_broadcast([128, S5SUB]), xt[:], h0[:, g:g + 1], ALU.mult, ALU.add),
                             reads=[mag_sp, xt, h0], writes=[ht])
                        P.tt("gpsimd", R(x1[:]), ht[:], ctab[:, g, :], ALU.mult, reads=[ht, ctab], writes=[x1])
                        P.tt("vector", R(x2[:]), ht[:], stab[:, g, :], ALU.mult, reads=[ht, stab], writes=[x2])
                        P.mm(pby[:, 0:S5SUB], R(Wc1g[:, g, :]), R(x1[:]), start=(g == 0), stop=False, reads=[Wc1g, x1], writes=[pby])
                        P.mm(pby[:, 0:S5SUB], R(Wc2g[:, g, :]), R(x2[:]), start=False, stop=(g == 7), reads=[Wc2g, x2], writes=[pby] )
                        pbh = gen_ps()
                        P.mm(pbh[:, 0:1], ident[:], x1[:, S5SUB - 1:S5SUB], start=True, stop=False, reads=[ident, x1], writes=[pbh])
                        P.mm(pbh[:, 0:1], jm[:], x2[:, S5SUB - 1:S5SUB], start=False, stop=True, reads=[jm, x2], writes=[pbh])
                        P.act(h0[:, g:g + 1], pbh[:, 0:1], AF.Copy, reads=[pbh], writes=[h0])
                        yield
                    P.stt("vector", s5y[:], us, V(13), pby[:, 0:S5SUB], ALU.mult, ALU.add, reads=[ubuf, vecs, pby], writes=[s5y])
                    P.act(s5o[:], s5y[:], AF.Gelu_apprx_tanh, reads=[s5y], writes=[s5o])
                    t0 = c0 + s * S5SUB
                    P.dma("sync", oap("os5", t0, S5SUB), s5o[:], reads=[s5o], writes=[os5_d])


        def _g_mla():
            yield
            if do_mla:
                pb = gen_ps()
                P.mm(pb[:, 0:TT], ones[:, :], sq[:, 0, :], start=True, stop=False, reads=[ones, sq], writes=[pb])
                P.mm(pb[:, 0:TT], ones[:, :], sq[:, 1, :], start=False, stop=True, reads=[ones, sq], writes=[pb])
                P.act(rstd[:], pb[:, 0:TT], AF.Sqrt, scale=1.0 / 256, bias=1e-6, reads=[pb], writes=[rstd])
                P.op("vector", lambda e: e.reciprocal(rstd[:], rstd[:]), reads=[rstd], writes=[rstd])
                for j in range(2):
                    P.stt("vector", cqn[:, j, :], cq[:, j, :], V(10 + j), rstd[:], ALU.mult, ALU.mult, reads=[cq, vecs, rstd], writes=[cqn])
                pb = gen_ps()
                P.mm(pb[:, 0:TT], ones[:, :], sqk[:], reads=[ones, sqk], writes=[pb])
                P.act(rstdk[:], pb[:, 0:TT], AF.Sqrt, scale=1.0 / 128, bias=1e-6, reads=[pb], writes=[rstdk])
                P.op("vector", lambda e: e.reciprocal(rstdk[:], rstdk[:]), reads=[rstdk], writes=[rstdk])
                P.stt("vector", ckvn[:], ckv[:], V(12), rstdk[:], ALU.mult, ALU.mult, reads=[ckv, vecs, rstdk], writes=[ckvn])
                yield
                for h in range(2):
                    pb = gen_ps()
                    P.mm(pb[0:64, 0:TT], kvupk[:, h * 64:(h + 1) * 64], ckvn[:], reads=[kvupk, ckvn], writes=[pb])
                    P.act(Kt[h][it][0:64, :], pb[0:64, 0:TT], AF.Copy, reads=[pb], writes=[Kt[h][it]])
                pb = gen_ps()
                for kb in range(QB):
                    P.mm(pb[:, kb * 128:(kb + 1) * 128], ckvn[:, kb * 128:(kb + 1) * 128], kvupv[:, :], reads=[ckvn, kvupv], writes=[pb])
                for kb in range(QB):
                    va = Vaug[it * QB + kb]
                    P.copy("vector", va[:, :, 0:64], pb[:, kb * 128:(kb + 1) * 128].rearrange("p (h v) -> p h v", h=2), reads=[pb], writes=[va])
                yield
                for h in range(2):
                    pb = gen_ps()
                    for j in range(2):
                        P.mm(pb[0:96, 0:TT], qup[:, j, h * 192:h * 192 + 96], cqn[:, j, :], start=(j == 0), stop=(j == 1), reads=[qup, cqn], writes=[pb])
                    pb2 = gen_ps()
                    for j in range(2):
                        P.mm(pb2[0:96, 0:TT], qup[:, j, h * 192 + 96:h * 192 + 192], cqn[:, j, :], start=(j == 0), stop=(j == 1), reads=[qup, cqn], writes=[pb2])
                    P.ts("vector", Qt[h][0:64, :], pb[0:64, 0:TT], ATTN_SCALE, None, ALU.mult, reads=[pb], writes=[Qt[h]])
                    P.stt("vector", qr1[64:96, :], pb[64:96, 0:TT], ATTN_SCALE, cct[64:96, :], ALU.mult, ALU.mult, reads=[pb, cct], writes=[qr1])
                    P.stt("vector", qr2[64:96, :], pb2[64:96, 0:TT], ATTN_SCALE, sst[64:96, :], ALU.mult, ALU.mult, reads=[pb2, sst], writes=[qr2])
                    P.tt("vector", Qt[h][64:96, :], qr1[64:96, :], qr2[64:96, :], ALU.add, reads=[qr1, qr2], writes=[Qt[h]])
                yield
                for h in range(2):
                    nkb = QB * it + QB
                    for kb in range(nkb):
                        kt = Kt[h][kb // QB]
                        kcol = (kb % QB) * 128
                        sbk = ps[2 + kb % 2]
                        diag = kb >= QB * it
                        qb0 = kb - QB * it if diag else 0
                        q0 = qb0 * 128
                        P.mm(sbk[:, q0:TT], kt[:, kcol:kcol + 128], Qt[h][:, q0:TT], reads=[kt, Qt[h]], writes=[sbk])
                        pt = Pt[pctr[0] % 3]; pctr[0] += 1
                        P.act(pt[:, q0:TT], sbk[:, q0:TT], AF.Exp, reads=[sbk], writes=[pt])
                        if diag:
                            P.tt("vector", pt[:, q0:q0 + 128], pt[:, q0:q0 + 128], tri[:, :], ALU.mult, reads=[pt, tri], writes=[pt])
                        va = Vaug[kb]
                        yield
                        for qb in range(qb0, QB):
                            last = (kb == QB * it + qb)
                            ob = ps[qb]
                            P.mm(ob[:, 0:65], pt[:, qb * 128:(qb + 1) * 128], va[:, h, :], start=(kb == 0), stop=last,
                                 reads=[pt, va], writes=[ob])
                    for qb in range(QB):
                        ob = ps[qb]
                        P.op("vector", lambda e, ob=ob, qb=qb: e.reciprocal(orec[:, qb, :], ob[:, 64:65]), reads=[ob], writes=[orec])
                        P.tt("vector", osb[:, qb, :], ob[:, 0:64], orec[:, qb, 0:1].to_broadcast([128, 64]), ALU.mult, reads=[ob, orec], writes=[osb])
                    P.dma("sync", oap("omla", c0, TT)[:, h * 64:(h + 1) * 64].rearrange("(q p) v -> p q v", p=128), osb[:], reads=[osb], writes=[omla_d])


        _gens = [_g_rwkv(), _g_s5(), _g_mla()]
        while _gens:
            for _g in list(_gens):
                try:
                    next(_g)
                except StopIteration:
                    _gens.remove(_g)

    if ctx is not None:
        return
    P.wait_tiles("sync", [orw_d, omla_d, os5_d])
    P.wait_all_dma("sync")
    P.emit()
    return nc, P


def consts():
    su = np.triu(np.ones((64, 64), np.float32), 1)
    uu = np.triu(np.ones((64, 64), np.float32), 0)
    z = np.zeros((64, 64), np.float32)
    bd = lambda m: np.block([[m, z], [z, m]])
    maskb = np.concatenate([bd(su), bd(su.T), bd(su), bd(uu), bd(uu)], axis=1)
    bdm = np.zeros((128, 2), np.float32); bdm[:64, 0] = 1; bdm[64:, 1] = 1
    blk = bd(np.ones((64, 64), np.float32))
    tri = np.triu(np.ones((128, 128), np.float32), 0)
    rmask = np.ones((128, SUB), np.float32); rmask[:, ::CH] = 0
    jm = np.zeros((128, 128), np.float32)
    for p in range(64):
        jm[64 + p, p] = -1.0
        jm[p, 64 + p] = 1.0
    gmask = np.zeros((128, 8), np.float32)
    for g in range(8):
        gmask[g * 16:(g + 1) * 16, g] = 1
    sign1 = np.ones((128, 1), np.float32); sign1[64:] = -1
    tidx = np.tile(np.arange(1, S5SUB + 1, dtype=np.float32)[None, :], (128, 1))
    pos = np.arange(8192, dtype=np.float32)
    inv_freq = (np.float32(10000.0) ** (-np.arange(0, 32, 2, dtype=np.float32) / np.float32(32))).astype(np.float32)
    ang = (pos[:, None] * inv_freq[None, :]).astype(np.float32)
    cos = np.cos(ang.astype(np.float64)).astype(np.float32).T
    sin = np.sin(ang.astype(np.float64)).astype(np.float32).T
    rope = np.stack([np.concatenate([cos, cos], 0), np.concatenate([-sin, sin], 0)], 0)
    return dict(ident=np.eye(128, dtype=np.float32), maskb=maskb, bdm=bdm, blkones=blk, tri=tri, rmask=rmask, jm=jm,
                gmask=gmask, sign1=sign1, tidx=tidx, rope=np.ascontiguousarray(rope))


def mixer_inputs(I, l, j, xT, C=None):
    C = C or consts()
    w_in = I['w_in'][l]
    hs = slice(j * 128, (j + 1) * 128)
    oM = 1792; oS = oM + 416
    z64 = np.zeros((1024, 64), np.float32); z32 = np.zeros((1024, 32), np.float32)
    kr = w_in[:, oM + 384:oM + 416]
    kr_sw = np.concatenate([kr[:, 16:32], kr[:, 0:16]], 1)
    cols = [w_in[:, 0:512][:, hs], w_in[:, 512:1024][:, hs], w_in[:, 1024:1536][:, hs],
            w_in[:, 1536:1664], w_in[:, 1664:1792],
            w_in[:, oM:oM + 256], w_in[:, oM + 256:oM + 384],
            np.concatenate([z64, kr, z32], 1),
            w_in[:, oS:oS + 512][:, hs],
            np.concatenate([z64, kr_sw, z32], 1)]
    wmix = np.concatenate(cols, 1)
    assert wmix.shape == (1024, 1408), wmix.shape
    mu = I['rwkv_mu'][l]
    vecs = np.zeros((128, 16), np.float32)
    vecs[:, 0] = mu[0:512][hs]; vecs[:, 1] = mu[512:1024][hs]; vecs[:, 2] = mu[1024:1536][hs]
    vecs[:, 3] = mu[1536:1664]; vecs[:, 4] = mu[1664:1792]
    vecs[:, 5] = I['rwkv_w0'][l][hs]; vecs[:, 6] = I['rwkv_a0'][l][hs]
    vecs[:, 7] = I['rwkv_k_k'][l][hs]; vecs[:, 8] = I['rwkv_k_a'][l][hs]
    vecs[:, 9] = I['rwkv_r_k'][l].reshape(512)[hs]
    vecs[:, 10] = I['mla_q_norm'][l][0:128]; vecs[:, 11] = I['mla_q_norm'][l][128:256]
    vecs[:, 12] = I['mla_kv_norm'][l]
    vecs[:, 13] = I['s5_d'][l][hs]
    w2a2 = np.concatenate([I['rwkv_w2'][l][:, hs], I['rwkv_a2'][l][:, hs]], 0)
    g2 = I['rwkv_g2'][l][:, hs]
    gng = np.repeat(I['rwkv_gn_g'][l][hs].reshape(2, 1, 64), 64, axis=1).reshape(128, 64)
    gnb = np.repeat(I['rwkv_gn_b'][l][hs].reshape(2, 1, 64), 64, axis=1).reshape(128, 64)
    qu = I['mla_q_up'][l].reshape(256, 8, 96)
    z = np.zeros((256, 64), np.float32)
    qparts = []
    for h in (2 * j, 2 * j + 1):
        main = qu[:, h, :]
        sw = np.concatenate([z, qu[:, h, 80:96], qu[:, h, 64:80]], 1)
        qparts += [main, sw]
    qup = np.concatenate(qparts, 1)
    kvu = I['mla_kv_up'][l].reshape(128, 8, 128)
    kvupk = np.concatenate([kvu[:, 2 * j, 0:64], kvu[:, 2 * j + 1, 0:64]], 1)
    kvupv = np.concatenate([kvu[:, 2 * j, 64:128], kvu[:, 2 * j + 1, 64:128]], 1)
    gs = slice(8 * j, 8 * j + 8)
    lre = I['s5_lambda_re'][l][gs]; lim = I['s5_lambda_im'][l][gs]; ls = I['s5_log_step'][l][gs]
    s5sp = np.concatenate([np.concatenate([lre.T, lre.T], 0), np.concatenate([lim.T, lim.T], 0), np.tile(ls[None, :], (128, 1))], 1)
    rep = lambda a: np.repeat(a, 16, axis=0)
    bre = I['s5_b_re'][l][gs].transpose(0, 2, 1).reshape(128, 64)
    bim = I['s5_b_im'][l][gs].transpose(0, 2, 1).reshape(128, 64)
    s5gc = np.concatenate([rep(lre), rep(lim), bre, bim, np.repeat(ls, 16)[:, None]], 1)
    cre = I['s5_c_re'][l][gs].reshape(128, 64).T
    cim = I['s5_c_im'][l][gs].reshape(128, 64).T
    cmain = np.concatenate([cre, cim], 0); cswap = np.concatenate([cim, cre], 0)
    d = dict(xT=xT, wmix=wmix, vecs=vecs, w2a2=w2a2, g2=g2, gng=gng, gnb=gnb, qup=qup, kvupk=kvupk, kvupv=kvupv,
             s5sp=s5sp, s5gc=s5gc, cmain=cmain, cswap=cswap)
    d.update(C)
    return {k: np.ascontiguousarray(v, dtype=np.float32) for k, v in d.items() if v is not None}


import math
import numpy as np

RT = 512
NROW = 2048
ALPHA = (2.0 * 2) ** 0.25
DFF = 2816


def build_row(ntile=NROW // RT, ctx=None):
    if ctx is None:
        nc = bass.Bass("TRN2", target_bir_lowering=False)
        P = Prog(nc)
        din = lambda n, s: P.dram(n, s, F32, kind="ExternalInput")
    else:
        P = ctx["P"]; din = ctx["din"]
    xT_d = din("xT", [1024, NROW])
    mo_d = [din("orwT", [512, NROW]), din("omlaT", [512, NROW]), din("os5T", [512, NROW])]
    wg_d = din("wgate", [1024, 3072])
    wo_d = [din("rwkv_out", [512, 1024]), din("mla_out", [512, 1024])]
    glu_d = din("s5_glu", [512, 2048])
    wout_d = din("w_out", [1024, 1024])
    w1_d = din("ffn_w1", [1024, DFF]); w3_d = din("ffn_w3", [1024, DFF]); w2_d = din("ffn_w2", [DFF, 1024])
    rv_d = din("rvecs", [128, 56])
    xo_d = P.dram("xout", [1024, NROW], F32, kind="ExternalOutput") if ctx is None else ctx["xout"]
    sb = P.sbuf
    rv = sb("rv", [128, 56]); P.dma("sync", rv[:], rv_d[:], writes=[rv])
    ones = sb("ones", [128, 128]); P.memset("vector", ones[:], 1.0, writes=[ones])
    xres = sb("xres", [128, 8, RT])
    xb = sb("xb", [128, 8, RT], BF16)
    mo = [sb(f"mo{i}", [128, 4, RT], BF16) for i in range(3)]
    merged = sb("merged", [128, 8, RT]); mb = sb("mb", [128, 8, RT], BF16)
    z = sb("z", [128, 8, RT])
    hb = sb("hb", [128, 22, RT], BF16)
    gt = sb("gt", [128, RT]); t1 = sb("t1", [128, RT]); t2 = sb("t2", [128, RT])
    mean = sb("mean", [128, RT]); rstd = sb("rstd", [128, RT])
    wbuf = [sb(f"wbuf{i}", [128, 4096], BF16) for i in range(3)]
    wctr = [0]
    ps = [P.psum(f"ps{i}", [128, 512]) for i in range(8)] if ctx is None else ctx["ps"]
    pctr = [0]

    def gen_ps():
        pctr[0] += 1
        return ps[pctr[0] % 8]

    def wload(d, r0, nk, c0, ncols):
        wb = wbuf[wctr[0] % 3]; wctr[0] += 1
        view = wb[:, 0:nk * ncols].rearrange("p (k c) -> p k c", k=nk)
        P.dma("gpsimd" if ctx is None else "sync", view, d.ap[r0:r0 + nk * 128, c0:c0 + ncols].rearrange("(k p) c -> p k c", p=128), reads=[d], writes=[wb])
        return wb, view

    def layer_norm(gcol, bcol, cs):
        pS = gen_ps(); pQ = gen_ps()
        for m in range(8):
            P.mm(pS[:, 0:RT], ones[:, :], z[:, m, :], start=(m == 0), stop=(m == 7), reads=[ones, z], writes=[pS])
        for m in range(8):
            P.act(t1[:], z[:, m, :], AF.Square, reads=[z], writes=[t1])
            P.mm(pQ[:, 0:RT], ones[:, :], t1[:], start=(m == 0), stop=(m == 7), reads=[ones, t1], writes=[pQ])
        P.ts("vector", mean[:], pS[:, 0:RT], 1.0 / 1024, None, ALU.mult, reads=[pS], writes=[mean])
        P.tt("vector", t2[:], mean[:], mean[:], ALU.mult, reads=[mean], writes=[t2])
        P.stt("vector", rstd[:], pQ[:, 0:RT], 1.0 / 1024, t2[:], ALU.mult, ALU.subtract, reads=[pQ, t2], writes=[rstd])
        P.act(rstd[:], rstd[:], AF.Sqrt, bias=1e-5, reads=[rstd], writes=[rstd])
        P.op("vector", lambda e: e.reciprocal(rstd[:], rstd[:]), reads=[rstd], writes=[rstd])
        for m in range(8):
            P.tt("vector", t2[:], z[:, m, :], mean[:], ALU.subtract, reads=[z, mean], writes=[t2])
            P.tt("gpsimd", t2[:], t2[:], rstd[:], ALU.mult, reads=[t2, rstd], writes=[t2])
            P.ts("vector", xres[:, m, :], t2[:], rv[:, gcol + m:gcol + m + 1], rv[:, bcol + m:bcol + m + 1], ALU.mult, ALU.add,
                 reads=[t2, rv], writes=[xres])

    for it in range(ntile):
        cs = slice(it * RT, (it + 1) * RT)
        P.dma("sync", xres[:], xT_d.ap.rearrange("(k p) t -> p k t", p=128)[:, :, cs], reads=[xT_d], writes=[xres])
        P.copy("vector", xb[:], xres[:], reads=[xres], writes=[xb])
        if ctx is None:
            for i in range(3):
                P.dma("gpsimd", mo[i][:], mo_d[i].ap[:, cs].rearrange("(k p) t -> p k t", p=128), writes=[mo[i]])
        else:
            ctx["moload"](mo, it, gen_ps)
        for br in range(3):
            for mc in range(2):
                wgt, wgv = wload(wg_d, 0, 8, br * 1024 + mc * 512, 512)
                if br < 2:
                    wyt, wyv = wload(wo_d[br], 0, 4, mc * 512, 512)
                else:
                    wyt, wyv = wload(glu_d, 0, 4, mc * 512, 512)
                    wy2t, wy2v = wload(glu_d, 0, 4, 1024 + mc * 512, 512)
                for mi in range(4):
                    m = mc * 4 + mi
                    pg = gen_ps()
                    for k in range(8):
                        P.mm(pg[:, 0:RT], wgv[:, k, mi * 128:(mi + 1) * 128], xb[:, k, :], start=(k == 0), stop=(k == 7), reads=[wgt, xb], writes=[pg])
                    P.act(gt[:], pg[:, 0:RT], AF.Sigmoid, bias=rv[:, br * 8 + m:br * 8 + m + 1], reads=[pg, rv], writes=[gt])
                    py = gen_ps()
                    for k in range(4):
                        P.mm(py[:, 0:RT], wyv[:, k, mi * 128:(mi + 1) * 128], mo[br][:, k, :], start=(k == 0), stop=(k == 3), reads=[wyt, mo[br]], writes=[py])
                    if br == 2:
                        py2 = gen_ps()
                        for k in range(4):
                            P.mm(py2[:, 0:RT], wy2v[:, k, mi * 128:(mi + 1) * 128], mo[br][:, k, :], start=(k == 0), stop=(k == 3), reads=[wy2t, mo[br]], writes=[py2])
                        P.act(t1[:], py2[:, 0:RT], AF.Sigmoid, reads=[py2], writes=[t1])
                        P.tt("vector", t1[:], py[:, 0:RT], t1[:], ALU.mult, reads=[py, t1], writes=[t1])
                        P.tt("vector", t1[:], t1[:], gt[:], ALU.mult, reads=[t1, gt], writes=[t1])
                        P.tt("gpsimd", merged[:, m, :], merged[:, m, :], t1[:], ALU.add, reads=[merged, t1], writes=[merged])
                    elif br == 0:
                        P.tt("vector", merged[:, m, :], py[:, 0:RT], gt[:], ALU.mult, reads=[py, gt], writes=[merged])
                    else:
                        P.tt("vector", t1[:], py[:, 0:RT], gt[:], ALU.mult, reads=[py, gt], writes=[t1])
                        P.tt("gpsimd", merged[:, m, :], merged[:, m, :], t1[:], ALU.add, reads=[merged, t1], writes=[merged])
        P.copy("vector", mb[:], merged[:], reads=[merged], writes=[mb])
        for mc in range(2):
            wt, wv = wload(wout_d, 0, 8, mc * 512, 512)
            for mi in range(4):
                m = mc * 4 + mi
                pz = gen_ps()
                for k in range(8):
                    P.mm(pz[:, 0:RT], wv[:, k, mi * 128:(mi + 1) * 128], mb[:, k, :], start=(k == 0), stop=(k == 7), reads=[wt, mb], writes=[pz])
                P.stt("vector", z[:, m, :], xres[:, m, :], ALPHA, pz[:, 0:RT], ALU.mult, ALU.add, reads=[xres, pz], writes=[z])
        layer_norm(24, 32, cs)
        P.copy("vector", xb[:], xres[:], reads=[xres], writes=[xb])
        for mc in range(6):
            ncols = 512 if mc < 5 else 256
            w1t, w1v = wload(w1_d, 0, 8, mc * 512, ncols)
            w3t, w3v = wload(w3_d, 0, 8, mc * 512, ncols)
            for mi in range(ncols // 128):
                m = mc * 4 + mi
                p1 = gen_ps(); p3 = gen_ps()
                for k in range(8):
                    P.mm(p1[:, 0:RT], w1v[:, k, mi * 128:(mi + 1) * 128], xb[:, k, :], start=(k == 0), stop=(k == 7), reads=[w1t, xb], writes=[p1])
                for k in range(8):
                    P.mm(p3[:, 0:RT], w3v[:, k, mi * 128:(mi + 1) * 128], xb[:, k, :], start=(k == 0), stop=(k == 7), reads=[w3t, xb], writes=[p3])
                P.act(t1[:], p1[:, 0:RT], AF.Silu, reads=[p1], writes=[t1])
                P.tt("vector", hb[:, m, :], p3[:, 0:RT], t1[:], ALU.mult, reads=[p3, t1], writes=[hb])
        for m in range(8):
            wt, wv = wload(w2_d, 0, 22, m * 128, 128)
            pz = gen_ps()
            for k in range(22):
                P.mm(pz[:, 0:RT], wv[:, k, :], hb[:, k, :], start=(k == 0), stop=(k == 21), reads=[wt, hb], writes=[pz])
            P.stt("vector", z[:, m, :], xres[:, m, :], ALPHA, pz[:, 0:RT], ALU.mult, ALU.add, reads=[xres, pz], writes=[z])
        layer_norm(40, 48, cs)
        P.dma("sync", xo_d.ap.rearrange("(k p) t -> p k t", p=128)[:, :, cs], xres[:], reads=[xres], writes=[xo_d])
        if ctx is not None and ctx.get("xbf_out") is not None:
            P.dma("gpsimd", ctx["xbf_out"].ap.rearrange("(k p) t -> p k t", p=128)[:, :, cs], xres[:], reads=[xres], writes=[ctx["xbf_out"]])
    if ctx is not None:
        return
    P.wait_tiles("sync", [xo_d])
    P.wait_all_dma("sync")
    P.emit()
    return nc, P


def row_inputs(I, l, xT_own, orwT, omlaT, os5T):
    rv = np.zeros((128, 56), np.float32)
    rv[:, 0:24] = I['gate_b'][l].reshape(24, 128).T
    rv[:, 24:32] = I['ln1_g'][l].reshape(8, 128).T
    rv[:, 32:40] = I['ln1_b'][l].reshape(8, 128).T
    rv[:, 40:48] = I['ln2_g'][l].reshape(8, 128).T
    rv[:, 48:56] = I['ln2_b'][l].reshape(8, 128).T
    d = dict(xT=xT_own, orwT=orwT, omlaT=omlaT, os5T=os5T, wgate=I['w_in'][l][:, 2720:5792],
             rwkv_out=I['rwkv_out'][l], mla_out=I['mla_out'][l], s5_glu=I['s5_glu'][l], w_out=I['w_out'][l],
             ffn_w1=I['ffn_w1'][l], ffn_w3=I['ffn_w3'][l], ffn_w2=I['ffn_w2'][l], rvecs=rv)
    return {k: np.ascontiguousarray(v, dtype=np.float32) for k, v in d.items() if v is not None}


import numpy as np

CONST_NAMES = ("ident", "maskb", "bdm", "blkones", "tri", "rmask", "jm", "gmask", "sign1", "tidx", "rope")
RG = [[0, 1, 2, 3], [4, 5, 6, 7]]


import os
FSKIP = os.environ.get('FSKIP', '')


def build_fused(nlayers=2):
    nc = bass.Bass("TRN2", target_bir_lowering=False)
    P = Prog(nc)
    P.use_arena()
    ext = {}
    qc = {}

    def getq(e):
        return e.partition_id() % 4

    def ext_in(name, shape):
        if name not in ext:
            ext[name] = P.dram(name, shape, F32, kind="ExternalInput")
        return ext[name]

    ps = [P.psum(f"ps{i}", [128, 512]) for i in range(8)]
    QW = 2048 * 128
    mixout = P.dram("mixout", [12 * 2048, 128], F32)
    gath = P.dram("gath", [12 * 4 * 2048, 128], F32)
    mine = P.dram("mine", [3 * 4 * 2048, 128], F32)
    xqbf = P.dram("xqbf", [1024, 2048], BF16)
    xg = P.dram("xg", [8 * 4 * 128, 2048], BF16)
    xres_d = P.dram("xres_d", [1024, 2048], F32)
    xout = P.dram("xout", [1024, 2048], F32, kind="ExternalOutput")

    ROW_W = [("wgate", [1024, 3072]), ("rwkv_out", [512, 1024]), ("mla_out", [512, 1024]), ("s5_glu", [512, 2048]),
             ("w_out", [1024, 1024]), ("ffn_w1", [1024, 2816]), ("ffn_w3", [1024, 2816]), ("ffn_w2", [2816, 1024])]
    wbf = {}
    for l in range(nlayers):
        for n, shp in ROW_W:
            src = ext_in(f"{n}_r{l}", shp)
            dst = P.dram(f"{n}_bf{l}", shp, BF16)
            for r0 in range(0, shp[0], 256):
                P.dma("gpsimd", dst[r0:r0 + 256, :], src[r0:r0 + 256, :], reads=[src], writes=[dst])
            wbf[(n, l)] = dst

    for l in range(nlayers):
        last = (l == nlayers - 1)
        P.arena_reset()
        mo_t = T("mixout_t", mixout.ap)
        outs = (mo_t, mo_t, mo_t)

        def oap(kind, t0, n):
            q = t0 // 2048; lt = t0 % 2048
            br = {"orw": 0, "omla": 1, "os5": 2}[kind]
            blk = mixout.ap[(q * 3 + br) * 2048:(q * 3 + br + 1) * 2048, :]
            if br < 2:
                return blk[lt:lt + n, :]
            return blk.rearrange("(f a) c -> f (a c)", f=128)[:, lt:lt + n]

        def din_m(n, s, l=l):
            if n in CONST_NAMES or n == "xT":
                return ext_in(n, s)
            return ext_in(f"{n}_m{l}", s)

        def xload(xbf, c0):
            r = c0 // 2048; col = c0 % 2048
            P.dma("sync", xbf[:], xg.ap.rearrange("(k r p) t -> r p k t", k=8, r=4)[r][:, :, col:col + TT], reads=[xg], writes=[xbf])

        build_mixer(ctx=dict(P=P, din=din_m, ps=ps, outs=outs, oap=oap, xload=(None if l == 0 else xload)))
        if 'ag' not in FSKIP:
            for i in range(12):
                P.op("gpsimd", lambda e, i=i: e.collective_compute("AllGather", ALU.bypass, replica_groups=RG, ins=[mixout.ap[i * 2048:(i + 1) * 2048, :]],
                                                                    outs=[gath.ap[i * 8192:(i + 1) * 8192, :]]),
                     reads=[mo_t], writes=[gath], dma=True, amt=1)

        def cp_mine(e):
            q = getq(e)
            gv = gath.ap.rearrange("(q x) f -> q (x f)", q=4)
            return e.dma_start(out=mine.ap.rearrange("(o x) f -> o (x f)", o=1), in_=gv[bass.ds(q, 1), :])
        if 'cp' not in FSKIP:
            P.op("sync", cp_mine, reads=[gath], writes=[mine], dma=True)
        if 'row' in FSKIP:
            break
        P.barrier()
        P.arena_reset()

        def din_r(n, s, l=l):
            if n in ("orwT", "omlaT", "os5T"):
                return None
            if n == "xT":
                return ext_in("xTq", s) if l == 0 else xres_d
            if (n, l) in wbf:
                return wbf[(n, l)]
            return ext_in(f"{n}_r{l}", s)

        st = {}

        def moload(mo, it, gen_ps):
            if "tmt" not in st:
                st["tmt"] = [P.sbuf(f"tmt{i}", [128, 4, 128]) for i in range(2)]
                st["s5st"] = [P.sbuf(f"s5st{i}", [128, RT]) for i in range(2)]
                st["identf"] = P.sbuf("identf", [128, 128])
                st["n"] = 0
                P.dma("sync", st["identf"][:], ext_in("ident", [128, 128])[:], writes=[st["identf"]])
            identf = st["identf"]
            t0 = it * RT
            for br in range(2):
                for r in range(4):
                    tmt = st["tmt"][st["n"] % 2]; st["n"] += 1
                    P.dma("sync", tmt[:], mine.ap[(br * 4 + r) * 2048 + t0:(br * 4 + r) * 2048 + t0 + RT, :].rearrange("(b p) f -> p b f", p=128), reads=[mine], writes=[tmt])
                    pb = gen_ps()
                    for b4 in range(4):
                        P.mm(pb[:, b4 * 128:(b4 + 1) * 128], tmt[:, b4, :], identf[:], reads=[tmt, identf], writes=[pb])
                    P.copy("vector", mo[br][:, r, :], pb[:, 0:512], reads=[pb], writes=[mo[br]])
            for r in range(4):
                P.dma("gpsimd", mo[2][:, r, :], mine.ap[(8 + r) * 2048:(9 + r) * 2048, :].rearrange("(f a) c -> f (a c)", f=128)[:, t0:t0 + RT], reads=[mine], writes=[mo[2]])

        build_row(ctx=dict(P=P, din=din_r, ps=ps, xout=(xout if last else xres_d), xbf_out=(None if last else xqbf), moload=moload))
        if not last:
            for k in range(8):
                P.op("gpsimd", lambda e, k=k: e.collective_compute("AllGather", ALU.bypass, replica_groups=RG, ins=[xqbf.ap[k * 128:(k + 1) * 128, :]],
                                                                    outs=[xg.ap[k * 512:(k + 1) * 512, :]]),
                     reads=[xqbf], writes=[xg], dma=True, amt=1)
            P.barrier()
    P.wait_tiles("sync", [xout])
    P.wait_all_dma("sync")
    P.emit()
    return nc, P


def fused_inputs(I, b, j, C, nlayers=2):
    x = I['x']
    d = dict(C)
    d["xT"] = x[b].T
    d["xTq"] = x[b, j * 2048:(j + 1) * 2048].T
    for l in range(nlayers):
        mi = mixer_inputs(I, l, j, None, C)
        for k, v in mi.items():
            if k in C or k == "xT":
                continue
            d[f"{k}_m{l}"] = v
        ri = row_inputs(I, l, None, None, None, None)
        for k, v in ri.items():
            if k in ("xT", "orwT", "omlaT", "os5T"):
                continue
            d[f"{k}_r{l}"] = v
    return {k: np.ascontiguousarray(v, dtype=np.float32) for k, v in d.items()}


from concourse.bass_utils import run_bass_kernel_spmd


def kernel(**inputs):
    I = {k: np.asarray(v, dtype=np.float32) for k, v in inputs.items()}
    C = consts()
    nc, P = build_fused(2)
    in_maps = [fused_inputs(I, b, j, C, 2) for b in range(2) for j in range(4)]
    res = run_bass_kernel_spmd(nc, in_maps, core_ids=list(range(8))).results
    P.close()
    out = np.stack([np.concatenate([res[b * 4 + q]["xout"] for q in range(4)], 1).T for b in range(2)], 0)
    return np.ascontiguousarray(out.astype(np.float32))
```

```python
import contextlib
import numpy as np
import concourse.bass as bass
import concourse.mybir as mybir

F32 = mybir.dt.float32
BF16 = mybir.dt.bfloat16
ALU = mybir.AluOpType
AF = mybir.ActivationFunctionType
AX = mybir.AxisListType


class T:
    def __init__(self, name, ap):
        self.name = name
        self.ap = ap
        self.w = None
        self.r = {}

    def __getitem__(self, k):
        return self.ap[k]


class Prog:
    NDMA = 48

    def __init__(self, nc):
        self.nc = nc
        self.st = contextlib.ExitStack()
        self.engs = ["tensor", "vector", "scalar", "gpsimd", "sync"]
        self.ops = {e: [] for e in self.engs}
        self.cnt = {e: 0 for e in self.engs}
        self.known = {e: {} for e in self.engs}
        self.esem = {e: self.st.enter_context(nc.semaphore("es_" + e)) for e in self.engs}
        self.dsem = [self.st.enter_context(nc.semaphore(f"ds{i}")) for i in range(self.NDMA)]
        self.dcnt = [0] * self.NDMA
        self.csem = self.st.enter_context(nc.semaphore("cc_sem"))
        self.ccnt = 0
        self.dma_i = 0
        self.nuniq = 0

    ARENA_WORDS = 52800

    ARENA_R_WORDS = 14600

    def use_arena(self):
        self.ARENA_WORDS = 52800 - self.ARENA_R_WORDS
        self.arena = self.st.enter_context(self.nc.sbuf_tensor("arena", [128, self.ARENA_WORDS], F32))
        self.arena_r = self.st.enter_context(self.nc.sbuf_tensor("arena_r", [128, self.ARENA_R_WORDS], F32))
        self.aoff = 0
        self.atop = 0

    def arena_reset(self):
        self.aoff = 0
        self.atop = 0

    def sbuf(self, name, shape, dt=F32, top=False):
        if getattr(self, "arena", None) is None:
            t = self.st.enter_context(self.nc.sbuf_tensor(name, list(shape), dt))
            return T(name, t)
        shape = list(shape)
        nelem = 1
        for d in shape[1:]:
            nelem *= d
        four = dt in (F32, mybir.dt.int32)
        words = nelem if four else (nelem + 1) // 2
        if top:
            off = self.atop
            self.atop += words
            assert self.atop <= self.ARENA_R_WORDS, ("arena_r overflow", name, self.atop, words)
            ap = self.arena_r[0:shape[0], off:off + words]
        else:
            off = self.aoff
            self.aoff += words
            assert self.aoff <= self.ARENA_WORDS, ("arena overflow", name, self.aoff, words)
            ap = self.arena[0:shape[0], off:off + words]
        if dt != F32:
            ap = ap.bitcast(dt)
            if not four:
                ap = ap[:, 0:nelem]
        if len(shape) == 3:
            ap = ap.rearrange("p (a b) -> p a b", a=shape[1])
        elif len(shape) == 4:
            ap = ap.rearrange("p (a b c) -> p a b c", a=shape[1], b=shape[2])
        return T(name, ap)

    def barrier(self):
        for eng in self.engs:
            waits = []
            kn = self.known[eng]
            for e2 in self.engs:
                if e2 == eng or self.cnt[e2] == 0:
                    continue
                k = ("e", e2)
                if kn.get(k, 0) < self.cnt[e2]:
                    kn[k] = self.cnt[e2]
                    waits.append((k, self.cnt[e2]))
            for slot in range(self.NDMA):
                k = ("d", slot)
                if self.dcnt[slot] > 0 and kn.get(k, 0) < self.dcnt[slot]:
                    kn[k] = self.dcnt[slot]
                    waits.append((k, self.dcnt[slot]))
            if self.ccnt > 0 and kn.get(("c", 0), 0) < self.ccnt:
                kn[("c", 0)] = self.ccnt
                waits.append((("c", 0), self.ccnt))
            self.ops[eng].append((None, waits, None))

    def psum(self, name, shape, dt=F32):
        t = self.st.enter_context(self.nc.psum_tensor(name, list(shape), dt))
        return T(name, t)

    def dram(self, name, shape, dt, kind="Internal"):
        t = self.nc.dram_tensor(name, list(shape), dt, kind=kind)
        return T(name, t.ap())

    def sub(self, t, name, key):
        return T(name, t.ap[key])

    def _tokkey(self, tok):
        return (tok[0], tok[1])

    def op(self, eng, fn, reads=(), writes=(), inc=True, dma=False, touch=(), amt=16):
        deps = {}

        def add(tok):
            if tok is None:
                return
            k = self._tokkey(tok)
            if deps.get(k, 0) < tok[2]:
                deps[k] = tok[2]

        for t in reads:
            add(t.w)
        for t in list(writes) + list(touch):
            add(t.w)
            for k, v in t.r.items():
                add((k[0], k[1], v))
        if dma and amt == 1:
            self.ccnt += 1
            tok = ("c", 0, self.ccnt, 1)
        elif dma:
            slot = self.dma_i % self.NDMA
            self.dma_i += 1
            if self.dcnt[slot] > 0:
                add(("d", slot, self.dcnt[slot]))
            self.dcnt[slot] += amt
            tok = ("d", slot, self.dcnt[slot], amt)
        else:
            tok = ("e", eng, self.cnt[eng] + 1)
            if inc:
                self.cnt[eng] += 1
        waits = []
        kn = self.known[eng]
        for k, v in deps.items():
            if k[0] == "e" and k[1] == eng:
                if eng == "tensor":
                    continue
                assert v <= self.cnt[eng] or (v == tok[2] and False), (eng, v, self.cnt[eng])
            if kn.get(k, 0) >= v:
                continue
            kn[k] = v
            waits.append((k, v))
        self.ops[eng].append((fn, waits, tok if (inc or dma) else None))
        for t in reads:
            k = self._tokkey(tok)
            if t.r.get(k, 0) < tok[2]:
                t.r[k] = tok[2]
        for t in writes:
            t.w = tok
            t.r = {}
        return tok

    def wait_tiles(self, eng, tiles):
        self.op(eng, None, reads=tiles, inc=False)

    def wait_all_dma(self, eng="sync"):
        waits = []
        for slot in range(self.NDMA):
            if self.dcnt[slot] > 0 and self.known[eng].get(("d", slot), 0) < self.dcnt[slot]:
                waits.append((("d", slot), self.dcnt[slot]))
                self.known[eng][("d", slot)] = self.dcnt[slot]
        if self.ccnt > 0 and self.known[eng].get(("c", 0), 0) < self.ccnt:
            self.known[eng][("c", 0)] = self.ccnt
            waits.append((("c", 0), self.ccnt))
        self.ops[eng].append((None, waits, None))

    def _sem(self, k):
        if k[0] == "c":
            return self.csem
        return self.esem[k[1]] if k[0] == "e" else self.dsem[k[1]]

    def emit(self):
        nc = self.nc
        with nc.Block() as block:
            def mk(name):
                def body(e):
                    for fn, waits, tok in self.ops[name]:
                        for k, v in waits:
                            e.wait_ge(self._sem(k), v)
                        if fn is None:
                            continue
                        ins = fn(e)
                        if tok is not None:
                            if tok[0] == "c":
                                ins.then_inc(self.csem, 1)
                            elif tok[0] == "d":
                                ins.then_inc(self.dsem[tok[1]], tok[3])
                            else:
                                ins.then_inc(self.esem[name], 1)
                return body
            block.tensor(mk("tensor"))
            block.vector(mk("vector"))
            block.scalar(mk("scalar"))
            block.gpsimd(mk("gpsimd"))
            block.sync(mk("sync"))

    def close(self):
        self.st.close()

    def dma(self, eng, out_ap, in_ap, reads=(), writes=()):
        return self.op(eng, lambda e: e.dma_start(out=out_ap, in_=in_ap), reads=reads, writes=writes, dma=True)

    def mm(self, out_ap, lhsT, rhs, start=True, stop=True, reads=(), writes=(), **kw):
        return self.op("tensor", lambda e: e.matmul(out_ap, lhsT, rhs, start=start, stop=stop, **kw),
                       reads=reads, writes=writes if stop else (), touch=() if stop else writes, inc=True)

    def tr(self, out_ap, in_ap, ident_ap, reads=(), writes=(), inc=True):
        return self.op("tensor", lambda e: e.matmul(out_ap, in_ap, ident_ap, start=True, stop=True), reads=reads, writes=writes if inc else (), inc=inc)

    def act(self, out_ap, in_ap, func, reads=(), writes=(), eng="scalar", **kw):
        return self.op(eng, lambda e: e.activation(out_ap, in_ap, func, **kw), reads=reads, writes=writes)

    def tt(self, eng, out_ap, a, b, op, reads=(), writes=()):
        return self.op(eng, lambda e: e.tensor_tensor(out_ap, a, b, op), reads=reads, writes=writes)

    def ts(self, eng, out_ap, a, s1, s2, op0, op1=None, reads=(), writes=()):
        if op1 is None:
            return self.op(eng, lambda e: e.tensor_scalar(out_ap, a, s1, None, op0), reads=reads, writes=writes)
        return self.op(eng, lambda e: e.tensor_scalar(out_ap, a, s1, s2, op0, op1), reads=reads, writes=writes)

    def stt(self, eng, out_ap, in0, scalar, in1, op0, op1, reads=(), writes=()):
        return self.op(eng, lambda e: e.scalar_tensor_tensor(out_ap, in0, scalar, in1, op0, op1), reads=reads, writes=writes)

    def copy(self, eng, out_ap, in_ap, reads=(), writes=()):
        if eng == "scalar":
            return self.op(eng, lambda e: e.copy(out_ap, in_ap), reads=reads, writes=writes)
        return self.op(eng, lambda e: e.tensor_copy(out_ap, in_ap), reads=reads, writes=writes)

    def memset(self, eng, ap, val, writes=()):
        return self.op(eng, lambda e: e.memset(ap, val), writes=writes)


import math, os
RSTOP = int(os.environ.get('RSTOP', '9'))
CSTOP = int(os.environ.get('CSTOP', '99'))
G2V = os.environ.get('G2V', '')
import numpy as np

TT = 256
NTILE = 32
QB = TT // 128
SUB = 256
S5SUB = 128
CH = 64
ATTN_SCALE = 1.0 / math.sqrt(96.0)
NEG_EXPM05 = -math.exp(-0.5)
USE_R = os.environ.get('USE_R', '1') == '1'
F32R = mybir.dt.float32r
R = (lambda ap: ap.bitcast(F32R)) if USE_R else (lambda ap: ap)


def build_mixer(ntile=NTILE, do_rwkv=True, do_mla=True, do_s5=True, ctx=None):
    if ctx is None:
        nc = bass.Bass("TRN2", target_bir_lowering=False)
        P = Prog(nc)
        din = lambda n, s: P.dram(n, s, F32, kind="ExternalInput")
    else:
        P = ctx["P"]; din = ctx["din"]
    xT_d = din("xT", [1024, 8192])
    wmix_d = din("wmix", [1024, 1408])
    vecs_d = din("vecs", [128, 16])
    w2a2_d = din("w2a2", [128, 128])
    g2_d = din("g2", [128, 128])
    gng_d = din("gng", [128, 64])
    gnb_d = din("gnb", [128, 64])
    qup_d = din("qup", [256, 4 * 96])
    kvupk_d = din("kvupk", [128, 128])
    kvupv_d = din("kvupv", [128, 128])
    rope_d = din("rope", [2, 32, 8192])
    ident_d = din("ident", [128, 128])
    maskb_d = din("maskb", [128, 640])
    bdm_d = din("bdm", [128, 2])
    blk_d = din("blkones", [128, 128])
    tri_d = din("tri", [128, 128])
    rmask_d = din("rmask", [128, SUB])
    jm_d = din("jm", [128, 128])
    s5sp_d = din("s5sp", [128, 24])
    s5gc_d = din("s5gc", [128, 4 * 64 + 1])
    gmask_d = din("gmask", [128, 8])
    cmain_d = din("cmain", [128, 128])
    cswap_d = din("cswap", [128, 128])
    sign1_d = din("sign1", [128, 1])
    tidx_d = din("tidx", [128, S5SUB])
    if ctx is None:
        orw_d = P.dram("orw", [8192, 128], F32, kind="ExternalOutput")
        omla_d = P.dram("omla", [8192, 128], F32, kind="ExternalOutput")
        os5_d = P.dram("os5", [128, 8192], F32, kind="ExternalOutput")
    else:
        orw_d, omla_d, os5_d = ctx["outs"]
    if ctx is not None and ctx.get("oap") is not None:
        oap = ctx["oap"]
    else:
        def oap(kind, t0, n):
            if kind == "orw":
                return orw_d.ap[t0:t0 + n, :]
            if kind == "omla":
                return omla_d.ap[t0:t0 + n, :]
            return os5_d.ap[:, t0:t0 + n]

    sb = P.sbuf
    sbt = lambda n, shp, dt=F32: P.sbuf(n, shp, dt, top=True)
    def load(name, d, shape, dt=F32, eng="sync"):
        t = sb(name, shape, dt)
        P.dma("gpsimd" if dt != F32 else eng, t[:], d[:], writes=[t])
        return t
    ident = load("ident_s", ident_d, [128, 128])
    maskb = load("maskb_s", maskb_d, [128, 640])
    bdm = load("bdm_s", bdm_d, [128, 2])
    blk = load("blk_s", blk_d, [128, 128])
    tri = load("tri_s", tri_d, [128, 128], BF16)
    rmask = load("rmask_s", rmask_d, [128, SUB])
    jm = load("jm_s", jm_d, [128, 128])
    vecs = load("vecs_s", vecs_d, [128, 16])
    w2a2 = load("w2a2_s", w2a2_d, [128, 128])
    g2 = load("g2_s", g2_d, [128, 128])
    gng = load("gng_s", gng_d, [128, 64])
    gnb = load("gnb_s", gnb_d, [128, 64])
    kvupk = load("kvupk_s", kvupk_d, [128, 128], BF16)
    kvupv = load("kvupv_s", kvupv_d, [128, 128], BF16)
    qup = sb("qup_s", [128, 2, 384], BF16)
    P.dma("gpsimd", qup[:], qup_d.ap.rearrange("(k p) m -> p k m", p=128), writes=[qup])
    wmix = sb("wmix_s", [128, 8, 1408], BF16)
    for k in range(8):
        P.dma("gpsimd", wmix[:, k, :], wmix_d[k * 128:(k + 1) * 128, :], writes=[wmix])
    ones = sb("ones_s", [128, 128])
    P.memset("vector", ones[:], 1.0, writes=[ones])
    identr = sbt("identr", [128, 128]); onesr = sbt("onesr", [128, 1]); jmr = sbt("jmr", [128, 128])
    P.copy("vector", R(identr[:]), ident[:], reads=[ident], writes=[identr])
    P.copy("vector", R(onesr[:]), ones[:, 0:1], reads=[ones], writes=[onesr])
    P.copy("vector", R(jmr[:]), jm[:], reads=[jm], writes=[jmr])

    ps = [P.psum(f"ps{i}", [128, 512]) for i in range(8)] if ctx is None else ctx["ps"]
    gctr = [0]

    def gen_ps():
        gctr[0] += 1
        return ps[5 + gctr[0] % 3]

    V = lambda c: vecs[:, c:c + 1]

    xbf = sb("xbf", [128, 8, TT], BF16)
    praw = [sb(f"praw{m}", [128, TT + 1]) for m in range(5)]
    for m in range(5):
        P.memset("vector", praw[m][:, 0:1], 0.0, writes=[praw[m]])
    dtmp = sb("dtmp", [128, TT])
    cct = sb("cct", [128, TT])
    sst = sb("sst", [128, TT])

    if do_s5:
        s5sp = load("s5sp_s", s5sp_d, [128, 24])
        s5gc = load("s5gc_s", s5gc_d, [128, 257])
        gmask = load("gmask_s", gmask_d, [128, 8])
        cmain = load("cmain_s", cmain_d, [128, 128])
        cswap = load("cswap_s", cswap_d, [128, 128])
        sign1 = load("sign1_s", sign1_d, [128, 1])
        tidx = load("tidx_s", tidx_d, [128, S5SUB])
        TWO_PI = 2.0 * math.pi

        scs = {}

        def sincos(name, ang, shape, want_cos, out=None):
            key = tuple(shape)
            if key not in scs:
                scs[key] = (sb(f"scf{len(scs)}", shape), sb(f"sci{len(scs)}", shape, mybir.dt.int32), sb(f"scg{len(scs)}", shape))
            f, fi, g = scs[key]
            o = sb(name + "_o", shape) if out is None else out
            P.ts("vector", f[:], ang[:], 1.0 / TWO_PI, 0.25 if want_cos else 0.0, ALU.mult, ALU.add, reads=[ang], writes=[f])
            P.copy("vector", fi[:], f[:], reads=[f], writes=[fi])
            P.copy("vector", g[:], fi[:], reads=[fi], writes=[g])
            P.tt("vector", f[:], f[:], g[:], ALU.subtract, reads=[f, g], writes=[f])
            P.ts("vector", g[:], f[:], 0.5, None, ALU.is_ge, reads=[f], writes=[g])
            P.tt("vector", f[:], f[:], g[:], ALU.subtract, reads=[f, g], writes=[f])
            P.ts("vector", g[:], f[:], -0.5, None, ALU.is_lt, reads=[f], writes=[g])
            P.tt("vector", f[:], f[:], g[:], ALU.add, reads=[f, g], writes=[f])
            oap = o[:] if out is None else out_ap[0]
            P.act(oap, f[:], AF.Sin, scale=TWO_PI, reads=[f], writes=[o])
            return o

        step_sp = sb("step_sp", [128, 8])
        P.act(step_sp[:], s5sp[:, 16:24], AF.Exp, reads=[s5sp], writes=[step_sp])
        lre_sp = sb("lre_sp", [128, 8])
        P.ts("vector", lre_sp[:], s5sp[:, 0:8], -1e-4, None, ALU.min, reads=[s5sp], writes=[lre_sp])
        P.tt("vector", lre_sp[:], lre_sp[:], step_sp[:], ALU.mult, reads=[lre_sp, step_sp], writes=[lre_sp])
        mag_sp = sb("mag_sp", [128, 8])
        P.act(mag_sp[:], lre_sp[:], AF.Exp, reads=[lre_sp], writes=[mag_sp])
        th_sp = sb("th_sp", [128, 8])
        P.tt("vector", th_sp[:], s5sp[:, 8:16], step_sp[:], ALU.mult, reads=[s5sp, step_sp], writes=[th_sp])
        thf = sb("thf", [128, 8]); thi = sb("thi", [128, 8], mybir.dt.int32); thg = sb("thg", [128, 8])
        P.ts("vector", thf[:], th_sp[:], 1.0 / TWO_PI, None, ALU.mult, reads=[th_sp], writes=[thf])
        P.copy("vector", thi[:], thf[:], reads=[thf], writes=[thi])
        P.copy("vector", thg[:], thi[:], reads=[thi], writes=[thg])
        P.tt("vector", thf[:], thf[:], thg[:], ALU.subtract, reads=[thf, thg], writes=[thf])
        ctab = sb("ctab", [128, 8, S5SUB]); stab = sb("stab", [128, 8, S5SUB])
        angt = sb("angt", [128, S5SUB])
        for g in range(8):
            P.ts("vector", angt[:], tidx[:], thf[:, g:g + 1], TWO_PI, ALU.mult, ALU.mult, reads=[tidx, thf], writes=[angt])
            P.ts("vector", angt[:], angt[:], TWO_PI * (S5SUB + 1), None, ALU.add, reads=[angt], writes=[angt])
            out_ap = [stab[:, g, :]]
            sincos(f"sc_s{g}", angt, [128, S5SUB], False, out=stab)
            out_ap = [ctab[:, g, :]]
            sincos(f"sc_c{g}", angt, [128, S5SUB], True, out=ctab)
        step_gc = sb("step_gc", [128, 1])
        P.act(step_gc[:], s5gc[:, 256:257], AF.Exp, reads=[s5gc], writes=[step_gc])
        lre = sb("lre_gc", [128, 64])
        P.ts("vector", lre[:], s5gc[:, 0:64], -1e-4, None, ALU.min, reads=[s5gc], writes=[lre])
        lim = s5gc
        magg = sb("mag_gc", [128, 64])
        P.act(magg[:], lre[:], AF.Exp, scale=step_gc[:, 0:1], reads=[lre, step_gc], writes=[magg])
        angg = sb("ang_gc", [128, 64])
        P.ts("vector", angg[:], s5gc[:, 64:128], step_gc[:, 0:1], None, ALU.mult, reads=[s5gc, step_gc], writes=[angg])
        sing = sincos("sg", angg, [128, 64], False)
        cosg = sincos("cg", angg, [128, 64], True)
        lbre = sb("lbre", [128, 64]); lbim = sb("lbim", [128, 64])
        P.tt("vector", lbre[:], magg[:], cosg[:], ALU.mult, reads=[magg, cosg], writes=[lbre])
        P.tt("vector", lbim[:], magg[:], sing[:], ALU.mult, reads=[magg, sing], writes=[lbim])
        den = sb("den", [128, 64]); t1 = sb("s5t1", [128, 64]); t2 = sb("s5t2", [128, 64])
        P.tt("vector", den[:], lre[:], lre[:], ALU.mult, reads=[lre], writes=[den])
        P.tt("vector", t1[:], s5gc[:, 64:128], s5gc[:, 64:128], ALU.mult, reads=[s5gc], writes=[t1])
        P.tt("vector", den[:], den[:], t1[:], ALU.add, reads=[den, t1], writes=[den])
        P.op("vector", lambda e: e.reciprocal(den[:], den[:]), reads=[den], writes=[den])
        nre = sb("nre", [128, 64])
        P.ts("vector", nre[:], lbre[:], -1.0, None, ALU.add, reads=[lbre], writes=[nre])
        fre = sb("fre", [128, 64]); fim = sb("fim", [128, 64])
        P.tt("vector", t1[:], nre[:], lre[:], ALU.mult, reads=[nre, lre], writes=[t1])
        P.tt("vector", t2[:], lbim[:], s5gc[:, 64:128], ALU.mult, reads=[lbim, s5gc], writes=[t2])
        P.tt("vector", t1[:], t1[:], t2[:], ALU.add, reads=[t1, t2], writes=[t1])
        P.tt("vector", fre[:], t1[:], den[:], ALU.mult, reads=[t1, den], writes=[fre])
        P.tt("vector", t1[:], lbim[:], lre[:], ALU.mult, reads=[lbim, lre], writes=[t1])
        P.tt("vector", t2[:], nre[:], s5gc[:, 64:128], ALU.mult, reads=[nre, s5gc], writes=[t2])
        P.tt("vector", t1[:], t1[:], t2[:], ALU.subtract, reads=[t1, t2], writes=[t1])
        P.tt("vector", fim[:], t1[:], den[:], ALU.mult, reads=[t1, den], writes=[fim])
        bbre = sb("bbre", [128, 64]); bbim = sb("bbim", [128, 64])
        bre = s5gc[:, 128:192]; bim = s5gc[:, 192:256]
        P.tt("vector", t1[:], fre[:], bre, ALU.mult, reads=[fre, s5gc], writes=[t1])
        P.tt("vector", t2[:], fim[:], bim, ALU.mult, reads=[fim, s5gc], writes=[t2])
        P.tt("vector", bbre[:], t1[:], t2[:], ALU.subtract, reads=[t1, t2], writes=[bbre])
        P.tt("vector", t1[:], fre[:], bim, ALU.mult, reads=[fre, s5gc], writes=[t1])
        P.tt("vector", t2[:], fim[:], bre, ALU.mult, reads=[fim, s5gc], writes=[t2])
        P.tt("vector", bbim[:], t1[:], t2[:], ALU.add, reads=[t1, t2], writes=[bbim])
        W1 = sbt("s5W1", [128, 8, 128]); W2 = sbt("s5W2", [128, 8, 128])
        ngm = sb("ngmask", [128, 8])
        P.ts("vector", ngm[:], gmask[:], -1.0, None, ALU.mult, reads=[gmask], writes=[ngm])
        for g in range(8):
            P.ts("vector", R(W1[:, g, 0:64]), bbre[:], gmask[:, g:g + 1], None, ALU.mult, reads=[bbre, gmask], writes=[W1])
            P.ts("vector", R(W1[:, g, 64:128]), bbim[:], gmask[:, g:g + 1], None, ALU.mult, reads=[bbim, gmask], writes=[W1])
            P.ts("vector", R(W2[:, g, 0:64]), bbim[:], gmask[:, g:g + 1], None, ALU.mult, reads=[bbim, gmask], writes=[W2])
            P.ts("vector", R(W2[:, g, 64:128]), bbre[:], ngm[:, g:g + 1], None, ALU.mult, reads=[bbre, ngm], writes=[W2])
        wc1 = sb("wc1", [128, 128]); wc2 = sb("wc2", [128, 128])
        P.ts("vector", wc1[:], cmain[:], sign1[:, 0:1], None, ALU.mult, reads=[cmain, sign1], writes=[wc1])
        P.ts("vector", wc2[:], cswap[:], -1.0, None, ALU.mult, reads=[cswap], writes=[wc2])
        Wc1g = sbt("Wc1g", [128, 8, 128]); Wc2g = sbt("Wc2g", [128, 8, 128])
        P.memset("vector", Wc1g[:], 0.0, writes=[Wc1g]); P.memset("vector", Wc2g[:], 0.0, writes=[Wc2g])
        P.copy("vector", R(Wc1g[:]), Wc1g[:], reads=[Wc1g], writes=[Wc1g]); P.copy("vector", R(Wc2g[:]), Wc2g[:], reads=[Wc2g], writes=[Wc2g])
        for g in range(8):
            P.copy("vector", R(Wc1g[:, g, g * 16:(g + 1) * 16]), wc1[:, g * 16:(g + 1) * 16], reads=[wc1], writes=[Wc1g])
            P.copy("vector", R(Wc2g[:, g, g * 16:(g + 1) * 16]), wc2[:, g * 16:(g + 1) * 16], reads=[wc2], writes=[Wc2g])
        h0 = sb("s5h0", [128, 8])
        P.memset("vector", h0[:], 0.0, writes=[h0])
        ubuf = sbt("s5u", [128, TT])
        s5x = [(sbt if i < 2 else sb)(f"s5x{i}", [128, S5SUB]) for i in range(4)]
        s5y = sb("s5y", [128, S5SUB]); s5y2 = sb("s5y2", [128, S5SUB]); s5o = sb("s5o", [128, S5SUB])

    if do_mla:
        Kt = [[P.sbuf(f"Kt{h}_{i}", [96, TT], BF16) for i in range(ntile)] for h in range(2)]
        Vaug = [P.sbuf(f"Va{i}", [128, 2, 65], BF16) for i in range(ntile * QB)]
        for va in Vaug:
            P.memset("gpsimd", va[:, :, 64:65], 1.0, writes=[va])
        Qt = [sb(f"Qt{h}", [96, TT], BF16) for h in range(2)]
        cq = sb("cq", [128, 2, TT]); sq = sb("sq", [128, 2, TT]); rstd = sb("rstd", [128, TT])
        cqn = sb("cqn", [128, 2, TT], BF16)
        ckv = sb("ckv", [128, TT]); sqk = sb("sqk", [128, TT]); rstdk = sb("rstdk", [128, TT]); ckvn = sb("ckvn", [128, TT], BF16)
        kr1 = dtmp; kr2 = sb("kr2", [128, TT])
        qr1 = dtmp; qr2 = kr2
        Pt = [sb(f"Pt{i}", [128, TT], BF16) for i in range(3)]
        osb = sb("osb", [128, QB, 64]); orec = sb("orec", [128, QB, 1])
        pctr = [0]

    if do_rwkv:
        f2 = lambda n: sb(n, [128, SUB])
        LW = f2("LW"); AS = f2("AS"); GF = f2("GF"); KK = f2("KK"); D2 = f2("D2"); D3 = f2("D3"); KM = f2("KM"); BV = f2("BV")
        CU = f2("CU"); E1 = f2("E1"); E2 = f2("E2"); RT = f2("RT"); KT_ = f2("KT_"); AT = f2("AT"); BT = f2("BT")
        KHF = f2("KHF"); BHF = f2("BHF"); RK = f2("RK")
        gC = sb("gC", [128, 4])
        BD = [[sbt(f"BD{q}_{r}", [128, 2, 64]) for r in range(2)] for q in range(9)]
        AM = [sbt(f"AM{r}", [128, 640]) for r in range(2)]
        TM = [sbt(f"TM{r}", [128, 640]) for r in range(2)]
        XS = [[sbt(f"XS{a}_{r}", [128, 256]) for r in range(3)] for a in range(2)]
        PS_ = [[sbt(f"PS{a}_{r}", [128, 256]) for r in range(3)] for a in range(2)]
        RH = [sbt(f"RH{r}", [128, 128]) for r in range(2)]
        GT = [sbt(f"GT{r}", [128, 128]) for r in range(2)]
        YH = [sbt(f"YH{r}", [128, 256]) for r in range(5)]
        P.memset("vector", YH[4][:], 0.0, writes=[YH[4]])
        P.copy("vector", R(YH[4][:]), YH[4][:], reads=[YH[4]], writes=[YH[4]])
        Vc = sb("Vc", [128, 4, 64]); Gc = sb("Gc", [128, 4, 64]); Yc = sb("Yc", [128, 4, 64]); Ysq = sb("Ysq", [128, 4, 64])
        coef = sb("coef", [128, 4])
        st1 = sb("st1", [128, 4]); st2 = sb("st2", [128, 4]); st3 = sb("st3", [128, 4])
        yo = sb("yo", [128, 4, 64])
        hstate = [4]

    for it in range(ntile):
        c0 = it * TT
        if ctx is None or ctx.get("xload") is None:
            P.dma("gpsimd", xbf[:], xT_d.ap.rearrange("(k p) t -> p k t", p=128)[:, :, c0:c0 + TT], writes=[xbf])
        else:
            ctx["xload"](xbf, c0)
        if do_mla:
            P.dma("sync", cct[64:96, :], rope_d[0, :, c0:c0 + TT], writes=[cct])
            P.dma("sync", sst[64:96, :], rope_d[1, :, c0:c0 + TT], writes=[sst])
        for m in range(11):
            if m < 5 and not do_rwkv:
                continue
            if m in (5, 6, 7, 8, 10) and not do_mla:
                continue
            if m == 9 and not do_s5:
                continue
            pb = gen_ps()
            for k in range(8):
                P.mm(pb[:, 0:TT], wmix[:, k, m * 128:(m + 1) * 128], xbf[:, k, :], start=(k == 0), stop=(k == 7), reads=[wmix, xbf], writes=[pb])
            if m < 5:
                pr = praw[m]
                P.act(pr[:, 1:TT + 1], pb[:, 0:TT], AF.Copy, reads=[pb], writes=[pr])
                P.tt("vector", dtmp[:], pr[:, 0:TT], pr[:, 1:TT + 1], ALU.subtract, reads=[pr], writes=[dtmp])
                P.copy("vector", pr[:, 0:1], pr[:, TT:TT + 1], reads=[pr], writes=[pr])
                P.stt("vector", pr[:, 1:TT + 1], dtmp[:], V(m), pr[:, 1:TT + 1], ALU.mult, ALU.add, reads=[dtmp, vecs, pr], writes=[pr])
            elif m in (5, 6):
                j = m - 5
                P.act(cq[:, j, :], pb[:, 0:TT], AF.Copy, reads=[pb], writes=[cq])
                P.act(sq[:, j, :], pb[:, 0:TT], AF.Square, reads=[pb], writes=[sq])
            elif m == 7:
                P.act(ckv[:], pb[:, 0:TT], AF.Copy, reads=[pb], writes=[ckv])
                P.act(sqk[:], pb[:, 0:TT], AF.Square, reads=[pb], writes=[sqk])
            elif m == 8:
                P.tt("vector", kr1[64:96, :], pb[64:96, 0:TT], cct[64:96, :], ALU.mult, reads=[pb, cct], writes=[kr1])
            elif m == 10:
                P.tt("vector", kr2[64:96, :], pb[64:96, 0:TT], sst[64:96, :], ALU.mult, reads=[pb, sst], writes=[kr2])
                P.tt("vector", Kt[0][it][64:96, :], kr1[64:96, :], kr2[64:96, :], ALU.add, reads=[kr1, kr2], writes=[Kt[0][it]])
                P.tt("vector", Kt[1][it][64:96, :], kr1[64:96, :], kr2[64:96, :], ALU.add, reads=[kr1, kr2], writes=[Kt[1][it]])
            elif m == 9:
                P.act(R(ubuf[:]), pb[:, 0:TT], AF.Copy, reads=[pb], writes=[ubuf])

        def _g_rwkv():
            yield
            if do_rwkv:
                for s in range(TT // SUB):
                    o = 1 + s * SUB
                    r_ = praw[0][:, o:o + SUB]; k_ = praw[1][:, o:o + SUB]; v_ = praw[2][:, o:o + SUB]
                    dwa = praw[3]; dg = praw[4]
                    P.act(dwa[0:64, o:o + SUB], dwa[0:64, o:o + SUB], AF.Tanh, reads=[dwa], writes=[dwa])
                    pb = gen_ps()
                    P.mm(pb[:, 0:SUB], w2a2[0:64, :], dwa[0:64, o:o + SUB], reads=[w2a2, dwa], writes=[pb])
                    P.act(LW[:], pb[:, 0:SUB], AF.Sigmoid, bias=V(5), reads=[pb, vecs], writes=[LW])
                    P.ts("vector", LW[:], LW[:], NEG_EXPM05, None, ALU.mult, reads=[LW], writes=[LW])
                    pb = gen_ps()
                    P.mm(pb[:, 0:SUB], w2a2[64:128, :], dwa[64:128, o:o + SUB], reads=[w2a2, dwa], writes=[pb])
                    P.act(AS[:], pb[:, 0:SUB], AF.Sigmoid, bias=V(6), reads=[pb, vecs], writes=[AS])
                    P.act(dg[:, o:o + SUB], dg[:, o:o + SUB], AF.Sigmoid, reads=[dg], writes=[dg])
                    pb = gen_ps()
                    P.mm(pb[:, 0:SUB], g2[:, :], dg[:, o:o + SUB], reads=[g2, dg], writes=[pb])
                    P.act(GF[:], pb[:, 0:SUB], AF.Copy, reads=[pb], writes=[GF])
                    yield
                    P.ts("vector", KK[:], k_, V(7), None, ALU.mult, reads=[praw[1], vecs], writes=[KK])
                    P.tt("vector", D2[:], KK[:], KK[:], ALU.mult, reads=[KK], writes=[D2])
                    pb = gen_ps()
                    P.mm(pb[:, 0:SUB], blk[:, :], D2[:], reads=[blk, D2], writes=[pb])
                    P.ts("vector", D2[:], pb[:, 0:SUB], 1e-12, None, ALU.max, reads=[pb], writes=[D2])
                    P.act(D2[:], D2[:], AF.Sqrt, reads=[D2], writes=[D2])
                    P.op("vector", lambda e: e.reciprocal(D2[:], D2[:]), reads=[D2], writes=[D2])
                    P.tt("vector", KK[:], KK[:], D2[:], ALU.mult, reads=[KK, D2], writes=[KK])
                    P.ts("vector", D2[:], AS[:], 1.0, V(8), ALU.subtract, ALU.mult, reads=[AS, vecs], writes=[D2])
                    P.stt("vector", KM[:], D2[:], 1.0, k_, ALU.add, ALU.mult, reads=[D2, praw[1]], writes=[KM])
                    P.tt("vector", BV[:], KK[:], AS[:], ALU.mult, reads=[KK, AS], writes=[BV])
                    P.stt("vector", RK[:], r_, V(9), KM[:], ALU.mult, ALU.mult, reads=[praw[0], vecs, KM], writes=[RK])
                    yield
                    P.op("vector", lambda e: e.tensor_tensor_scan(CU[:], rmask[:], LW[:], 0.0, ALU.mult, ALU.add), reads=[rmask, LW], writes=[CU])
                    cu3 = CU[:].rearrange("p (c t) -> p c t", t=CH)
                    P.act(gC[:], CU[:].rearrange("p (c t) -> p c t", t=CH)[:, :, CH - 1], AF.Exp, reads=[CU], writes=[gC])
                    P.act(E1[:], CU[:], AF.Exp, reads=[CU], writes=[E1])
                    P.tt("vector", RT[:], r_, E1[:], ALU.mult, reads=[praw[0], E1], writes=[RT])
                    P.act(E2[:], CU[:], AF.Exp, scale=-1.0, reads=[CU], writes=[E2])
                    P.tt("vector", KT_[:], KM[:], E2[:], ALU.mult, reads=[KM, E2], writes=[KT_])
                    P.tt("vector", BT[:], BV[:], E2[:], ALU.mult, reads=[BV, E2], writes=[BT])
                    P.tt("vector", D3[:], CU[:], LW[:], ALU.subtract, reads=[CU, LW], writes=[D3])
                    P.act(E1[:], D3[:], AF.Exp, reads=[D3], writes=[E1])
                    P.stt("vector", AT[:], KK[:], -1.0, E1[:], ALU.mult, ALU.mult, reads=[KK, E1], writes=[AT])
                    P.tt("vector", D3[:].rearrange("p (c t) -> p c t", t=CH), cu3[:, :, CH - 1:CH].to_broadcast([128, SUB // CH, CH]), cu3, ALU.subtract, reads=[CU], writes=[D3])
                    P.act(E2[:], D3[:], AF.Exp, reads=[D3], writes=[E2])
                    P.tt("vector", KHF[:], KM[:], E2[:], ALU.mult, reads=[KM, E2], writes=[KHF])
                    P.tt("vector", BHF[:], BV[:], E2[:], ALU.mult, reads=[BV, E2], writes=[BHF])
                    if RSTOP < 2:
                        continue
                    srcs = [(RT, None), (KT_, None), (AT, None), (BT, None), (KHF, None), (BHF, None), (praw[2], v_), (GF, None), (RK, None)]
                    f = lambda t: t[:].rearrange("p a b -> p (a b)")
                    LS = 2
                    for cp in range(0, SUB // CH, LS):
                        chunks = list(range(cp, cp + LS))
                        S = {c: {} for c in chunks}
                        for c in chunks:
                            st = S[c]; rot = c % LS
                            cs = slice(c * CH, (c + 1) * CH)
                            bd = []
                            for q, (tl, view) in enumerate(srcs):
                                src = (view if view is not None else tl[:])[:, cs]
                                dst = BD[q][rot]
                                P.tt("gpsimd", R(dst[:]), src.unsqueeze(1).to_broadcast([128, 2, CH]), bdm[:, 0:2].unsqueeze(2).to_broadcast([128, 2, CH]),
                                     ALU.mult, reads=[tl, bdm], writes=[dst])
                                bd.append(dst)
                            st["bd"] = bd
                            Rb, Kb, Ab, Bb, KHb, BHb, Vb, Gb, RKb = bd
                            am = AM[rot]; tm = TM[rot]
                            st["am"] = am; st["tm"] = tm
                            pb = gen_ps()
                            P.mm(pb[:, 0:128], R(f(Bb)), R(f(Ab)), reads=[Bb, Ab], writes=[pb])
                            P.mm(pb[:, 128:256], R(f(Ab)), R(f(Bb)), reads=[Bb, Ab], writes=[pb])
                            P.mm(pb[:, 256:384], R(f(Kb)), R(f(Ab)), reads=[Kb, Ab], writes=[pb])
                            P.mm(pb[:, 384:512], R(f(Bb)), R(f(Rb)), reads=[Bb, Rb], writes=[pb])
                            P.tt("vector", R(am[:, 0:512]), pb[:, 0:512], maskb[:, 0:512], ALU.mult, reads=[pb, maskb], writes=[am])
                            yield
                        for c in chunks:
                            st = S[c]; am = st["am"]; tm = st["tm"]
                            Rb, Kb, Ab, Bb, KHb, BHb, Vb, Gb, RKb = st["bd"]
                            pb = gen_ps()
                            P.mm(pb[:, 0:128], R(f(Kb)), R(f(Rb)), reads=[Kb, Rb], writes=[pb])
                            P.tr(pb[:, 128:256], R(f(KHb)), R(identr[:]), reads=[KHb, ident], writes=[pb])
                            P.tr(pb[:, 256:384], R(f(BHb)), R(identr[:]), reads=[BHb, ident], writes=[pb])
                            P.tr(pb[:, 384:512], R(f(Vb)), R(identr[:]), reads=[Vb, ident], writes=[pb])
                            P.tt("vector", R(am[:, 512:640]), pb[:, 0:128], maskb[:, 512:640], ALU.mult, reads=[pb, maskb], writes=[am])
                            P.copy("vector", R(tm[:, 0:384]), pb[:, 128:512], reads=[pb], writes=[tm])
                            yield
                        for c in chunks:
                            st = S[c]; am = st["am"]; tm = st["tm"]; rot = c % LS
                            Rb, Kb, Ab, Bb, KHb, BHb, Vb, Gb, RKb = st["bd"]
                            AakT = am[:, 256:384]; Vt = tm[:, 256:384]
                            pb = gen_ps()
                            P.tr(pb[:, 0:128], R(f(Ab)), R(identr[:]), reads=[Ab, ident], writes=[pb])
                            P.tr(pb[:, 256:384], R(f(Gb)), R(identr[:]), reads=[Gb, ident], writes=[pb])
                            P.mm(pb[:, 128:256], R(AakT), R(Vt), reads=[am, tm], writes=[pb])
                            xs = XS[rot][0]
                            P.act(R(xs[:]), pb[:, 0:256], AF.Copy, reads=[pb], writes=[xs])
                            P.act(R(tm[:, 512:640]), pb[:, 256:384], AF.Copy, reads=[pb], writes=[tm])
                            pbc = gen_ps()
                            P.mm(pbc[:, 0:1], f(RKb), ones[:, 0:1], reads=[RKb, ones], writes=[pbc])
                            P.act(coef[:, c:c + 1], pbc[:, 0:1], AF.Copy, reads=[pbc], writes=[coef])
                            P.tt("gpsimd", Vc[:, c, :], tm[:, 256:320], tm[:, 320:384], ALU.add, reads=[tm], writes=[Vc])
                            P.tt("gpsimd", Gc[:, c, :], tm[:, 512:576], tm[:, 576:640], ALU.add, reads=[tm], writes=[Gc])
                            st["pn"] = am[:, 128:256]; st["pt"] = am[:, 0:128]; st["pT"] = am; st["xi"] = 0
                            yield
                        for i in range(6):
                            for c in chunks:
                                st = S[c]; rot = c % LS
                                pb = gen_ps()
                                xcur = XS[rot][st["xi"] % 3]; xnew = XS[rot][(st["xi"] + 1) % 3]
                                P.mm(pb[:, 0:256], R(st["pt"]), R(xcur[:]), reads=[st["pT"], xcur], writes=[pb])
                                if i < 5:
                                    P.mm(pb[:, 256:384], R(st["pt"]), R(st["pn"]), reads=[st["pT"]], writes=[pb])
                                    P.mm(pb[:, 384:512], R(st["pn"]), R(st["pt"]), reads=[st["pT"]], writes=[pb])
                                P.tt("vector", R(xnew[:]), xcur[:], pb[:, 0:256], ALU.add, reads=[xcur, pb], writes=[xnew])
                                if i < 5:
                                    pnew = PS_[rot][i % 3]
                                    P.copy("vector", R(pnew[:]), pb[:, 256:512], reads=[pb], writes=[pnew])
                                    st["pn"], st["pt"], st["pT"] = pnew[:, 0:128], pnew[:, 128:256], pnew
                                st["xi"] += 1
                                yield
                        for c in chunks:
                            st = S[c]; am = st["am"]; tm = st["tm"]; rot = c % LS
                            Rb = st["bd"][0]
                            X = XS[rot][st["xi"] % 3]; st["X"] = X
                            Ah = X[:, 0:128]
                            ArbT = am[:, 384:512]; Bh = tm[:, 128:256]
                            pb = gen_ps()
                            P.mm(pb[:, 0:128], R(Ah), R(ArbT), reads=[X, am], writes=[pb])
                            P.mm(pb[:, 128:256], R(Ah), R(Bh), reads=[X, tm], writes=[pb])
                            rh = RH[rot]; gt = GT[rot]
                            P.tt("vector", R(rh[:]), pb[:, 0:128], f(Rb), ALU.add, reads=[pb, Rb], writes=[rh])
                            P.stt("vector", R(gt[:]), ident[:], gC[:, c:c + 1], pb[:, 128:256], ALU.mult, ALU.add, reads=[ident, gC, pb], writes=[gt])
                            yield
                        for c in chunks:
                            st = S[c]; am = st["am"]; tm = st["tm"]; rot = c % LS
                            X = st["X"]; U0 = X[:, 128:256]
                            ArbT = am[:, 384:512]; ArkT = am[:, 512:640]
                            Kh = tm[:, 0:128]; Bh = tm[:, 128:256]; Vt = tm[:, 256:384]
                            rh = RH[rot]; gt = GT[rot]
                            hcur = YH[hstate[0]]
                            hn_i = (hstate[0] + 1) % 5
                            yh = YH[hn_i]
                            pb = gen_ps()
                            P.mm(pb[:, 0:128], R(ArbT), R(U0), start=True, stop=False, reads=[am, X], writes=[pb])
                            P.mm(pb[:, 0:128], R(ArkT), R(Vt), start=False, stop=False, reads=[am, tm], writes=[pb])
                            P.mm(pb[:, 0:128], R(rh[:]), R(hcur[:, 128:256]), start=False, stop=True, reads=[rh, hcur], writes=[pb])
                            P.mm(pb[:, 128:256], R(Bh), R(U0), start=True, stop=False, reads=[tm, X], writes=[pb])
                            P.mm(pb[:, 128:256], R(Kh), R(Vt), start=False, stop=False, reads=[tm], writes=[pb])
                            P.mm(pb[:, 128:256], R(gt[:]), R(hcur[:, 128:256]), start=False, stop=True, reads=[gt, hcur], writes=[pb])
                            P.act(R(yh[:]), pb[:, 0:256], AF.Copy, reads=[pb], writes=[yh])
                            hstate[0] = hn_i
                            P.tt("gpsimd", Yc[:, c, :], yh[:, 0:64], yh[:, 64:128], ALU.add, reads=[yh], writes=[Yc])
                            yield
                    if RSTOP < 3:
                        continue
                    yield
                    nchk = SUB // CH
                    P.op("vector", lambda e: e.tensor_reduce(st1[:], Yc[:], AX.X, ALU.add), reads=[Yc], writes=[st1])
                    P.tt("gpsimd", Ysq[:], Yc[:], Yc[:], ALU.mult, reads=[Yc], writes=[Ysq])
                    P.op("vector", lambda e: e.tensor_reduce(st2[:], Ysq[:], AX.X, ALU.add), reads=[Ysq], writes=[st2])
                    P.ts("vector", st1[:], st1[:], 1.0 / 64, None, ALU.mult, reads=[st1], writes=[st1])
                    P.tt("vector", st3[:], st1[:], st1[:], ALU.mult, reads=[st1], writes=[st3])
                    P.stt("vector", st2[:], st2[:], 1.0 / 64, st3[:], ALU.mult, ALU.subtract, reads=[st2, st3], writes=[st2])
                    P.act(st2[:], st2[:], AF.Sqrt, bias=64e-5, reads=[st2], writes=[st2])
                    P.op("vector", lambda e: e.reciprocal(st2[:], st2[:]), reads=[st2], writes=[st2])
                    bc = lambda t: t[:, :].unsqueeze(2).to_broadcast([128, nchk, 64])
                    bg = lambda t: t[:, :].unsqueeze(1).to_broadcast([128, nchk, 64])
                    P.tt("vector", yo[:], Yc[:], bc(st1), ALU.subtract, reads=[Yc, st1], writes=[yo])
                    P.tt("vector", yo[:], yo[:], bc(st2), ALU.mult, reads=[yo, st2], writes=[yo])
                    P.tt("vector", yo[:], yo[:], bg(gng), ALU.mult, reads=[yo, gng], writes=[yo])
                    P.tt("vector", yo[:], yo[:], bg(gnb), ALU.add, reads=[yo, gnb], writes=[yo])
                    P.tt("vector", Ysq[:], Vc[:], bc(coef), ALU.mult, reads=[Vc, coef], writes=[Ysq])
                    P.tt("vector", yo[:], yo[:], Ysq[:], ALU.add, reads=[yo, Ysq], writes=[yo])
                    P.tt("vector", yo[:], yo[:], Gc[:], ALU.mult, reads=[yo, Gc], writes=[yo])
                    t0 = c0 + s * SUB
                    for h in range(2 if RSTOP >= 4 else 0):
                        P.dma("sync", oap("orw", t0, SUB)[:, h * 64:(h + 1) * 64].rearrange("(c t) v -> t c v", t=CH),
                              yo[h * 64:(h + 1) * 64, :, :], reads=[yo], writes=[orw_d])


        def _g_s5():
            yield
            if do_s5:
                for s in range(TT // S5SUB):
                    us = ubuf[:, s * S5SUB:(s + 1) * S5SUB]
                    pby = ps[4]
                    for g in range(8):
                        pb = gen_ps()
                        P.mm(pb[:, 0:S5SUB], R(W1[:, g, :]), R(us), reads=[W1, ubuf], writes=[pb])
                        P.mm(pb[:, S5SUB:2 * S5SUB], R(W2[:, g, :]), R(us), reads=[W2, ubuf], writes=[pb])
                        x1, x2, xt, ht = s5x
                        P.tt("vector", R(x1[:]), pb[:, 0:S5SUB], ctab[:, g, :], ALU.mult, reads=[pb, ctab], writes=[x1])
                        P.tt("vector", R(x2[:]), pb[:, S5SUB:2 * S5SUB], stab[:, g, :], ALU.mult, reads=[pb, stab], writes=[x2])
                        P.tt("gpsimd", xt[:], x1[:], x2[:], ALU.add, reads=[x1, x2], writes=[xt])
                        P.op("vector", lambda e, g=g, xt=xt, ht=ht: e.tensor_tensor_scan(ht[:], mag_sp[:, g:g + 1].to_broadcast([128, S5SUB]), xt[:], h0[:, g:g + 1], ALU.mult, ALU.add),
                             reads=[mag_sp, xt, h0], writes=[ht])
                        P.tt("gpsimd", R(x1[:]), ht[:], ctab[:, g, :], ALU.mult, reads=[ht, ctab], writes=[x1])
                        P.tt("vector", R(x2[:]), ht[:], stab[:, g, :], ALU.mult, reads=[ht, stab], writes=[x2])
                        P.mm(pby[:, 0:S5SUB], R(Wc1g[:, g, :]), R(x1[:]), start=(g == 0), stop=False, reads=[Wc1g, x1], writes=[pby])
                        P.mm(pby[:, 0:S5SUB], R(Wc2g[:, g, :]), R(x2[:]), start=False, stop=(g == 7), reads=[Wc2g, x2], writes=[pby] )
                        pbh = gen_ps()
                        P.mm(pbh[:, 0:1], ident[:], x1[:, S5SUB - 1:S5SUB], start=True, stop=False, reads=[ident, x1], writes=[pbh])
                        P.mm(pbh[:, 0:1], jm[:], x2[:, S5SUB - 1:S5SUB], start=False, stop=True, reads=[jm, x2], writes=[pbh])
                        P.act(h0[:, g:g + 1], pbh[:, 0:1], AF.Copy, reads=[pbh], writes=[h0])
                        yield
                    P.stt("vector", s5y[:], us, V(13), pby[:, 0:S5SUB], ALU.mult, ALU.add, reads=[ubuf, vecs, pby], writes=[s5y])
                    P.act(s5o[:], s5y[:], AF.Gelu_apprx_tanh, reads=[s5y], writes=[s5o])
                    t0 = c0 + s * S5SUB
                    P.dma("sync", oap("os5", t0, S5SUB), s5o[:], reads=[s5o], writes=[os5_d])


        def _g_mla():
            yield
            if do_mla:
                pb = gen_ps()
                P.mm(pb[:, 0:TT], ones[:, :], sq[:, 0, :], start=True, stop=False, reads=[ones, sq], writes=[pb])
                P.mm(pb[:, 0:TT], ones[:, :], sq[:, 1, :], start=False, stop=True, reads=[ones, sq], writes=[pb])
                P.act(rstd[:], pb[:, 0:TT], AF.Sqrt, scale=1.0 / 256, bias=1e-6, reads=[pb], writes=[rstd])
                P.op("vector", lambda e: e.reciprocal(rstd[:], rstd[:]), reads=[rstd], writes=[rstd])
                for j in range(2):
                    P.stt("vector", cqn[:, j, :], cq[:, j, :], V(10 + j), rstd[:], ALU.mult, ALU.mult, reads=[cq, vecs, rstd], writes=[cqn])
                pb = gen_ps()
                P.mm(pb[:, 0:TT], ones[:, :], sqk[:], reads=[ones, sqk], writes=[pb])
                P.act(rstdk[:], pb[:, 0:TT], AF.Sqrt, scale=1.0 / 128, bias=1e-6, reads=[pb], writes=[rstdk])
                P.op("vector", lambda e: e.reciprocal(rstdk[:], rstdk[:]), reads=[rstdk], writes=[rstdk])
                P.stt("vector", ckvn[:], ckv[:], V(12), rstdk[:], ALU.mult, ALU.mult, reads=[ckv, vecs, rstdk], writes=[ckvn])
                yield
                for h in range(2):
                    pb = gen_ps()
                    P.mm(pb[0:64, 0:TT], kvupk[:, h * 64:(h + 1) * 64], ckvn[:], reads=[kvupk, ckvn], writes=[pb])
                    P.act(Kt[h][it][0:64, :], pb[0:64, 0:TT], AF.Copy, reads=[pb], writes=[Kt[h][it]])
                pb = gen_ps()
                for kb in range(QB):
                    P.mm(pb[:, kb * 128:(kb + 1) * 128], ckvn[:, kb * 128:(kb + 1) * 128], kvupv[:, :], reads=[ckvn, kvupv], writes=[pb])
                for kb in range(QB):
                    va = Vaug[it * QB + kb]
                    P.copy("vector", va[:, :, 0:64], pb[:, kb * 128:(kb + 1) * 128].rearrange("p (h v) -> p h v", h=2), reads=[pb], writes=[va])
                yield
                for h in range(2):
                    pb = gen_ps()
                    for j in range(2):
                        P.mm(pb[0:96, 0:TT], qup[:, j, h * 192:h * 192 + 96], cqn[:, j, :], start=(j == 0), stop=(j == 1), reads=[qup, cqn], writes=[pb])
                    pb2 = gen_ps()
                    for j in range(2):
                        P.mm(pb2[0:96, 0:TT], qup[:, j, h * 192 + 96:h * 192 + 192], cqn[:, j, :], start=(j == 0), stop=(j == 1), reads=[qup, cqn], writes=[pb2])
                    P.ts("vector", Qt[h][0:64, :], pb[0:64, 0:TT], ATTN_SCALE, None, ALU.mult, reads=[pb], writes=[Qt[h]])
                    P.stt("vector", qr1[64:96, :], pb[64:96, 0:TT], ATTN_SCALE, cct[64:96, :], ALU.mult, ALU.mult, reads=[pb, cct], writes=[qr1])
                    P.stt("vector", qr2[64:96, :], pb2[64:96, 0:TT], ATTN_SCALE, sst[64:96, :], ALU.mult, ALU.mult, reads=[pb2, sst], writes=[qr2])
                    P.tt("vector", Qt[h][64:96, :], qr1[64:96, :], qr2[64:96, :], ALU.add, reads=[qr1, qr2], writes=[Qt[h]])
                yield
                for h in range(2):
                    nkb = QB * it + QB
                    for kb in range(nkb):
                        kt = Kt[h][kb // QB]
                        kcol = (kb % QB) * 128
                        sbk = ps[2 + kb % 2]
                        diag = kb >= QB * it
                        qb0 = kb - QB * it if diag else 0
                        q0 = qb0 * 128
                        P.mm(sbk[:, q0:TT], kt[:, kcol:kcol + 128], Qt[h][:, q0:TT], reads=[kt, Qt[h]], writes=[sbk])
                        pt = Pt[pctr[0] % 3]; pctr[0] += 1
                        P.act(pt[:, q0:TT], sbk[:, q0:TT], AF.Exp, reads=[sbk], writes=[pt])
                        if diag:
                            P.tt("vector", pt[:, q0:q0 + 128], pt[:, q0:q0 + 128], tri[:, :], ALU.mult, reads=[pt, tri], writes=[pt])
                        va = Vaug[kb]
                        yield
                        for qb in range(qb0, QB):
                            last = (kb == QB * it + qb)
                            ob = ps[qb]
                            P.mm(ob[:, 0:65], pt[:, qb * 128:(qb + 1) * 128], va[:, h, :], start=(kb == 0), stop=last,
                                 reads=[pt, va], writes=[ob])
                    for qb in range(QB):
                        ob = ps[qb]
                        P.op("vector", lambda e, ob=ob, qb=qb: e.reciprocal(orec[:, qb, :], ob[:, 64:65]), reads=[ob], writes=[orec])
                        P.tt("vector", osb[:, qb, :], ob[:, 0:64], orec[:, qb, 0:1].to_broadcast([128, 64]), ALU.mult, reads=[ob, orec], writes=[osb])
                    P.dma("sync", oap("omla", c0, TT)[:, h * 64:(h + 1) * 64].rearrange("(q p) v -> p q v", p=128), osb[:], reads=[osb], writes=[omla_d])


        _gens = [_g_rwkv(), _g_s5(), _g_mla()]
        while _gens:
            for _g in list(_gens):
                try:
                    next(_g)
                except StopIteration:
                    _gens.remove(_g)
        if ctx is not None and ctx.get("tile_hook") is not None:
            ctx["tile_hook"](it)

    if ctx is not None:
        return
    P.wait_tiles("sync", [orw_d, omla_d, os5_d])
    P.wait_all_dma("sync")
    P.emit()
    return nc, P


def consts():
    su = np.triu(np.ones((64, 64), np.float32), 1)
    uu = np.triu(np.ones((64, 64), np.float32), 0)
    z = np.zeros((64, 64), np.float32)
    bd = lambda m: np.block([[m, z], [z, m]])
    maskb = np.concatenate([bd(su), bd(su.T), bd(su), bd(uu), bd(uu)], axis=1)
    bdm = np.zeros((128, 2), np.float32); bdm[:64, 0] = 1; bdm[64:, 1] = 1
    blk = bd(np.ones((64, 64), np.float32))
    tri = np.triu(np.ones((128, 128), np.float32), 0)
    rmask = np.ones((128, SUB), np.float32); rmask[:, ::CH] = 0
    jm = np.zeros((128, 128), np.float32)
    for p in range(64):
        jm[64 + p, p] = -1.0
        jm[p, 64 + p] = 1.0
    gmask = np.zeros((128, 8), np.float32)
    for g in range(8):
        gmask[g * 16:(g + 1) * 16, g] = 1
    sign1 = np.ones((128, 1), np.float32); sign1[64:] = -1
    tidx = np.tile(np.arange(1, S5SUB + 1, dtype=np.float32)[None, :], (128, 1))
    pos = np.arange(8192, dtype=np.float32)
    inv_freq = (np.float32(10000.0) ** (-np.arange(0, 32, 2, dtype=np.float32) / np.float32(32))).astype(np.float32)
    ang = (pos[:, None] * inv_freq[None, :]).astype(np.float32)
    cos = np.cos(ang.astype(np.float64)).astype(np.float32).T
    sin = np.sin(ang.astype(np.float64)).astype(np.float32).T
    rope = np.stack([np.concatenate([cos, cos], 0), np.concatenate([-sin, sin], 0)], 0)
    return dict(ident=np.eye(128, dtype=np.float32), maskb=maskb, bdm=bdm, blkones=blk, tri=tri, rmask=rmask, jm=jm,
                gmask=gmask, sign1=sign1, tidx=tidx, rope=np.ascontiguousarray(rope))


def mixer_inputs(I, l, j, xT, C=None):
    C = C or consts()
    w_in = I['w_in'][l]
    hs = slice(j * 128, (j + 1) * 128)
    oM = 1792; oS = oM + 416
    z64 = np.zeros((1024, 64), np.float32); z32 = np.zeros((1024, 32), np.float32)
    kr = w_in[:, oM + 384:oM + 416]
    kr_sw = np.concatenate([kr[:, 16:32], kr[:, 0:16]], 1)
    cols = [w_in[:, 0:512][:, hs], w_in[:, 512:1024][:, hs], w_in[:, 1024:1536][:, hs],
            w_in[:, 1536:1664], w_in[:, 1664:1792],
            w_in[:, oM:oM + 256], w_in[:, oM + 256:oM + 384],
            np.concatenate([z64, kr, z32], 1),
            w_in[:, oS:oS + 512][:, hs],
            np.concatenate([z64, kr_sw, z32], 1)]
    wmix = np.concatenate(cols, 1)
    assert wmix.shape == (1024, 1408), wmix.shape
    mu = I['rwkv_mu'][l]
    vecs = np.zeros((128, 16), np.float32)
    vecs[:, 0] = mu[0:512][hs]; vecs[:, 1] = mu[512:1024][hs]; vecs[:, 2] = mu[1024:1536][hs]
    vecs[:, 3] = mu[1536:1664]; vecs[:, 4] = mu[1664:1792]
    vecs[:, 5] = I['rwkv_w0'][l][hs]; vecs[:, 6] = I['rwkv_a0'][l][hs]
    vecs[:, 7] = I['rwkv_k_k'][l][hs]; vecs[:, 8] = I['rwkv_k_a'][l][hs]
    vecs[:, 9] = I['rwkv_r_k'][l].reshape(512)[hs]
    vecs[:, 10] = I['mla_q_norm'][l][0:128]; vecs[:, 11] = I['mla_q_norm'][l][128:256]
    vecs[:, 12] = I['mla_kv_norm'][l]
    vecs[:, 13] = I['s5_d'][l][hs]
    w2a2 = np.concatenate([I['rwkv_w2'][l][:, hs], I['rwkv_a2'][l][:, hs]], 0)
    g2 = I['rwkv_g2'][l][:, hs]
    gng = np.repeat(I['rwkv_gn_g'][l][hs].reshape(2, 1, 64), 64, axis=1).reshape(128, 64)
    gnb = np.repeat(I['rwkv_gn_b'][l][hs].reshape(2, 1, 64), 64, axis=1).reshape(128, 64)
    qu = I['mla_q_up'][l].reshape(256, 8, 96)
    z = np.zeros((256, 64), np.float32)
    qparts = []
    for h in (2 * j, 2 * j + 1):
        main = qu[:, h, :]
        sw = np.concatenate([z, qu[:, h, 80:96], qu[:, h, 64:80]], 1)
        qparts += [main, sw]
    qup = np.concatenate(qparts, 1)
    kvu = I['mla_kv_up'][l].reshape(128, 8, 128)
    kvupk = np.concatenate([kvu[:, 2 * j, 0:64], kvu[:, 2 * j + 1, 0:64]], 1)
    kvupv = np.concatenate([kvu[:, 2 * j, 64:128], kvu[:, 2 * j + 1, 64:128]], 1)
    gs = slice(8 * j, 8 * j + 8)
    lre = I['s5_lambda_re'][l][gs]; lim = I['s5_lambda_im'][l][gs]; ls = I['s5_log_step'][l][gs]
    s5sp = np.concatenate([np.concatenate([lre.T, lre.T], 0), np.concatenate([lim.T, lim.T], 0), np.tile(ls[None, :], (128, 1))], 1)
    rep = lambda a: np.repeat(a, 16, axis=0)
    bre = I['s5_b_re'][l][gs].transpose(0, 2, 1).reshape(128, 64)
    bim = I['s5_b_im'][l][gs].transpose(0, 2, 1).reshape(128, 64)
    s5gc = np.concatenate([rep(lre), rep(lim), bre, bim, np.repeat(ls, 16)[:, None]], 1)
    cre = I['s5_c_re'][l][gs].reshape(128, 64).T
    cim = I['s5_c_im'][l][gs].reshape(128, 64).T
    cmain = np.concatenate([cre, cim], 0); cswap = np.concatenate([cim, cre], 0)
    d = dict(xT=xT, wmix=wmix, vecs=vecs, w2a2=w2a2, g2=g2, gng=gng, gnb=gnb, qup=qup, kvupk=kvupk, kvupv=kvupv,
             s5sp=s5sp, s5gc=s5gc, cmain=cmain, cswap=cswap)
    d.update(C)
    return {k: np.ascontiguousarray(v, dtype=np.float32) for k, v in d.items() if v is not None}


import math
import numpy as np

RT = 512
NROW = 2048
ALPHA = (2.0 * 2) ** 0.25
DFF = 2816


def build_row(ntile=NROW // RT, ctx=None):
    if ctx is None:
        nc = bass.Bass("TRN2", target_bir_lowering=False)
        P = Prog(nc)
        din = lambda n, s: P.dram(n, s, F32, kind="ExternalInput")
    else:
        P = ctx["P"]; din = ctx["din"]
    xT_d = din("xT", [1024, NROW])
    mo_d = [din("orwT", [512, NROW]), din("omlaT", [512, NROW]), din("os5T", [512, NROW])]
    wg_d = din("wgate", [1024, 3072])
    wo_d = [din("rwkv_out", [512, 1024]), din("mla_out", [512, 1024])]
    glu_d = din("s5_glu", [512, 2048])
    wout_d = din("w_out", [1024, 1024])
    w1_d = din("ffn_w1", [1024, DFF]); w3_d = din("ffn_w3", [1024, DFF]); w2_d = din("ffn_w2", [DFF, 1024])
    rv_d = din("rvecs", [128, 56])
    xo_d = P.dram("xout", [1024, NROW], F32, kind="ExternalOutput") if ctx is None else ctx["xout"]
    sb = P.sbuf
    rv = sb("rv", [128, 56]); P.dma("sync", rv[:], rv_d[:], writes=[rv])
    ones = sb("ones", [128, 128]); P.memset("vector", ones[:], 1.0, writes=[ones])
    xres = sb("xres", [128, 8, RT])
    xb = sb("xb", [128, 8, RT], BF16)
    mo = [sb(f"mo{i}", [128, 4, RT], BF16) for i in range(3)]
    merged = sb("merged", [128, 8, RT]); mb = sb("mb", [128, 8, RT], BF16)
    z = sb("z", [128, 8, RT])
    hb = sb("hb", [128, 22, RT], BF16)
    gt = sb("gt", [128, RT]); t1 = sb("t1", [128, RT]); t2 = sb("t2", [128, RT])
    mean = sb("mean", [128, RT]); rstd = sb("rstd", [128, RT])
    wbuf = [sb(f"wbuf{i}", [128, 4096], BF16) for i in range(3)]
    wctr = [0]
    ps = [P.psum(f"ps{i}", [128, 512]) for i in range(8)] if ctx is None else ctx["ps"]
    pctr = [0]

    def gen_ps():
        pctr[0] += 1
        return ps[pctr[0] % 8]

    def wload(d, r0, nk, c0, ncols):
        wb = wbuf[wctr[0] % 3]; wctr[0] += 1
        view = wb[:, 0:nk * ncols].rearrange("p (k c) -> p k c", k=nk)
        P.dma("gpsimd" if ctx is None else "sync", view, d.ap[r0:r0 + nk * 128, c0:c0 + ncols].rearrange("(k p) c -> p k c", p=128), reads=[d], writes=[wb])
        return wb, view

    def layer_norm(gcol, bcol, cs):
        pS = gen_ps(); pQ = gen_ps()
        for m in range(8):
            P.mm(pS[:, 0:RT], ones[:, :], z[:, m, :], start=(m == 0), stop=(m == 7), reads=[ones, z], writes=[pS])
        for m in range(8):
            P.act(t1[:], z[:, m, :], AF.Square, reads=[z], writes=[t1])
            P.mm(pQ[:, 0:RT], ones[:, :], t1[:], start=(m == 0), stop=(m == 7), reads=[ones, t1], writes=[pQ])
        P.ts("vector", mean[:], pS[:, 0:RT], 1.0 / 1024, None, ALU.mult, reads=[pS], writes=[mean])
        P.tt("vector", t2[:], mean[:], mean[:], ALU.mult, reads=[mean], writes=[t2])
        P.stt("vector", rstd[:], pQ[:, 0:RT], 1.0 / 1024, t2[:], ALU.mult, ALU.subtract, reads=[pQ, t2], writes=[rstd])
        P.act(rstd[:], rstd[:], AF.Sqrt, bias=1e-5, reads=[rstd], writes=[rstd])
        P.op("vector", lambda e: e.reciprocal(rstd[:], rstd[:]), reads=[rstd], writes=[rstd])
        for m in range(8):
            P.tt("vector", t2[:], z[:, m, :], mean[:], ALU.subtract, reads=[z, mean], writes=[t2])
            P.tt("gpsimd", t2[:], t2[:], rstd[:], ALU.mult, reads=[t2, rstd], writes=[t2])
            P.ts("vector", xres[:, m, :], t2[:], rv[:, gcol + m:gcol + m + 1], rv[:, bcol + m:bcol + m + 1], ALU.mult, ALU.add,
                 reads=[t2, rv], writes=[xres])

    for it in range(ntile):
        cs = slice(it * RT, (it + 1) * RT)
        P.dma("sync", xres[:], xT_d.ap.rearrange("(k p) t -> p k t", p=128)[:, :, cs], reads=[xT_d], writes=[xres])
        P.copy("vector", xb[:], xres[:], reads=[xres], writes=[xb])
        if ctx is None:
            for i in range(3):
                P.dma("gpsimd", mo[i][:], mo_d[i].ap[:, cs].rearrange("(k p) t -> p k t", p=128), writes=[mo[i]])
        else:
            ctx["moload"](mo, it, gen_ps)
        for br in range(3):
            for mc in range(2):
                wgt, wgv = wload(wg_d, 0, 8, br * 1024 + mc * 512, 512)
                if br < 2:
                    wyt, wyv = wload(wo_d[br], 0, 4, mc * 512, 512)
                else:
                    wyt, wyv = wload(glu_d, 0, 4, mc * 512, 512)
                    wy2t, wy2v = wload(glu_d, 0, 4, 1024 + mc * 512, 512)
                for mi in range(4):
                    m = mc * 4 + mi
                    pg = gen_ps()
                    for k in range(8):
                        P.mm(pg[:, 0:RT], wgv[:, k, mi * 128:(mi + 1) * 128], xb[:, k, :], start=(k == 0), stop=(k == 7), reads=[wgt, xb], writes=[pg])
                    P.act(gt[:], pg[:, 0:RT], AF.Sigmoid, bias=rv[:, br * 8 + m:br * 8 + m + 1], reads=[pg, rv], writes=[gt])
                    py = gen_ps()
                    for k in range(4):
                        P.mm(py[:, 0:RT], wyv[:, k, mi * 128:(mi + 1) * 128], mo[br][:, k, :], start=(k == 0), stop=(k == 3), reads=[wyt, mo[br]], writes=[py])
                    if br == 2:
                        py2 = gen_ps()
                        for k in range(4):
                            P.mm(py2[:, 0:RT], wy2v[:, k, mi * 128:(mi + 1) * 128], mo[br][:, k, :], start=(k == 0), stop=(k == 3), reads=[wy2t, mo[br]], writes=[py2])
                        P.act(t1[:], py2[:, 0:RT], AF.Sigmoid, reads=[py2], writes=[t1])
                        P.tt("vector", t1[:], py[:, 0:RT], t1[:], ALU.mult, reads=[py, t1], writes=[t1])
                        P.tt("vector", t1[:], t1[:], gt[:], ALU.mult, reads=[t1, gt], writes=[t1])
                        P.tt("gpsimd", merged[:, m, :], merged[:, m, :], t1[:], ALU.add, reads=[merged, t1], writes=[merged])
                    elif br == 0:
                        P.tt("vector", merged[:, m, :], py[:, 0:RT], gt[:], ALU.mult, reads=[py, gt], writes=[merged])
                    else:
                        P.tt("vector", t1[:], py[:, 0:RT], gt[:], ALU.mult, reads=[py, gt], writes=[t1])
                        P.tt("gpsimd", merged[:, m, :], merged[:, m, :], t1[:], ALU.add, reads=[merged, t1], writes=[merged])
        P.copy("vector", mb[:], merged[:], reads=[merged], writes=[mb])
        for mc in range(2):
            wt, wv = wload(wout_d, 0, 8, mc * 512, 512)
            for mi in range(4):
                m = mc * 4 + mi
                pz = gen_ps()
                for k in range(8):
                    P.mm(pz[:, 0:RT], wv[:, k, mi * 128:(mi + 1) * 128], mb[:, k, :], start=(k == 0), stop=(k == 7), reads=[wt, mb], writes=[pz])
                P.stt("vector", z[:, m, :], xres[:, m, :], ALPHA, pz[:, 0:RT], ALU.mult, ALU.add, reads=[xres, pz], writes=[z])
        layer_norm(24, 32, cs)
        P.copy("vector", xb[:], xres[:], reads=[xres], writes=[xb])
        for mc in range(6):
            ncols = 512 if mc < 5 else 256
            w1t, w1v = wload(w1_d, 0, 8, mc * 512, ncols)
            w3t, w3v = wload(w3_d, 0, 8, mc * 512, ncols)
            for mi in range(ncols // 128):
                m = mc * 4 + mi
                p1 = gen_ps(); p3 = gen_ps()
                for k in range(8):
                    P.mm(p1[:, 0:RT], w1v[:, k, mi * 128:(mi + 1) * 128], xb[:, k, :], start=(k == 0), stop=(k == 7), reads=[w1t, xb], writes=[p1])
                for k in range(8):
                    P.mm(p3[:, 0:RT], w3v[:, k, mi * 128:(mi + 1) * 128], xb[:, k, :], start=(k == 0), stop=(k == 7), reads=[w3t, xb], writes=[p3])
                P.act(t1[:], p1[:, 0:RT], AF.Silu, reads=[p1], writes=[t1])
                P.tt("vector", hb[:, m, :], p3[:, 0:RT], t1[:], ALU.mult, reads=[p3, t1], writes=[hb])
        for m in range(8):
            wt, wv = wload(w2_d, 0, 22, m * 128, 128)
            pz = gen_ps()
            for k in range(22):
                P.mm(pz[:, 0:RT], wv[:, k, :], hb[:, k, :], start=(k == 0), stop=(k == 21), reads=[wt, hb], writes=[pz])
            P.stt("vector", z[:, m, :], xres[:, m, :], ALPHA, pz[:, 0:RT], ALU.mult, ALU.add, reads=[xres, pz], writes=[z])
        layer_norm(40, 48, cs)
        P.dma("sync", xo_d.ap.rearrange("(k p) t -> p k t", p=128)[:, :, cs], xres[:], reads=[xres], writes=[xo_d])
        if ctx is not None and ctx.get("xbf_out") is not None:
            P.dma("gpsimd", ctx["xbf_out"].ap.rearrange("(k p) t -> p k t", p=128)[:, :, cs], xres[:], reads=[xres], writes=[ctx["xbf_out"]])
    if ctx is not None:
        return
    P.wait_tiles("sync", [xo_d])
    P.wait_all_dma("sync")
    P.emit()
    return nc, P


def row_inputs(I, l, xT_own, orwT, omlaT, os5T):
    rv = np.zeros((128, 56), np.float32)
    rv[:, 0:24] = I['gate_b'][l].reshape(24, 128).T
    rv[:, 24:32] = I['ln1_g'][l].reshape(8, 128).T
    rv[:, 32:40] = I['ln1_b'][l].reshape(8, 128).T
    rv[:, 40:48] = I['ln2_g'][l].reshape(8, 128).T
    rv[:, 48:56] = I['ln2_b'][l].reshape(8, 128).T
    d = dict(xT=xT_own, orwT=orwT, omlaT=omlaT, os5T=os5T, wgate=I['w_in'][l][:, 2720:5792],
             rwkv_out=I['rwkv_out'][l], mla_out=I['mla_out'][l], s5_glu=I['s5_glu'][l], w_out=I['w_out'][l],
             ffn_w1=I['ffn_w1'][l], ffn_w3=I['ffn_w3'][l], ffn_w2=I['ffn_w2'][l], rvecs=rv)
    return {k: np.ascontiguousarray(v, dtype=np.float32) for k, v in d.items() if v is not None}


import numpy as np

CONST_NAMES = ("ident", "maskb", "bdm", "blkones", "tri", "rmask", "jm", "gmask", "sign1", "tidx", "rope")
RG = [[0, 1, 2, 3], [4, 5, 6, 7]]


import os
FSKIP = os.environ.get('FSKIP', '')


def build_fused(nlayers=2):
    nc = bass.Bass("TRN2", target_bir_lowering=False)
    P = Prog(nc)
    P.use_arena()
    ext = {}
    qc = {}

    def getq(e):
        return e.partition_id() % 4

    def ext_in(name, shape):
        if name not in ext:
            ext[name] = P.dram(name, shape, F32, kind="ExternalInput")
        return ext[name]

    ps = [P.psum(f"ps{i}", [128, 512]) for i in range(8)]
    QW = 2048 * 128
    mixout = P.dram("mixout", [12 * 2048, 128], F32)
    gath = P.dram("gath", [12 * 4 * 2048, 128], F32)
    mine = P.dram("mine", [3 * 4 * 2048, 128], F32)
    xqbf = P.dram("xqbf", [1024, 2048], BF16)
    xg = P.dram("xg", [8 * 4 * 128, 2048], BF16)
    xres_d = P.dram("xres_d", [1024, 2048], F32)
    xout = P.dram("xout", [1024, 2048], F32, kind="ExternalOutput")

    ROW_W = [("wgate", [1024, 3072]), ("rwkv_out", [512, 1024]), ("mla_out", [512, 1024]), ("s5_glu", [512, 2048]),
             ("w_out", [1024, 1024]), ("ffn_w1", [1024, 2816]), ("ffn_w3", [1024, 2816]), ("ffn_w2", [2816, 1024])]
    wbf = {}
    cast_jobs = []
    for l in range(nlayers):
        for n, shp in ROW_W:
            src = ext_in(f"{n}_r{l}", shp)
            dst = P.dram(f"{n}_bf{l}", shp, BF16)
            for r0 in range(0, shp[0], 256):
                cast_jobs.append((dst, src, r0))
            wbf[(n, l)] = dst

    def do_casts(k):
        for _ in range(k):
            if cast_jobs:
                dst, src, r0 = cast_jobs.pop(0)
                P.dma("gpsimd", dst[r0:r0 + 256, :], src[r0:r0 + 256, :], reads=[src], writes=[dst])

    for l in range(nlayers):
        last = (l == nlayers - 1)
        P.arena_reset()
        mo_blk = [T(f"mixout_b{i}", mixout.ap[i * 2048:(i + 1) * 2048, :]) for i in range(12)]

        class _OutT:
            pass
        cur_blk = {}

        def oap(kind, t0, n):
            q = t0 // 2048; lt = t0 % 2048
            br = {"orw": 0, "omla": 1, "os5": 2}[kind]
            blk = mixout.ap[(q * 3 + br) * 2048:(q * 3 + br + 1) * 2048, :]
            if br < 2:
                return blk[lt:lt + n, :]
            return blk.rearrange("(f a) c -> f (a c)", f=128)[:, lt:lt + n]

        mo_t = T("mixout_t", mixout.ap)
        outs = (mo_t, mo_t, mo_t)
        gath_b = [T(f"gath_b{i}", gath.ap[i * 8192:(i + 1) * 8192, :]) for i in range(12)]

        def tile_hook(it, l=l):
            do_casts(3 if l == 0 else 0)
            if (it + 1) % 8 == 0 and 'ag' not in FSKIP:
                q = it // 8
                for br in range(3):
                    i = q * 3 + br
                    P.op("gpsimd", lambda e, i=i: e.collective_compute("AllGather", ALU.bypass, replica_groups=RG, ins=[mixout.ap[i * 2048:(i + 1) * 2048, :]],
                                                                        outs=[gath.ap[i * 8192:(i + 1) * 8192, :]]),
                         reads=[mo_t], writes=[gath_b[i]], dma=True, amt=1)
                mo_t.r = {}

        def din_m(n, s, l=l):
            if n in CONST_NAMES or n == "xT":
                return ext_in(n, s)
            return ext_in(f"{n}_m{l}", s)

        def xload(xbf, c0):
            r = c0 // 2048; col = c0 % 2048
            P.dma("sync", xbf[:], xg.ap.rearrange("(k r p) t -> r p k t", k=8, r=4)[r][:, :, col:col + TT], reads=[xg], writes=[xbf])

        build_mixer(ctx=dict(P=P, din=din_m, ps=ps, outs=outs, oap=oap, tile_hook=tile_hook, xload=(None if l == 0 else xload)))
        do_casts(1000 if l == 0 else 0)
        def cp_mine(e):
            q = getq(e)
            gv = gath.ap.rearrange("(q x) f -> q (x f)", q=4)
            return e.dma_start(out=mine.ap.rearrange("(o x) f -> o (x f)", o=1), in_=gv[bass.ds(q, 1), :])
        if 'cp' not in FSKIP:
            P.op("sync", cp_mine, reads=gath_b, writes=[mine], dma=True)
        if 'row' in FSKIP:
            break
        P.barrier()
        P.arena_reset()

        def din_r(n, s, l=l):
            if n in ("orwT", "omlaT", "os5T"):
                return None
            if n == "xT":
                return ext_in("xTq", s) if l == 0 else xres_d
            if (n, l) in wbf:
                return wbf[(n, l)]
            return ext_in(f"{n}_r{l}", s)

        st = {}

        def moload(mo, it, gen_ps):
            if "tmt" not in st:
                st["tmt"] = [P.sbuf(f"tmt{i}", [128, 4, 128]) for i in range(2)]
                st["s5st"] = [P.sbuf(f"s5st{i}", [128, RT]) for i in range(2)]
                st["identf"] = P.sbuf("identf", [128, 128])
                st["n"] = 0
                P.dma("sync", st["identf"][:], ext_in("ident", [128, 128])[:], writes=[st["identf"]])
            identf = st["identf"]
            t0 = it * RT
            for br in range(2):
                for r in range(4):
                    tmt = st["tmt"][st["n"] % 2]; st["n"] += 1
                    P.dma("sync", tmt[:], mine.ap[(br * 4 + r) * 2048 + t0:(br * 4 + r) * 2048 + t0 + RT, :].rearrange("(b p) f -> p b f", p=128), reads=[mine], writes=[tmt])
                    pb = gen_ps()
                    for b4 in range(4):
                        P.mm(pb[:, b4 * 128:(b4 + 1) * 128], tmt[:, b4, :], identf[:], reads=[tmt, identf], writes=[pb])
                    P.copy("vector", mo[br][:, r, :], pb[:, 0:512], reads=[pb], writes=[mo[br]])
            for r in range(4):
                P.dma("gpsimd", mo[2][:, r, :], mine.ap[(8 + r) * 2048:(9 + r) * 2048, :].rearrange("(f a) c -> f (a c)", f=128)[:, t0:t0 + RT], reads=[mine], writes=[mo[2]])

        build_row(ctx=dict(P=P, din=din_r, ps=ps, xout=(xout if last else xres_d), xbf_out=(None if last else xqbf), moload=moload))
        if not last:
            for k in range(8):
                P.op("gpsimd", lambda e, k=k: e.collective_compute("AllGather", ALU.bypass, replica_groups=RG, ins=[xqbf.ap[k * 128:(k + 1) * 128, :]],
                                                                    outs=[xg.ap[k * 512:(k + 1) * 512, :]]),
                     reads=[xqbf], writes=[xg], dma=True, amt=1)
            P.barrier()
    P.wait_tiles("sync", [xout])
    P.wait_all_dma("sync")
    P.emit()
    return nc, P


def fused_inputs(I, b, j, C, nlayers=2):
    x = I['x']
    d = dict(C)
    d["xT"] = x[b].T
    d["xTq"] = x[b, j * 2048:(j + 1) * 2048].T
    for l in range(nlayers):
        mi = mixer_inputs(I, l, j, None, C)
        for k, v in mi.items():
            if k in C or k == "xT":
                continue
            d[f"{k}_m{l}"] = v
        ri = row_inputs(I, l, None, None, None, None)
        for k, v in ri.items():
            if k in ("xT", "orwT", "omlaT", "os5T"):
                continue
            d[f"{k}_r{l}"] = v
    return {k: np.ascontiguousarray(v, dtype=np.float32) for k, v in d.items()}


from concourse.bass_utils import run_bass_kernel_spmd


def kernel(**inputs):
    I = {k: np.asarray(v, dtype=np.float32) for k, v in inputs.items()}
    C = consts()
    nc, P = build_fused(2)
    in_maps = [fused_inputs(I, b, j, C, 2) for b in range(2) for j in range(4)]
    res = run_bass_kernel_spmd(nc, in_maps, core_ids=list(range(8))).results
    P.close()
    out = np.stack([np.concatenate([res[b * 4 + q]["xout"] for q in range(4)], 1).T for b in range(2)], 0)
    return np.ascontiguousarray(out.astype(np.float32))
```
